# Optimizing a Trainium2 kernel written in Bass

```python
import jax, jax.numpy as jnp
from jax import lax
import numpy as np

D_MODEL = 1024
BATCH = 2
SEQ = 8192
DEPTH = 2
DEC_BATCH = 128
DEC_SEQ = 8
PAST_LEN = 16384
PAGE_SIZE = 128

N_A_LAYERS = DEPTH // 2
N_B_LAYERS = DEPTH - N_A_LAYERS
HEAD_DIM = 64
N_HEADS = D_MODEL // HEAD_DIM
N_KV_HEADS = N_HEADS // 4
GROUP = N_HEADS // N_KV_HEADS
WINDOW = 128
Q_BLOCK = 128
ROPE_THETA = 500000.0
ROT_DIM = HEAD_DIM // 4
CONV_WIDTH = 3
D_FF = 2816
N_SUB = 3
EPS = 1e-6
NEG = -1e30

kernel_name = "yoco_shortconv_swa_sink_macaron_adaln_step"


def _rmsnorm(x, g):
    x32 = x.astype(jnp.float32)
    y = x32 * lax.rsqrt(jnp.mean(x32 * x32, axis=-1, keepdims=True) + EPS)
    return (y * g.astype(jnp.float32)).astype(x.dtype)


def _modulate(h, shift, scale):
    return h * (1 + scale[:, None, :]) + shift[:, None, :]


def _adaln(c, w, b, n):
    mod = jax.nn.silu(c) @ w + b
    return jnp.split(mod, n, axis=-1)


def _swiglu(h, wg, wu, wd):
    return (jax.nn.silu(h @ wg) * (h @ wu)) @ wd


def _rope(x, pos):
    half = ROT_DIM // 2
    inv_freq = ROPE_THETA ** (-jnp.arange(0, ROT_DIM, 2, dtype=jnp.float32) / ROT_DIM)
    ang = pos.astype(jnp.float32)[:, None] * inv_freq[None, :]
    cos = jnp.cos(ang)[:, None, :].astype(x.dtype)
    sin = jnp.sin(ang)[:, None, :].astype(x.dtype)
    x1 = x[..., :half]
    x2 = x[..., half:ROT_DIM]
    xp = x[..., ROT_DIM:]
    return jnp.concatenate([x1 * cos - x2 * sin, x2 * cos + x1 * sin, xp], axis=-1)


def _short_conv_mixer(h, conv_state, w_in, conv_w, w_out):
    b_g, c_g, v = jnp.split(h @ w_in, 3, axis=-1)
    u = c_g * v
    t = u.shape[1]
    padded = jnp.concatenate([conv_state.astype(u.dtype), u], axis=1)
    conv = padded[:, 0:t] * conv_w[0]
    for j in range(1, CONV_WIDTH):
        conv = conv + padded[:, j:j + t] * conv_w[j]
    y = (b_g * conv) @ w_out
    return y, padded[:, t:]


def _banded(k):
    bt, t = k.shape[:2]
    nb = t // Q_BLOCK
    pad = jnp.concatenate([jnp.zeros_like(k[:, :Q_BLOCK]), k], axis=1)
    prev = pad[:, :t].reshape(bt, nb, Q_BLOCK, N_KV_HEADS, HEAD_DIM)
    cur = k.reshape(bt, nb, Q_BLOCK, N_KV_HEADS, HEAD_DIM)
    return jnp.concatenate([prev, cur], axis=2)


def _sink_attention(q, k, v, q_pos, k_pos, sinks):
    bt, nb, qb = q.shape[:3]
    qg = q.reshape(bt, nb, qb, N_KV_HEADS, GROUP, HEAD_DIM)
    s = jnp.einsum('bnqkgd,bnskd->bnkgqs', qg, k,
                   preferred_element_type=jnp.float32) * (HEAD_DIM ** -0.5)
    rel = q_pos[:, :, None] - k_pos[:, None, :]
    mask = (rel >= 0) & (rel < WINDOW) & (k_pos[:, None, :] >= 0)
    s = jnp.where(mask[None, :, None, None], s, NEG)
    sink = sinks.astype(jnp.float32).reshape(1, 1, N_KV_HEADS, GROUP, 1, 1)
    m = jnp.maximum(jnp.max(s, axis=-1, keepdims=True), sink)
    p = jnp.exp(s - m)
    denom = jnp.sum(p, axis=-1, keepdims=True) + jnp.exp(sink - m)
    p = (p / denom).astype(v.dtype)
    o = jnp.einsum('bnkgqs,bnskd->bnqkgd', p, v)
    return o.reshape(bt, nb * qb, N_HEADS * HEAD_DIM)


def _window_attention(q, k, v, pos, k_buf, v_buf, sinks):
    bt, t = q.shape[:2]
    if k_buf is None:
        nb = t // Q_BLOCK
        qb = q.reshape(bt, nb, Q_BLOCK, N_HEADS, HEAD_DIM)
        kb = _banded(k)
        vb = _banded(v)
        q_pos = pos.reshape(nb, Q_BLOCK)
        k_pos = jnp.concatenate([q_pos - Q_BLOCK, q_pos], axis=1)
    else:
        w = k_buf.shape[1]
        qb = q[:, None]
        kb = jnp.concatenate([k_buf.astype(k.dtype), k], axis=1)[:, None]
        vb = jnp.concatenate([v_buf.astype(v.dtype), v], axis=1)[:, None]
        q_pos = pos[None]
        k_pos = jnp.concatenate([pos[0] - w + jnp.arange(w, dtype=jnp.int32), pos])[None]
    return _sink_attention(qb, kb, vb, q_pos, k_pos, sinks)


def _shared_kv(x, c, pos, p):
    sh, sc = _adaln(c, p['w_ada_kv'], p['b_ada_kv'], 2)
    h = _modulate(_rmsnorm(x, p['kv_norm_g']), sh, sc)
    bt, t = x.shape[:2]
    k = _rope((h @ p['w_k']).reshape(bt, t, N_KV_HEADS, HEAD_DIM), pos)
    v = (h @ p['w_v']).reshape(bt, t, N_KV_HEADS, HEAD_DIM)
    return k, v


def _trunk(x, c, pos, conv_states, k_buf, v_buf, p):
    bt, t = x.shape[:2]
    new_conv = []
    k = v = None
    for layer in range(DEPTH):
        if layer == N_A_LAYERS:
            k, v = _shared_kv(x, c, pos, p)
        sh1, sc1, g1, sh2, sc2, g2, sh3, sc3, g3 = _adaln(c, p['w_ada'][layer], p['b_ada'][layer], 3 * N_SUB)
        h = _modulate(_rmsnorm(x, p['norm_g'][layer, 0]), sh1, sc1)
        x = x + 0.5 * g1[:, None, :] * _swiglu(h, p['w_ffn_gate'][layer, 0], p['w_ffn_up'][layer, 0], p['w_ffn_down'][layer, 0])
        h = _modulate(_rmsnorm(x, p['norm_g'][layer, 1]), sh2, sc2)
        if layer < N_A_LAYERS:
            y, st = _short_conv_mixer(h, conv_states[layer], p['conv_w_in'][layer], p['conv_w'][layer], p['conv_w_out'][layer])
            new_conv.append(st)
        else:
            j = layer - N_A_LAYERS
            q = _rope((h @ p['attn_w_q'][j]).reshape(bt, t, N_HEADS, HEAD_DIM), pos)
            y = _window_attention(q, k, v, pos, k_buf, v_buf, p['attn_sinks'][j]) @ p['attn_w_o'][j]
        x = x + g2[:, None, :] * y
        h = _modulate(_rmsnorm(x, p['norm_g'][layer, 2]), sh3, sc3)
        x = x + 0.5 * g3[:, None, :] * _swiglu(h, p['w_ffn_gate'][layer, 1], p['w_ffn_up'][layer, 1], p['w_ffn_down'][layer, 1])
    y = _rmsnorm(x, p['final_norm_g'])
    if k_buf is None:
        k_state = k[:, -WINDOW:]
        v_state = v[:, -WINDOW:]
    else:
        w = k_buf.shape[1]
        k_state = jnp.concatenate([k_buf.astype(k.dtype), k], axis=1)[:, -w:]
        v_state = jnp.concatenate([v_buf.astype(v.dtype), v], axis=1)[:, -w:]
    return y, jnp.stack(new_conv, axis=0), k_state, v_state


def setup_inputs(seed: int = 0) -> dict:
    key = jax.random.key(seed)
    ks = jax.random.split(key, 26)
    f32 = jnp.float32
    D = D_MODEL
    w_buf = min(WINDOW, PAST_LEN)

    def nrm(k, shape, scale):
        return jax.random.normal(k, shape, f32) * scale

    return {
        'x_prompt': nrm(ks[0], (BATCH, SEQ, D), 1.0),
        'x_sample': nrm(ks[1], (DEC_BATCH, DEC_SEQ, D), 1.0),
        'state_conv': nrm(ks[2], (N_A_LAYERS, DEC_BATCH, CONV_WIDTH - 1, D), 1.0),
        'cache_k_win': nrm(ks[3], (DEC_BATCH, w_buf, N_KV_HEADS, HEAD_DIM), 1.0),
        'cache_v_win': nrm(ks[4], (DEC_BATCH, w_buf, N_KV_HEADS, HEAD_DIM), 1.0),
        'c_prompt': nrm(ks[5], (BATCH, D), 1.0),
        'c_sample': nrm(ks[6], (DEC_BATCH, D), 1.0),
        'norm_g': 1.0 + nrm(ks[7], (DEPTH, N_SUB, D), 0.02),
        'w_ada': nrm(ks[8], (DEPTH, D, 3 * N_SUB * D), 0.5 * D ** -0.5),
        'b_ada': nrm(ks[9], (DEPTH, 3 * N_SUB * D), 0.02),
        'w_ffn_gate': nrm(ks[10], (DEPTH, 2, D, D_FF), D ** -0.5),
        'w_ffn_up': nrm(ks[11], (DEPTH, 2, D, D_FF), D ** -0.5),
        'w_ffn_down': nrm(ks[12], (DEPTH, 2, D_FF, D), D_FF ** -0.5),
        'conv_w_in': nrm(ks[13], (N_A_LAYERS, D, 3 * D), D ** -0.5),
        'conv_w': nrm(ks[14], (N_A_LAYERS, CONV_WIDTH, D), CONV_WIDTH ** -0.5),
        'conv_w_out': nrm(ks[15], (N_A_LAYERS, D, D), D ** -0.5),
        'kv_norm_g': 1.0 + nrm(ks[16], (D,), 0.02),
        'w_ada_kv': nrm(ks[17], (D, 2 * D), 0.5 * D ** -0.5),
        'b_ada_kv': nrm(ks[18], (2 * D,), 0.02),
        'w_k': nrm(ks[19], (D, N_KV_HEADS * HEAD_DIM), D ** -0.5),
        'w_v': nrm(ks[20], (D, N_KV_HEADS * HEAD_DIM), D ** -0.5),
        'attn_w_q': nrm(ks[21], (N_B_LAYERS, D, N_HEADS * HEAD_DIM), D ** -0.5),
        'attn_sinks': nrm(ks[22], (N_B_LAYERS, N_HEADS), 1.0),
        'attn_w_o': nrm(ks[23], (N_B_LAYERS, N_HEADS * HEAD_DIM, D), (N_HEADS * HEAD_DIM) ** -0.5),
        'final_norm_g': 1.0 + nrm(ks[24], (D,), 0.02),
    }


def reference(x_prompt, x_sample, state_conv, cache_k_win, cache_v_win, c_prompt, c_sample,
              norm_g, w_ada, b_ada, w_ffn_gate, w_ffn_up, w_ffn_down,
              conv_w_in, conv_w, conv_w_out, kv_norm_g, w_ada_kv, b_ada_kv, w_k, w_v,
              attn_w_q, attn_sinks, attn_w_o, final_norm_g):
    p = {
        'norm_g': norm_g, 'w_ada': w_ada, 'b_ada': b_ada,
        'w_ffn_gate': w_ffn_gate, 'w_ffn_up': w_ffn_up, 'w_ffn_down': w_ffn_down,
        'conv_w_in': conv_w_in, 'conv_w': conv_w, 'conv_w_out': conv_w_out,
        'kv_norm_g': kv_norm_g, 'w_ada_kv': w_ada_kv, 'b_ada_kv': b_ada_kv,
        'w_k': w_k, 'w_v': w_v,
        'attn_w_q': attn_w_q, 'attn_sinks': attn_sinks, 'attn_w_o': attn_w_o,
        'final_norm_g': final_norm_g,
    }
    conv0 = jnp.zeros((N_A_LAYERS, x_prompt.shape[0], CONV_WIDTH - 1, D_MODEL), x_prompt.dtype)
    pos_p = jnp.arange(x_prompt.shape[1], dtype=jnp.int32)
    y_prompt, conv_p, k_p, v_p = _trunk(x_prompt, c_prompt, pos_p, conv0, None, None, p)
    pos_s = PAST_LEN + jnp.arange(x_sample.shape[1], dtype=jnp.int32)
    y_sample, conv_s, k_s, v_s = _trunk(x_sample, c_sample, pos_s, state_conv, cache_k_win, cache_v_win, p)
    return (y_prompt, y_sample, conv_p, conv_s, k_p, v_p, k_s, v_s)
```

```python
import numpy as np
from contextlib import ExitStack
import concourse.bass as bass
import concourse.mybir as mybir
from concourse.bass_utils import run_bass_kernel_spmd

F32 = mybir.dt.float32
BF16 = mybir.dt.bfloat16
AF = mybir.ActivationFunctionType
ALU = mybir.AluOpType

D = 1024
DFF = 2816
NCORE = 8
T = 2306
TKV = 2304
SLOT = 12288
EPS = 1e-6
NG0, KVG, FNG, BADA, BKV, CW, SINK, FLAG, NV = 0, 48, 56, 64, 208, 224, 248, 264, 265


class Res:
    __slots__ = ("w", "r")

    def __init__(self):
        self.w = None
        self.r = {}


class Tracker:
    ENG = ("pe", "act", "dve", "pool", "sp")

    def __init__(self, sems):
        self.sems = sems
        self.q = {e: [] for e in self.ENG}
        self.cnt = {e: 0 for e in self.ENG}
        self.waited = {e: {} for e in self.ENG}

    def _waits(self, eng, reads, writes, strict=False):
        need = {}

        def add(ev, same_ok):
            if ev is None:
                return
            sem, val, src = ev
            if src == eng and not same_ok and not strict:
                return
            k = id(sem)
            if k not in need or need[k][1] < val:
                need[k] = (sem, val)

        for r in reads:
            add(r.w, True)
        for w in writes:
            add(w.w, False)
            for ev in w.r.values():
                add(ev, False)
        out = []
        for k, (sem, val) in need.items():
            if self.waited[eng].get(k, 0) >= val:
                continue
            self.waited[eng][k] = val
            out.append((sem, val))
        return out

    def op(self, eng, fn, reads=(), writes=()):
        waits = self._waits(eng, reads, writes, strict=(eng != "pe"))
        self.cnt[eng] += 1
        sem = self.sems[eng]
        ev = (sem, self.cnt[eng], eng)

        def run(e):
            for s, v in waits:
                e.wait_ge(s, v)
            fn(e).then_inc(sem, 1)

        self.q[eng].append(run)
        for r in reads:
            r.r[eng] = ev
        for w in writes:
            w.w = ev
            w.r = {}
        return ev

    def mm(self, mms, reads, writes):
        def fn(e):
            ins = None
            for (o, l, r, st, sp) in mms:
                ins = e.matmul(o, l, r, start=st, stop=sp)
            return ins

        return self.op("pe", fn, reads, writes)

    def dma(self, qeng, pairs, dsem, reads=(), writes=()):
        waits = self._waits(qeng, reads, writes, strict=True)
        dsem[1] += 16 * len(pairs)
        sem = dsem[0]
        ev = (sem, dsem[1], "dma")

        def run(e):
            for s, v in waits:
                e.wait_ge(s, v)
            for (o, i) in pairs:
                e.dma_start(out=o, in_=i).then_inc(sem, 16)

        self.q[qeng].append(run)
        for r in reads:
            r.r["dma" + str(id(sem))] = ev
        for w in writes:
            w.w = ev
            w.r = {}
        return ev

    def wait_only(self, eng, evs):
        ws = []
        for (sem, val, _) in evs:
            ws.append((sem, val))

        def run(e):
            for s, v in ws:
                e.wait_ge(s, v)

        self.q[eng].append(run)


class Buf:
    def __init__(self, t, mk_sem):
        self.t = t
        self.res = Res()
        self._mk = mk_sem
        self._ds = None

    @property
    def ds(self):
        if self._ds is None:
            self._ds = [self._mk(), 0]
        return self._ds

    @property
    def ds_pool(self):
        if getattr(self, "_ds2", None) is None:
            self._ds2 = [self._mk(), 0]
        return self._ds2


class Rot:
    def __init__(self, bufs):
        self.bufs = bufs
        self.i = 0

    def get(self):
        b = self.bufs[self.i % len(self.bufs)]
        self.i += 1
        return b


def build():
    nc = bass.Bass("TRN2", target_bir_lowering=False)
    es = ExitStack()
    with es:
        import os as _os0
        KDBG = _os0.environ.get("KDBG", "")
        nsem = [0]

        def mk_sem():
            nsem[0] += 1
            return es.enter_context(nc.semaphore("s%d" % nsem[0]))

        def din(name, shape):
            return nc.dram_tensor(name, shape, F32, kind="ExternalInput").ap()

        def dout(name, shape):
            return nc.dram_tensor(name, shape, F32, kind="ExternalOutput").ap()

        xin = din("xin", [T, D])
        cin = din("cin", [17, D])
        scin = din("scin", [32, D])
        kcache = din("kcache", [16, 128, 256])
        vcache = din("vcache", [16, 128, 256])
        ropec = din("ropec", [128, TKV])
        ropes = din("ropes", [128, TKV])
        masks_d = din("masks", [128, 6, 128])
        vec_d = din("vec", [128, NV])
        ident_d = din("ident", [128, 128])
        w_ada = din("w_ada", [2, D, 9 * D])
        wg_d = din("wg", [2, 2, D, DFF])
        wu_d = din("wu", [2, 2, D, DFF])
        wd_d = din("wd", [2, 2, DFF, D])
        cwin = din("cwin", [D, 3 * D])
        cwout = din("cwout", [D, D])
        wadakv = din("wadakv", [D, 2 * D])
        wk_d = din("wk", [D, 256])
        wksw_d = din("wksw", [D, 256])
        wv_d = din("wv", [D, 256])
        wq_d = din("wq", [D, D])
        wqsw_d = din("wqsw", [D, D])
        wo_d = din("wo", [D, D])

        y_p = dout("y_p", [2048, D])
        y_s = dout("y_s", [128, D])
        cso_p = dout("cso_p", [2, D])
        cso_s = dout("cso_s", [32, D])
        kout_p = dout("kout_p", [128, 256])
        vout_p = dout("vout_p", [128, 256])
        kout_s = dout("kout_s", [16, 128, 256])
        vout_s = dout("vout_s", [16, 128, 256])

        kt_scr = nc.dram_tensor("kt_scr", [4, 128, TKV], BF16).ap()
        v_scr = nc.dram_tensor("v_scr", [18, 128, 256], BF16).ap()

        def sb(name, shape, dt):
            return es.enter_context(nc.sbuf_tensor(name, shape, dt))

        def mkbuf(name, shape, dt):
            return Buf(sb(name, shape, dt), mk_sem)

        X = sb("X", [128, 8, T], F32)
        Hb = sb("Hb", [128, 8, T], BF16)
        RING = [mkbuf("ring%d" % i, [128, SLOT], BF16) for i in range(2)]
        MOD = mkbuf("MOD", [128, 72, 17], F32)
        MODKV = mkbuf("MODKV", [128, 16, 17], F32)
        SC = mkbuf("SC", [128, 8, 17], BF16)
        Ab = mkbuf("Ab", [128, 8, 17], F32)
        Gb = mkbuf("Gb", [128, 8, 17], F32)
        VEC = mkbuf("VEC", [128, NV], F32)
        IDF = mkbuf("IDF", [128, 128], F32)
        IDB = mkbuf("IDB", [128, 128], BF16)
        ONES = mkbuf("ONES", [128, 128], BF16)
        MASKS = mkbuf("MASKS", [128, 6, 128], BF16)
        ESINK = mkbuf("ESINK", [128, 8], F32)
        ACTP = Rot([mkbuf("act%d" % i, [128, 4, 512], BF16) for i in range(2)])
        T32 = Rot([mkbuf("t32_%d" % i, [128, 520], F32) for i in range(4)])
        RS = Rot([mkbuf("rs%d" % i, [128, 512], F32) for i in range(2)])
        TB = Rot([mkbuf("tb%d" % i, [128, 512], BF16) for i in range(4)])
        P2 = Rot([mkbuf("p2_%d" % i, [128, 2, 512], BF16) for i in range(4)])
        KTC = mkbuf("KTC", [128, 16, 128], BF16)
        CARRY = mkbuf("CARRY", [128, 2, 2], F32)
        CSOP = mkbuf("CSOP", [128, 8, 2], F32)
        CSOS = mkbuf("CSOS", [128, 8, 32], F32)
        SCS = mkbuf("SCS", [128, 8, 32], F32)
        KVO = Rot([mkbuf("kvo%d" % i, [128, 256], F32) for i in range(2)])

        banks = []
        for i in range(8):
            pt = es.enter_context(nc.psum_tensor("bk%d" % i, [128, 512], F32))
            banks.append((pt, Res()))
        bank_i = [0]

        def bank():
            b = banks[bank_i[0] % 6]
            bank_i[0] += 1
            return b

        nbank_i = [0]

        def nbank():
            b = banks[6 + nbank_i[0] % 2]
            nbank_i[0] += 1
            return b

        sems = {e: mk_sem() for e in Tracker.ENG}
        tr = Tracker(sems)

        TILES0 = [(0, 0, 130, "halo")] + [(1 + k, 130 + 512 * k, 512, "prompt") for k in range(4)] + [(5, 2178, 128, "sample")]
        TILES1 = TILES0[1:]
        XR = [[Res() for _ in range(8)] for _ in range(6)]
        HR = [[Res() for _ in range(8)] for _ in range(6)]
        KTSCR = [[Res() for _ in range(6)] for _ in range(4)]
        VSCR = [Res() for _ in range(6)]
        out_events = []

        def ACT(out, in_, func, reads, writes, bias=None, scale=None):
            kw = {}
            if bias is not None:
                kw["bias"] = bias
            if scale is not None:
                kw["scale"] = scale
            tr.op("act", lambda e: e.activation(out=out, in_=in_, func=func, **kw), reads, writes)

        def TT(out, in0, in1, op, reads, writes):
            tr.op("dve", lambda e: e.tensor_tensor(out=out, in0=in0, in1=in1, op=op), reads, writes)

        def PTT(out, in0, in1, op, reads, writes):
            tr.op("pool", lambda e: e.tensor_tensor(out=out, in0=in0, in1=in1, op=op), reads, writes)

        def TS(out, in0, s1, s2, op0, op1, reads, writes):
            if s2 is None:
                tr.op("dve", lambda e: e.tensor_scalar(out=out, in0=in0, scalar1=s1, scalar2=None, op0=op0), reads, writes)
            else:
                tr.op("dve", lambda e: e.tensor_scalar(out=out, in0=in0, scalar1=s1, scalar2=s2, op0=op0, op1=op1), reads, writes)

        def STT(out, in0, scalar, in1, op0, op1, reads, writes):
            tr.op("dve", lambda e: e.scalar_tensor_tensor(out=out, in0=in0, scalar=scalar, in1=in1, op0=op0, op1=op1), reads, writes)

        def CPY(eng, out, in_, reads, writes):
            if eng == "act":
                tr.op("act", lambda e: e.copy(out=out, in_=in_), reads, writes)
            else:
                tr.op("dve", lambda e: e.tensor_copy(out=out, in_=in_), reads, writes)

        def TRANSP(out, in_, ident, reads, writes):
            tr.op("pe", lambda e: e.transpose(out, in_, ident), reads, writes)

        cp_i = [0]

        def evac(out, in_, reads, writes):
            cp_i[0] += 1
            CPY("act" if cp_i[0] % 2 else "dve", out, in_, reads, writes)

        def s3(ap):
            return ap.rearrange("p (b i) -> p b i", b=16)

        def bc8(ap16):
            return ap16.unsqueeze(2).broadcast_to([128, 16, 8])

        tr.dma("sp", [(VEC.t[:, :], vec_d)], VEC.ds, writes=[VEC.res])
        tr.dma("sp", [(IDF.t[:, :], ident_d)], IDF.ds, writes=[IDF.res])
        tr.dma("pool", [(IDB.t[:, :], ident_d)], IDB.ds, writes=[IDB.res])
        tr.dma("pool", [(MASKS.t[:, :, :], masks_d)], MASKS.ds, writes=[MASKS.res])
        tr.op("dve", lambda e: e.memset(ONES.t[:, :], 1.0), [], [ONES.res])
        ACT(ESINK.t[:, :], VEC.t[:, SINK:SINK + 8], AF.Exp, [VEC.res], [ESINK.res])

        def f32view(b):
            return b.t[:, :, :].rearrange("p a b -> p (a b)").bitcast(F32)

        cb = ACTP.get()
        cv_ = f32view(cb)
        tr.dma("sp", [(cv_[0:17, :], cin)], cb.ds, writes=[cb.res])
        ACT(cv_[0:17, :], cv_[0:17, :], AF.Silu, [cb.res], [cb.res])
        bk, rk = bank()
        for j in range(8):
            TRANSP(bk[:, j * 17:(j + 1) * 17], cv_[0:17, j * 128:(j + 1) * 128], IDF.t[0:17, 0:17], [cb.res, IDF.res], [rk])
        CPY("dve", SC.t[:, :, :], bk[:, 0:136].rearrange("p (j s) -> p j s", j=8), [rk], [SC.res])
        cb = ACTP.get()
        cv_ = f32view(cb)
        tr.dma("sp", [(cv_[0:32, :], scin)], cb.ds, writes=[cb.res])
        bk, rk = bank()
        for j in range(8):
            TRANSP(bk[:, j * 32:(j + 1) * 32], cv_[0:32, j * 128:(j + 1) * 128], IDF.t[0:32, 0:32], [cb.res, IDF.res], [rk])
        CPY("dve", SCS.t[:, :, :], bk[:, 0:256].rearrange("p (j s) -> p j s", j=8), [rk], [SCS.res])
        rowtiles = [(0, 2, 0, 0), (2, 128, 0, 2)] + [(130 + 128 * k, 128, 1 + k // 4, 130 + 128 * k) for k in range(16)] + [(2178, 128, 5, 2178)]
        for (r0, nr, ti, c0) in rowtiles:
            cb = ACTP.get()
            cv_ = f32view(cb)
            tr.dma("sp", [(cv_[0:nr, :], xin[r0:r0 + nr, :])], cb.ds, writes=[cb.res])
            for half in range(2):
                bk, rk = bank()
                for jj in range(4):
                    j = half * 4 + jj
                    TRANSP(bk[:, jj * 128:jj * 128 + nr], cv_[0:nr, j * 128:(j + 1) * 128], IDF.t[0:nr, 0:nr], [cb.res, IDF.res], [rk])
                evac(X[:, half * 4:half * 4 + 4, c0:c0 + nr], bk[:, :].rearrange("p (j c) -> p j c", j=4)[:, :, 0:nr],
                     [rk], [XR[ti][half * 4 + jj] for jj in range(4)])
        dsk = [mk_sem(), 0]
        out_events.append(tr.dma("sp", [(kout_s[:, 0:120, :], kcache[:, 8:128, :]), (vout_s[:, 0:120, :], vcache[:, 8:128, :])], dsk))

        parts = []
        ring_n = [0]

        def modv(m, j):
            return MOD.t[:, m * 8 + j, :]

        def gate_acc(kind_src):
            if kind_src[0] == "G":
                return (lambda i: Gb.t[:, i, 0:1]), (lambda i: Gb.t[:, i, 1:17]), Gb.res
            m = kind_src[1]
            return (lambda i: MOD.t[:, m * 8 + i, 0:1]), (lambda i: MOD.t[:, m * 8 + i, 1:17]), MOD.res

        def resid(tile, i, bk, rk, gacc):
            ti, c0, n, kind = tile
            gp, gs, gres = gacc
            xs = X[:, i, c0:c0 + n]
            if kind != "sample":
                STT(xs, bk[:, :n], gp(i), xs, ALU.mult, ALU.add, [rk, gres, XR[ti][i]], [XR[ti][i]])
            else:
                tb_ = T32.get()
                TT(s3(tb_.t[:, 0:128]), s3(bk[:, 0:128]), bc8(gs(i)), ALU.mult, [rk, gres], [tb_.res])
                TT(xs, xs, tb_.t[:, 0:128], ALU.add, [tb_.res, XR[ti][i]], [XR[ti][i]])

        def prep_mod(l, sub, ffn):
            ng = VEC.t[:, NG0 + (l * 3 + sub) * 8: NG0 + (l * 3 + sub) * 8 + 8]
            sc = MOD.t[:, (3 * sub + 1) * 8:(3 * sub + 1) * 8 + 8, :]
            TS(Ab.t[:, :, :], sc, 1.0, None, ALU.add, None, [MOD.res], [Ab.res])
            TT(Ab.t[:, :, :], Ab.t[:, :, :], ng.unsqueeze(2).broadcast_to([128, 8, 17]), ALU.mult, [Ab.res, VEC.res], [Ab.res])
            if ffn:
                gt = MOD.t[:, (3 * sub + 2) * 8:(3 * sub + 2) * 8 + 8, :]
                TS(Gb.t[:, :, :], gt, 0.5, None, ALU.mult, None, [MOD.res], [Gb.res])

        def prep_mod_kv():
            ng = VEC.t[:, KVG:KVG + 8]
            TS(Ab.t[:, :, :], MODKV.t[:, 8:16, :], 1.0, None, ALU.add, None, [MODKV.res], [Ab.res])
            TT(Ab.t[:, :, :], Ab.t[:, :, :], ng.unsqueeze(2).broadcast_to([128, 8, 17]), ALU.mult, [Ab.res, VEC.res], [Ab.res])

        def norm_items(tiles, bview, bres, pre=None, final=False):
            items = []
            first = [True]
            for tile in tiles:
                ti, c0, n, kind = tile
                st = {}

                def Afn(tile=tile, st=st):
                    ti, c0, n, kind = tile
                    if first[0] and pre is not None:
                        pre()
                    first[0] = False
                    bk, rk = nbank()
                    sqb = ACTP.get()
                    for hh in range(2):
                        ACT(sqb.t[:, :, :n], X[:, hh * 4:hh * 4 + 4, c0:c0 + n], AF.Square, [XR[ti][hh * 4 + jj] for jj in range(4)], [sqb.res])
                        tr.mm([(bk[:, :n], ONES.t[:, :], sqb.t[:, jj, :n], hh == 0 and jj == 0, hh == 1 and jj == 3) for jj in range(4)], [sqb.res, ONES.res], [rk])
                    st["bk"] = (bk, rk)

                def Bfn(tile=tile, st=st):
                    ti, c0, n, kind = tile
                    bk, rk = st["bk"]
                    rs = RS.get()
                    ACT(rs.t[:, :n], bk[:, :n], AF.Ln, [rk], [rs.res], bias=EPS, scale=1.0 / D)
                    ACT(rs.t[:, :n], rs.t[:, :n], AF.Exp, [rs.res], [rs.res], scale=-0.5)
                    st["rs"] = rs
                    if final:
                        return
                    for j in range(8):
                        xs = X[:, j, c0:c0 + n]
                        tmp = T32.get()
                        if kind != "sample":
                            STT(tmp.t[:, :n], xs, Ab.t[:, j, 0:1], rs.t[:, :n], ALU.mult, ALU.mult, [XR[ti][j], Ab.res, rs.res], [tmp.res])
                            if j % 2 == 0:
                                ACT(Hb[:, j, c0:c0 + n], tmp.t[:, :n], AF.Identity, [tmp.res, bres], [HR[ti][j]], bias=bview(j)[:, 0:1])
                            else:
                                TS(Hb[:, j, c0:c0 + n], tmp.t[:, :n], bview(j)[:, 0:1], None, ALU.add, None, [tmp.res, bres], [HR[ti][j]])
                        else:
                            TT(tmp.t[:, :n], xs, rs.t[:, :n], ALU.mult, [XR[ti][j], rs.res], [tmp.res])
                            TT(s3(tmp.t[:, :n]), s3(tmp.t[:, :n]), bc8(Ab.t[:, j, 1:17]), ALU.mult, [tmp.res, Ab.res], [tmp.res])
                            TT(s3(Hb[:, j, c0:c0 + n]), s3(tmp.t[:, :n]), bc8(bview(j)[:, 1:17]), ALU.add, [tmp.res, bres], [HR[ti][j]])

                items.append((Afn, Bfn, st))
            return items

        def add_plain(items):
            parts.append(dict(dma=None, items=lambda slot: [(a, b) for (a, b, *_r) in items]))

        def ada_part(src, c0, ncol, dstbuf, ch0, bcol0):
            nch = ncol // 128

            def dma(slot):
                v = slot.t[:, 0:8 * ncol].rearrange("p (k f) -> p k f", k=8)
                return [(v, src.rearrange("(k p) f -> p k f", p=128)[:, :, c0:c0 + ncol])]

            def items(slot):
                W = slot.t[:, 0:8 * ncol].rearrange("p (k f) -> p k f", k=8)

                def Afn():
                    bk, rk = bank()
                    for oc in range(nch):
                        tr.mm([(bk[:, oc * 17:(oc + 1) * 17], W[:, kk, oc * 128:(oc + 1) * 128], SC.t[:, kk, :], kk == 0, kk == 7) for kk in range(8)],
                              [slot.res, SC.res], [rk])
                    TT(dstbuf.t[:, ch0:ch0 + nch, :], bk[:, 0:nch * 17].rearrange("p (c s) -> p c s", c=nch),
                       VEC.t[:, bcol0:bcol0 + nch].unsqueeze(2).broadcast_to([128, nch, 17]), ALU.add, [rk, VEC.res], [dstbuf.res])

                return [(Afn, None)]

            parts.append(dict(dma=dma, items=items))

        def ada_micros(src, ncols, dstbuf, ch0, bcol0):
            return [(src, c * 128, dstbuf, ch0 + c, bcol0 + c) for c in range(ncols // 128)]

        class Side:
            def __init__(self, micros, per_item):
                self.todo = list(micros)
                self.pending = []
                self.k = per_item

            def _issue(self):
                (src, col0, dstbuf, ch, bcol) = self.todo.pop(0)
                pb_ = P2.get()
                v = pb_.t[:, :, :].rearrange("p a b -> p (a b)").rearrange("p (k f) -> p k f", k=8)
                tr.dma("pool", [(v, src.rearrange("(k p) f -> p k f", p=128)[:, :, col0:col0 + 128])], pb_.ds_pool, writes=[pb_.res])
                self.pending.append((pb_, v, dstbuf, ch, bcol))

            def _compute(self):
                (pb_, v, dstbuf, ch, bcol) = self.pending.pop(0)
                bk, rk = bank()
                tr.mm([(bk[:, 0:17], v[:, kk, :], SC.t[:, kk, :], kk == 0, kk == 7) for kk in range(8)], [pb_.res, SC.res], [rk])
                TT(dstbuf.t[:, ch, :], bk[:, 0:17], VEC.t[:, bcol:bcol + 1].broadcast_to([128, 17]), ALU.add, [rk, VEC.res], [dstbuf.res])

            def step(self):
                for _ in range(self.k):
                    if self.pending:
                        self._compute()
                for _ in range(self.k):
                    if self.todo:
                        self._issue()

            def flush(self):
                while self.pending or self.todo:
                    while self.pending:
                        self._compute()
                    for _ in range(3):
                        if self.todo:
                            self._issue()

        def ffn_parts(l, w, tiles, side=None):
            f0 = 0
            while f0 < 22:
                F = min(4, 22 - f0)

                def dma(slot, f0=f0, F=F):
                    c0 = f0 * 128
                    nc_ = F * 128
                    g = slot.t[:, 0:8 * nc_].rearrange("p (k f) -> p k f", k=8)
                    u = slot.t[:, 8 * nc_:16 * nc_].rearrange("p (k f) -> p k f", k=8)
                    dd = slot.t[:, 16 * nc_:16 * nc_ + F * D].rearrange("p (f d) -> p f d", f=F)
                    return [(g, wg_d[l, w].rearrange("(k p) f -> p k f", p=128)[:, :, c0:c0 + nc_]),
                            (u, wu_d[l, w].rearrange("(k p) f -> p k f", p=128)[:, :, c0:c0 + nc_]),
                            (dd, wd_d[l, w][c0:c0 + nc_, :].rearrange("(f p) d -> p f d", p=128))]

                def items(slot, F=F):
                    nc_ = F * 128
                    Wg = slot.t[:, 0:8 * nc_].rearrange("p (k f) -> p k f", k=8)
                    Wu = slot.t[:, 8 * nc_:16 * nc_].rearrange("p (k f) -> p k f", k=8)
                    Wd = slot.t[:, 16 * nc_:16 * nc_ + F * D].rearrange("p (f d) -> p f d", f=F)
                    gacc = gate_acc(("G",))
                    its = []
                    for tile in tiles:
                        st = {}

                        def Afn(tile=tile, st=st):
                            ti, c0, n, kind = tile
                            if side is not None:
                                side.step()
                            ab = ACTP.get()
                            st["ab"] = ab
                            for f in range(F):
                                bg, rg = bank()
                                bu, ru = bank()
                                tr.mm([(bg[:, :n], Wg[:, kk, f * 128:(f + 1) * 128], Hb[:, kk, c0:c0 + n], kk == 0, kk == 7) for kk in range(8)],
                                      [slot.res] + HR[ti], [rg])
                                tr.mm([(bu[:, :n], Wu[:, kk, f * 128:(f + 1) * 128], Hb[:, kk, c0:c0 + n], kk == 0, kk == 7) for kk in range(8)],
                                      [slot.res] + HR[ti], [ru])
                                sg = T32.get()
                                ACT(sg.t[:, :n], bg[:, :n], AF.Silu, [rg], [sg.res])
                                TT(ab.t[:, f, :n], sg.t[:, :n], bu[:, :n], ALU.mult, [sg.res, ru], [ab.res])

                        def Bfn(tile=tile, st=st):
                            ti, c0, n, kind = tile
                            ab = st["ab"]
                            for i in range(8):
                                bd, rd = bank()
                                tr.mm([(bd[:, :n], Wd[:, f, i * 128:(i + 1) * 128], ab.t[:, f, :n], f == 0, f == F - 1) for f in range(F)],
                                      [slot.res, ab.res], [rd])
                                resid(tile, i, bd, rd, gacc)

                        its.append((Afn, Bfn))
                    return its

                parts.append(dict(dma=dma, items=items))
                f0 += F

        def conv_parts(tiles):
            for p in range(4):
                def dma(slot, p=p):
                    prs = []
                    for q in range(3):
                        v = slot.t[:, q * 2048:(q + 1) * 2048].rearrange("p (k f) -> p k f", k=8)
                        prs.append((v, cwin.rearrange("(k p) f -> p k f", p=128)[:, :, q * D + p * 256:q * D + p * 256 + 256]))
                    v = slot.t[:, 6144:8192].rearrange("p (c d) -> p c d", c=2)
                    prs.append((v, cwout[p * 256:(p + 1) * 256, :].rearrange("(c p) d -> p c d", p=128)))
                    return prs

                def items(slot, p=p):
                    Wb = slot.t[:, 0:2048].rearrange("p (k f) -> p k f", k=8)
                    Wc = slot.t[:, 2048:4096].rearrange("p (k f) -> p k f", k=8)
                    Wv_ = slot.t[:, 4096:6144].rearrange("p (k f) -> p k f", k=8)
                    Wo_ = slot.t[:, 6144:8192].rearrange("p (c d) -> p c d", c=2)
                    gacc = gate_acc(("M", 5))
                    its = []
                    for tile in tiles:
                        st = {}

                        def Afn(tile=tile, st=st):
                            ti, c0, n, kind = tile
                            zb = P2.get()
                            st["zb"] = zb
                            if kind == "halo":
                                tr.op("dve", lambda e: e.memset(CARRY.t[:, :, :], 0.0), [], [CARRY.res])
                            for jl in range(2):
                                j = 2 * p + jl
                                bb, rb = bank()
                                bc, rc = bank()
                                bv, rv = bank()
                                for (bkk, rkk, W) in ((bb, rb, Wb), (bc, rc, Wc), (bv, rv, Wv_)):
                                    tr.mm([(bkk[:, :n], W[:, kk, jl * 128:(jl + 1) * 128], Hb[:, kk, c0:c0 + n], kk == 0, kk == 7) for kk in range(8)],
                                          [slot.res] + HR[ti], [rkk])
                                csb = T32.get()
                                ACT(csb.t[:, :n], bc[:, :n], AF.Identity, [rc], [csb.res])
                                U = T32.get()
                                cvb = T32.get()
                                w0 = VEC.t[:, CW + j:CW + j + 1]
                                w1 = VEC.t[:, CW + 8 + j:CW + 8 + j + 1]
                                w2 = VEC.t[:, CW + 16 + j:CW + 16 + j + 1]
                                if kind != "sample":
                                    CPY("dve", U.t[:, 0:2], CARRY.t[:, jl, :], [CARRY.res], [U.res])
                                    TT(U.t[:, 2:2 + n], csb.t[:, :n], bv[:, :n], ALU.mult, [csb.res, rv], [U.res])
                                    if kind == "halo":
                                        TS(U.t[:, 2:2 + n], U.t[:, 2:2 + n], VEC.t[:, FLAG:FLAG + 1], None, ALU.mult, None, [U.res, VEC.res], [U.res])
                                    CPY("dve", CARRY.t[:, jl, :], U.t[:, n:n + 2], [U.res], [CARRY.res])
                                    if ti == 4:
                                        CPY("dve", CSOP.t[:, j, :], U.t[:, n:n + 2], [U.res], [CSOP.res])
                                    TS(cvb.t[:, :n], U.t[:, 0:n], w0, None, ALU.mult, None, [U.res, VEC.res], [cvb.res])
                                    STT(cvb.t[:, :n], U.t[:, 1:n + 1], w1, cvb.t[:, :n], ALU.mult, ALU.add, [U.res, cvb.res], [cvb.res])
                                    STT(cvb.t[:, :n], U.t[:, 2:n + 2], w2, cvb.t[:, :n], ALU.mult, ALU.add, [U.res, cvb.res], [cvb.res])
                                    TT(zb.t[:, jl, :n], bb[:, :n], cvb.t[:, :n], ALU.mult, [rb, cvb.res], [zb.res])
                                else:
                                    U3 = U.t[:, 0:160].rearrange("p (b i) -> p b i", b=16)
                                    CPY("dve", U3[:, :, 0:2], SCS.t[:, j, :].rearrange("p (b i) -> p b i", b=16), [SCS.res], [U.res])
                                    TT(U3[:, :, 2:10], s3(csb.t[:, :n]), s3(bv[:, :n]), ALU.mult, [csb.res, rv], [U.res])
                                    CPY("dve", CSOS.t[:, j, :].rearrange("p (b i) -> p b i", b=16), U3[:, :, 8:10], [U.res], [CSOS.res])
                                    c3 = s3(cvb.t[:, :n])
                                    TS(c3, U3[:, :, 0:8], w0, None, ALU.mult, None, [U.res, VEC.res], [cvb.res])
                                    STT(c3, U3[:, :, 1:9], w1, c3, ALU.mult, ALU.add, [U.res, cvb.res], [cvb.res])
                                    STT(c3, U3[:, :, 2:10], w2, c3, ALU.mult, ALU.add, [U.res, cvb.res], [cvb.res])
                                    TT(zb.t[:, jl, :n], bb[:, :n], cvb.t[:, :n], ALU.mult, [rb, cvb.res], [zb.res])

                        def Bfn(tile=tile, st=st):
                            ti, c0, n, kind = tile
                            zb = st["zb"]
                            for i in range(8):
                                bd, rd = bank()
                                tr.mm([(bd[:, :n], Wo_[:, c, i * 128:(i + 1) * 128], zb.t[:, c, :n], c == 0, c == 1) for c in range(2)],
                                      [slot.res, zb.res], [rd])
                                resid(tile, i, bd, rd, gacc)

                        its.append((Afn, Bfn))
                    return its

                parts.append(dict(dma=dma, items=items))

        KVT = [(0, 2, 128, "halo")] + TILES1

        def kv_part():
            def dma(slot):
                prs = []
                for (off, src) in ((0, wk_d), (4096, wksw_d)):
                    v = slot.t[:, off:off + 4096].rearrange("p (k g e d) -> p k g e d", k=8, g=4, e=2)
                    s = src.rearrange("(k p) (g d) -> p k g d", p=128, g=4)
                    for e in range(2):
                        for g_ in range(4):
                            prs.append((v[:, :, g_, e, :], s[:, :, g_, :]))
                v = slot.t[:, 8192:8192 + 2048].rearrange("p (k d) -> p k d", k=8)
                prs.append((v, wv_d.rearrange("(k p) d -> p k d", p=128)))
                return prs

            def items(slot):
                Wk = slot.t[:, 0:4096].rearrange("p (k g m) -> p k g m", k=8, g=4)
                Wks = slot.t[:, 4096:8192].rearrange("p (k g m) -> p k g m", k=8, g=4)
                Wv_ = slot.t[:, 8192:8192 + 2048].rearrange("p (k d) -> p k d", k=8)
                its = []
                for tile in KVT:
                    def Afn(tile=tile):
                        ti, c0, n, kind = tile
                        rp = ACTP.get()
                        rv_ = f32view(rp).rearrange("p (a b) -> p a b", a=2)
                        tr.dma("sp", [(rv_[:, 0, 0:n], ropec[:, c0 - 2:c0 - 2 + n]), (rv_[:, 1, 0:n], ropes[:, c0 - 2:c0 - 2 + n])], rp.ds, writes=[rp.res])
                        is_out = (ti == 4) or (ti == 5)
                        for g in range(4):
                            bk_, rk_ = bank()
                            bs_, rs_ = bank()
                            tr.mm([(bk_[:, :n], Wk[:, kk, g, :], Hb[:, kk, c0:c0 + n], kk == 0, kk == 7) for kk in range(8)], [slot.res] + HR[ti], [rk_])
                            tr.mm([(bs_[:, :n], Wks[:, kk, g, :], Hb[:, kk, c0:c0 + n], kk == 0, kk == 7) for kk in range(8)], [slot.res] + HR[ti], [rs_])
                            t1 = T32.get()
                            t2 = T32.get()
                            TT(t1.t[:, :n], bk_[:, :n], rv_[:, 0, 0:n], ALU.mult, [rk_, rp.res], [t1.res])
                            TT(t2.t[:, :n], bs_[:, :n], rv_[:, 1, 0:n], ALU.mult, [rs_, rp.res], [t2.res])
                            TT(t1.t[:, :n], t1.t[:, :n], t2.t[:, :n], ALU.add, [t1.res, t2.res], [t1.res])
                            kb = TB.get()
                            ACT(kb.t[:, :n], t1.t[:, :n], AF.Identity, [t1.res], [kb.res])
                            if "a" not in KDBG:
                                tr.dma("sp", [(kt_scr[g, :, c0 - 2:c0 - 2 + n], kb.t[:, :n])], kb.ds, reads=[kb.res], writes=[KTSCR[g][ti]])
                            if is_out and "b" not in KDBG:
                                bt_, rt_ = bank()
                                TRANSP(bt_[:, 0:128], t1.t[:, n - 128:n], IDF.t[:, :], [t1.res, IDF.res], [rt_])
                                if ti not in st_k:
                                    st_k[ti] = KVO.get()
                                ko = st_k[ti]
                                evac(ko.t[:, g * 64:(g + 1) * 64], bt_[:, 0:64], [rt_], [ko.res])
                        if is_out and "b" not in KDBG:
                            ko = st_k[ti]
                            if ti == 4:
                                out_events.append(tr.dma("sp", [(kout_p, ko.t[:, :])], ko.ds, reads=[ko.res]))
                            else:
                                out_events.append(tr.dma("sp", [(kout_s[b, 120:128, :], ko.t[b * 8:(b + 1) * 8, :]) for b in range(16)], ko.ds, reads=[ko.res]))

                    def Bfn(tile=tile):
                        ti, c0, n, kind = tile
                        if "c" in KDBG:
                            return
                        nb = n // 128
                        vb = P2.get()
                        vv = vb.t[:, :, :].rearrange("p a b -> p (a b)").rearrange("p (b d) -> p b d", b=4)
                        is_out = (ti == 4) or (ti == 5)
                        for b_ in range(nb):
                            bk_, rk_ = bank()
                            tr.mm([(bk_[:, 0:256], Hb[:, kk, c0 + b_ * 128:c0 + (b_ + 1) * 128], Wv_[:, kk, :], kk == 0, kk == 7) for kk in range(8)],
                                  [slot.res] + HR[ti], [rk_])
                            if not (is_out and b_ == nb - 1):
                                evac(vv[:, b_, :], bk_[:, 0:256], [rk_], [vb.res])
                            else:
                                vo = KVO.get()
                                CPY("dve", vo.t[:, :], bk_[:, 0:256], [rk_], [vo.res])
                                CPY("act", vv[:, b_, :], vo.t[:, :], [vo.res], [vb.res])
                                if ti == 4:
                                    out_events.append(tr.dma("sp", [(vout_p, vo.t[:, :])], vo.ds, reads=[vo.res]))
                                else:
                                    out_events.append(tr.dma("sp", [(vout_s[b, 120:128, :], vo.t[b * 8:(b + 1) * 8, :]) for b in range(16)], vo.ds, reads=[vo.res]))
                        blk0 = (c0 - 2) // 128
                        if "d" not in KDBG:
                            tr.dma("sp", [(v_scr[blk0:blk0 + nb, :, :].rearrange("b p d -> p b d"), vv[:, 0:nb, :])], vb.ds, reads=[vb.res], writes=[VSCR[ti]])

                    its.append((Afn, Bfn))
                return its

            parts.append(dict(dma=dma, items=items))

        st_k = {}

        def attn_parts(tiles):
            for g in range(4):
                def dma(slot, g=g):
                    prs = []
                    prs.append((slot.t[:, 0:2048].rearrange("p (k f) -> p k f", k=8), wq_d.rearrange("(k p) f -> p k f", p=128)[:, :, g * 256:(g + 1) * 256]))
                    prs.append((slot.t[:, 2048:4096].rearrange("p (k f) -> p k f", k=8), wqsw_d.rearrange("(k p) f -> p k f", p=128)[:, :, g * 256:(g + 1) * 256]))
                    prs.append((slot.t[:, 4096:6144].rearrange("p (c d) -> p c d", c=2), wo_d[g * 256:(g + 1) * 256, :].rearrange("(c p) d -> p c d", p=128)))
                    prs.append((slot.t[:, 6144:6144 + TKV], kt_scr[g, :, :]))
                    prs.append((slot.t[:, 8448:8448 + 1152].rearrange("p (b d) -> p b d", b=18), v_scr[:, :, g * 64:(g + 1) * 64].rearrange("b p d -> p b d")))
                    prs.append((slot.t[:, 9600:9600 + 1024].rearrange("p (b d) -> p b d", b=16), kcache[:, :, g * 64:(g + 1) * 64].rearrange("b p d -> p b d")))
                    prs.append((slot.t[:, 10624:10624 + 1024].rearrange("p (b d) -> p b d", b=16), vcache[:, :, g * 64:(g + 1) * 64].rearrange("b p d -> p b d")))
                    return prs

                def items(slot, g=g):
                    Wq = slot.t[:, 0:2048].rearrange("p (k f) -> p k f", k=8)
                    Wqs = slot.t[:, 2048:4096].rearrange("p (k f) -> p k f", k=8)
                    Wo_ = slot.t[:, 4096:6144].rearrange("p (c d) -> p c d", c=2)
                    KT = slot.t[:, 6144:6144 + TKV]
                    Vg = slot.t[:, 8448:8448 + 1152].rearrange("p (b d) -> p b d", b=18)
                    kc = slot.t[:, 9600:9600 + 1024].rearrange("p (b d) -> p b d", b=16)
                    vc = slot.t[:, 10624:10624 + 1024].rearrange("p (b d) -> p b d", b=16)
                    gacc = gate_acc(("M", 5))
                    esk = ESINK.t[:, g * 2:(g + 1) * 2].unsqueeze(2).broadcast_to([128, 2, 128])
                    its = []

                    def Pfn():
                        for q4 in range(4):
                            bk_, rk_ = bank()
                            for bl in range(4):
                                b = q4 * 4 + bl
                                for e in range(2):
                                    tr.mm([(bk_[e * 64:(e + 1) * 64, bl * 128:(bl + 1) * 128], kc[:, b, :], IDB.t[:, :], True, True)], [slot.res, IDB.res], [rk_])
                            evac(KTC.t[:, q4 * 4:(q4 + 1) * 4, :], bk_[:, :].rearrange("p (b k) -> p b k", b=4), [rk_], [KTC.res])

                    for tile in tiles:
                        st = {}

                        def Afn(tile=tile, st=st):
                            ti, c0, n, kind = tile
                            if kind == "sample":
                                Pfn()
                            rp = ACTP.get()
                            rv_ = f32view(rp).rearrange("p (a b) -> p a b", a=2)
                            tr.dma("sp", [(rv_[:, 0, 0:n], ropec[:, c0 - 2:c0 - 2 + n]), (rv_[:, 1, 0:n], ropes[:, c0 - 2:c0 - 2 + n])], rp.ds, writes=[rp.res])
                            qt = P2.get()
                            st["qt"] = qt
                            for c in range(2):
                                bq, rq = bank()
                                bs_, rs_ = bank()
                                tr.mm([(bq[:, :n], Wq[:, kk, c * 128:(c + 1) * 128], Hb[:, kk, c0:c0 + n], kk == 0, kk == 7) for kk in range(8)], [slot.res] + HR[ti], [rq])
                                tr.mm([(bs_[:, :n], Wqs[:, kk, c * 128:(c + 1) * 128], Hb[:, kk, c0:c0 + n], kk == 0, kk == 7) for kk in range(8)], [slot.res] + HR[ti], [rs_])
                                t1 = T32.get()
                                t2 = T32.get()
                                TT(t1.t[:, :n], bq[:, :n], rv_[:, 0, 0:n], ALU.mult, [rq, rp.res], [t1.res])
                                TT(t2.t[:, :n], bs_[:, :n], rv_[:, 1, 0:n], ALU.mult, [rs_, rp.res], [t2.res])
                                PTT(qt.t[:, c, :n], t1.t[:, :n], t2.t[:, :n], ALU.add, [t1.res, t2.res], [qt.res])

                        def Bfn(tile=tile, st=st):
                            ti, c0, n, kind = tile
                            if "q" in KDBG or ("s" in KDBG and kind == "sample") or ("m" in KDBG and kind != "sample"):
                                return
                            qt = st["qt"]
                            ot = P2.get()
                            bst = {}

                            def stage1(blk):
                                q0 = blk * 128
                                kown = c0 - 2 + q0
                                be = [bank(), bank()]
                                pe_ = [TB.get(), TB.get()]
                                mlist = []
                                if kind != "sample":
                                    for (c_lo, k_lo) in ((0, kown - 128), (256, kown)):
                                        for e in range(2):
                                            es_ = slice(e * 64, (e + 1) * 64)
                                            mlist.append((be[e][0][:, c_lo:c_lo + 256].rearrange("p (c q) -> p c q", c=2), KT[es_, k_lo:k_lo + 128], qt.t[es_, :, q0:q0 + 128], True, True))
                                else:
                                    for b in range(16):
                                        for c_ in range(2):
                                            for e in range(2):
                                                es_ = slice(e * 64, (e + 1) * 64)
                                                o = be[e][0][:, c_ * 128 + b * 8:c_ * 128 + b * 8 + 8]
                                                mlist.append((o, KTC.t[es_, b, :], qt.t[es_, c_, b * 8:(b + 1) * 8], True, True))
                                    for e in range(2):
                                        es_ = slice(e * 64, (e + 1) * 64)
                                        mlist.append((be[e][0][:, 256:512].rearrange("p (c q) -> p c q", c=2), KT[es_, kown:kown + 128], qt.t[es_, :, q0:q0 + 128], True, True))
                                tr.mm(mlist, [slot.res, KTC.res, qt.res], [be[0][1], be[1][1]])
                                if kind != "sample":
                                    mi = 2 if (ti == 1 and blk == 0) else 0
                                else:
                                    mi = 4
                                msk = MASKS.t[:, mi:mi + 2, :].unsqueeze(2).broadcast_to([128, 2, 2, 128])
                                for e in range(2):
                                    bke, rke = be[e]
                                    ACT(pe_[e].t[:, :], bke[:, :], AF.Exp, [rke], [pe_[e].res], scale=0.125)
                                    p4 = pe_[e].t[:, :].rearrange("p (t c q) -> p t c q", t=2, c=2)
                                    PTT(p4, p4, msk, ALU.mult, [pe_[e].res, MASKS.res], [pe_[e].res])
                                bst[blk] = pe_

                            def stage2(blk):
                                q0 = blk * 128
                                kown = c0 - 2 + q0
                                vown = kown // 128
                                pe_ = bst.pop(blk)
                                bd_, rd_ = bank()
                                tr.mm([(bd_[e * 64:(e + 1) * 64, 0:256], ONES.t[:, 0:64], pe_[e].t[:, t_ * 256:(t_ + 1) * 256], t_ == 0, t_ == 1) for t_ in range(2) for e in range(2)],
                                      [pe_[0].res, pe_[1].res, ONES.res], [rd_])
                                rden = RS.get()
                                r3 = rden.t[:, 0:256].rearrange("p (h q) -> p h q", h=2)
                                TT(r3, bd_[:, 0:256].rearrange("p (h q) -> p h q", h=2), esk, ALU.add, [rd_, ESINK.res], [rden.res])
                                tr.op("dve", lambda e, rden=rden: e.reciprocal(out=rden.t[:, 0:256], in_=rden.t[:, 0:256]), [rden.res], [rden.res])
                                bo, ro = bank()
                                mlist = []
                                if kind != "sample":
                                    for (vb_, lo, st_, sp_) in ((vown - 1, 0, True, False), (vown, 256, False, True)):
                                        for e in range(2):
                                            es_ = slice(e * 64, (e + 1) * 64)
                                            mlist.append((bo[es_, 0:256], Vg[:, vb_, :], pe_[e].t[:, lo:lo + 256], st_, sp_))
                                else:
                                    for e in range(2):
                                        es_ = slice(e * 64, (e + 1) * 64)
                                        mlist.append((bo[es_, 0:256], Vg[:, vown, :], pe_[e].t[:, 256:512], True, False))
                                    for b in range(16):
                                        for c_ in range(2):
                                            for e in range(2):
                                                es_ = slice(e * 64, (e + 1) * 64)
                                                o = bo[es_, c_ * 128 + b * 8:c_ * 128 + b * 8 + 8]
                                                r_ = pe_[e].t[:, c_ * 128 + b * 8:c_ * 128 + b * 8 + 8]
                                                mlist.append((o, vc[:, b, :], r_, False, (b == 15 and c_ == 1)))
                                tr.mm(mlist, [slot.res, pe_[0].res, pe_[1].res], [ro])
                                TT(ot.t[:, :, q0:q0 + 128], bo[:, 0:256].rearrange("p (c q) -> p c q", c=2),
                                   rden.t[:, 0:256].rearrange("p (c q) -> p c q", c=2), ALU.mult, [ro, rden.res], [ot.res])

                            nblk = n // 128
                            for blk in range(nblk):
                                stage1(blk)
                                if blk >= 1:
                                    stage2(blk - 1)
                            stage2(nblk - 1)
                            for i in range(8):
                                bd, rd = bank()
                                tr.mm([(bd[:, :n], Wo_[:, c, i * 128:(i + 1) * 128], ot.t[:, c, :n], c == 0, c == 1) for c in range(2)], [slot.res, ot.res], [rd])
                                resid(tile, i, bd, rd, gacc)

                        its.append((Afn, Bfn))
                    return its

                parts.append(dict(dma=dma, items=items, extra=[KTSCR[g][t_] for t_ in range(6)] + VSCR))

        def final_items(tiles):
            base = norm_items(tiles, None, None, final=True)
            its = []
            for (Afn, Bfn, st), tile in zip(base, tiles):
                def B2(tile=tile, st=st, Bfn=Bfn):
                    ti, c0, n, kind = tile
                    Bfn()
                    rs = st["rs"]
                    for blk in range(n // 128):
                        q0 = blk * 128
                        sg_ = ACTP.get()
                        sv_ = f32view(sg_)
                        for half in range(2):
                            bk_, rk_ = bank()
                            for jj in range(4):
                                j = half * 4 + jj
                                yt = T32.get()
                                STT(yt.t[:, 0:128], X[:, j, c0 + q0:c0 + q0 + 128], VEC.t[:, FNG + j:FNG + j + 1], rs.t[:, q0:q0 + 128], ALU.mult, ALU.mult,
                                    [XR[ti][j], VEC.res, rs.res], [yt.res])
                                TRANSP(bk_[:, jj * 128:(jj + 1) * 128], yt.t[:, 0:128], IDF.t[:, :], [yt.res, IDF.res], [rk_])
                            evac(sv_[:, half * 512:(half + 1) * 512], bk_[:, :], [rk_], [sg_.res])
                        if kind == "sample":
                            dst = y_s
                        else:
                            r0 = c0 - 130 + q0
                            dst = y_p[r0:r0 + 128, :]
                        out_events.append(tr.dma("sp", [(dst, sv_[:, :])], sg_.ds, reads=[sg_.res]))

                its.append((Afn, B2))
            parts.append(dict(dma=None, items=lambda slot: its))

        def cso_out():
            def Afn():
                sg_ = ACTP.get()
                sv_ = f32view(sg_)
                for half in range(2):
                    bk_, rk_ = bank()
                    for jj in range(4):
                        TRANSP(bk_[0:32, jj * 128:(jj + 1) * 128], CSOS.t[:, half * 4 + jj, :], IDF.t[:, :], [CSOS.res, IDF.res], [rk_])
                    evac(sv_[0:32, half * 512:(half + 1) * 512], bk_[0:32, :], [rk_], [sg_.res])
                out_events.append(tr.dma("sp", [(cso_s, sv_[0:32, :])], sg_.ds, reads=[sg_.res]))
                sg2 = ACTP.get()
                sv2 = f32view(sg2)
                for half in range(2):
                    bk3, rk3 = bank()
                    for jj in range(4):
                        TRANSP(bk3[0:2, jj * 128:(jj + 1) * 128], CSOP.t[:, half * 4 + jj, :], IDF.t[:, :], [CSOP.res, IDF.res], [rk3])
                    evac(sv2[0:2, half * 512:(half + 1) * 512], bk3[0:2, :], [rk3], [sg2.res])
                out_events.append(tr.dma("sp", [(cso_p, sv2[0:2, :])], sg2.ds, reads=[sg2.res]))

            parts.append(dict(dma=None, items=lambda slot: [(Afn, None)]))

        side_l0 = Side(ada_micros(w_ada[0], 9 * D, MOD, 0, BADA)[24:], 2)
        side_l1 = Side(ada_micros(wadakv, 2 * D, MODKV, 0, BKV) + ada_micros(w_ada[1], 9 * D, MOD, 0, BADA + 72), 3)

        def flush_part(sd):
            parts.append(dict(dma=None, items=lambda slot: [(sd.flush, None)]))

        ada_part(w_ada[0], 0, 1536, MOD, 0, BADA + 0)
        ada_part(w_ada[0], 1536, 1536, MOD, 12, BADA + 12)
        add_plain(norm_items(TILES0, lambda j: modv(0, j), MOD.res, pre=lambda: prep_mod(0, 0, True)))
        ffn_parts(0, 0, TILES0, side=side_l0)
        flush_part(side_l0)
        add_plain(norm_items(TILES0, lambda j: modv(3, j), MOD.res, pre=lambda: prep_mod(0, 1, False)))
        conv_parts(TILES0)
        cso_out()
        add_plain(norm_items(TILES0, lambda j: modv(6, j), MOD.res, pre=lambda: prep_mod(0, 2, True)))
        ffn_parts(0, 1, TILES0, side=side_l1)
        flush_part(side_l1)
        add_plain(norm_items(TILES0, lambda j: MODKV.t[:, j, :], MODKV.res, pre=prep_mod_kv))
        kv_part()
        add_plain(norm_items(TILES1, lambda j: modv(0, j), MOD.res, pre=lambda: prep_mod(1, 0, True)))
        ffn_parts(1, 0, TILES1)
        add_plain(norm_items(TILES1, lambda j: modv(3, j), MOD.res, pre=lambda: prep_mod(1, 1, False)))
        attn_parts(TILES1)
        add_plain(norm_items(TILES1, lambda j: modv(6, j), MOD.res, pre=lambda: prep_mod(1, 2, True)))
        ffn_parts(1, 1, TILES1)
        final_items(TILES1)

        import os as _os
        _ks = _os.environ.get("KSTOP")
        if _ks is not None:
            parts = parts[:int(_ks)]
        dparts = [p for p in parts if p["dma"] is not None]
        for k, p in enumerate(dparts):
            p["slot"] = RING[k % 2]
            p["didx"] = k

        def issue(k):
            if k < len(dparts):
                p = dparts[k]
                slot = p["slot"]
                tr.dma("pool", p["dma"](slot), slot.ds, reads=p.get("extra", []), writes=[slot.res])

        issue(0)
        prevB = [None]
        for p in parts:
            slot = p.get("slot")
            its = p["items"](slot)
            for n_, it in enumerate(its):
                a, b = it[0], it[1]
                if a:
                    a()
                if prevB[0]:
                    prevB[0]()
                prevB[0] = b
                if n_ == 0 and p["dma"] is not None:
                    issue(p["didx"] + 1)
        if prevB[0]:
            prevB[0]()

        tr.wait_only("sp", out_events)

        block = es.enter_context(nc.Block())

        @block.tensor
        def _(e):
            for f in tr.q["pe"]:
                f(e)

        @block.scalar
        def _(e):
            for f in tr.q["act"]:
                f(e)

        @block.vector
        def _(e):
            for f in tr.q["dve"]:
                f(e)

        @block.gpsimd
        def _(e):
            for f in tr.q["pool"]:
                f(e)

        @block.sync
        def _(e):
            for f in tr.q["sp"]:
                f(e)
    return nc


_NC = [None]


def _rope_tables(pos):
    inv = (np.float32(500000.0) ** (-np.arange(0, 16, 2, dtype=np.float32) / np.float32(16))).astype(np.float32)
    ang = pos.astype(np.float32)[None, :] * inv[:, None]
    cos = np.cos(ang).astype(np.float32)
    sin = np.sin(ang).astype(np.float32)
    n = pos.shape[0]
    C = np.ones((64, n), np.float32)
    S = np.zeros((64, n), np.float32)
    C[0:8] = cos
    C[8:16] = cos
    S[0:8] = -sin
    S[8:16] = sin
    return C, S


def _swap_cols(w, nheads):
    perm = []
    for h in range(nheads):
        base = h * 64
        perm += [base + 8 + d for d in range(8)] + [base + d for d in range(8)] + [base + d for d in range(16, 64)]
    return np.ascontiguousarray(w[:, perm])


def kernel(x_prompt, x_sample, state_conv, cache_k_win, cache_v_win, c_prompt, c_sample,
           norm_g, w_ada, b_ada, w_ffn_gate, w_ffn_up, w_ffn_down,
           conv_w_in, conv_w, conv_w_out, kv_norm_g, w_ada_kv, b_ada_kv, w_k, w_v,
           attn_w_q, attn_sinks, attn_w_o, final_norm_g):
    f = lambda a: np.ascontiguousarray(np.asarray(a, dtype=np.float32))
    x_prompt, x_sample, state_conv = f(x_prompt), f(x_sample), f(state_conv)
    cache_k_win, cache_v_win, c_prompt, c_sample = f(cache_k_win), f(cache_v_win), f(c_prompt), f(c_sample)
    if _NC[0] is None:
        _NC[0] = build()
    nc = _NC[0]

    def fm(v):
        v = f(v)
        lead = v.shape[:-1]
        return np.moveaxis(v.reshape(lead + (8, 128)), -1, 0)

    vec_common = np.zeros((128, NV), np.float32)
    vec_common[:, NG0:NG0 + 48] = fm(norm_g).reshape(128, 48)
    vec_common[:, KVG:KVG + 8] = fm(kv_norm_g).reshape(128, 8)
    vec_common[:, FNG:FNG + 8] = fm(final_norm_g).reshape(128, 8)
    ba = f(b_ada).reshape(2, 72, 128)
    vec_common[:, BADA:BADA + 144] = np.moveaxis(ba, -1, 0).reshape(128, 144)
    vec_common[:, BKV:BKV + 16] = np.moveaxis(f(b_ada_kv).reshape(16, 128), -1, 0)
    vec_common[:, CW:CW + 24] = fm(f(conv_w)[0]).reshape(128, 24)
    sk = f(attn_sinks)[0]
    for e in range(2):
        sperm = [4 * g + 2 * c + e for g in range(4) for c in range(2)]
        vec_common[e * 64:(e + 1) * 64, SINK:SINK + 8] = np.broadcast_to(sk[sperm][None, :], (64, 8))

    ident = np.eye(128, dtype=np.float32)
    jj = np.arange(128)[:, None]
    ii = np.arange(128)[None, :]
    mA = (jj > ii).astype(np.float32)
    mB = (jj <= ii).astype(np.float32)
    mAs = (jj > (ii % 8)).astype(np.float32)
    mBs = (((jj // 8) == (ii // 8)) & ((jj % 8) <= (ii % 8))).astype(np.float32)

    shared = dict(
        ident=ident, w_ada=f(w_ada), wg=f(w_ffn_gate), wu=f(w_ffn_up), wd=f(w_ffn_down),
        cwin=f(conv_w_in)[0], cwout=f(conv_w_out)[0], wadakv=f(w_ada_kv),
        wk=f(w_k), wksw=_swap_cols(f(w_k), 4), wv=f(w_v),
        wq=f(attn_w_q)[0], wqsw=_swap_cols(f(attn_w_q)[0], 16), wo=f(attn_w_o)[0],
    )
    in_maps = []
    for c in range(NCORE):
        b, q = c // 4, c % 4
        s0 = q * 2048
        xin = np.zeros((T, D), np.float32)
        if q > 0:
            xin[0:130] = x_prompt[b, s0 - 130:s0]
        xin[130:2178] = x_prompt[b, s0:s0 + 2048]
        xin[2178:2306] = x_sample[16 * c:16 * c + 16].reshape(128, D)
        cin = np.concatenate([c_prompt[b:b + 1], c_sample[16 * c:16 * c + 16]], axis=0)
        scin = state_conv[0, 16 * c:16 * c + 16].reshape(32, D)
        pos = np.concatenate([np.arange(s0 - 128, s0 + 2048), np.tile(16384 + np.arange(8), 16)]).astype(np.int64)
        C, S = _rope_tables(pos)
        masks = np.stack([mA, mB, mA if q > 0 else np.zeros_like(mA), mB, mAs, mBs], axis=1)
        vec = vec_common.copy()
        vec[:, FLAG] = 1.0 if q > 0 else 0.0
        m = dict(shared)
        m.update(xin=xin, cin=np.ascontiguousarray(cin), scin=np.ascontiguousarray(scin),
                 kcache=np.ascontiguousarray(cache_k_win[16 * c:16 * c + 16].reshape(16, 128, 256)),
                 vcache=np.ascontiguousarray(cache_v_win[16 * c:16 * c + 16].reshape(16, 128, 256)),
                 ropec=np.ascontiguousarray(np.concatenate([C, C], axis=0)), ropes=np.ascontiguousarray(np.concatenate([S, S], axis=0)),
                 masks=np.ascontiguousarray(masks), vec=vec)
        in_maps.append(m)
    res = run_bass_kernel_spmd(nc, in_maps, core_ids=list(range(NCORE)))
    R = res.results
    y_prompt = np.stack([np.concatenate([R[4 * b + q]["y_p"] for q in range(4)], axis=0) for b in range(2)], axis=0)
    y_sample = np.concatenate([R[c]["y_s"].reshape(16, 8, D) for c in range(NCORE)], axis=0)
    conv_p = np.stack([R[3]["cso_p"], R[7]["cso_p"]], axis=0)[None]
    conv_s = np.concatenate([R[c]["cso_s"].reshape(16, 2, D) for c in range(NCORE)], axis=0)[None]
    k_p = np.stack([R[3]["kout_p"], R[7]["kout_p"]], axis=0).reshape(2, 128, 4, 64)
    v_p = np.stack([R[3]["vout_p"], R[7]["vout_p"]], axis=0).reshape(2, 128, 4, 64)
    k_s = np.concatenate([R[c]["kout_s"] for c in range(NCORE)], axis=0).reshape(128, 128, 4, 64)
    v_s = np.concatenate([R[c]["vout_s"] for c in range(NCORE)], axis=0).reshape(128, 128, 4, 64)
    outs = (y_prompt, y_sample, conv_p, conv_s, k_p, v_p, k_s, v_s)
    return tuple(np.ascontiguousarray(o.astype(np.float32)) for o in outs)
```

```python
import numpy as np
from contextlib import ExitStack
import concourse.bass as bass
import concourse.mybir as mybir
from concourse.bass_utils import run_bass_kernel_spmd

F32 = mybir.dt.float32
BF16 = mybir.dt.bfloat16
AF = mybir.ActivationFunctionType
ALU = mybir.AluOpType

D = 1024
DFF = 2816
NCORE = 8
T = 2306
TKV = 2304
SLOT = 12288
EPS = 1e-6
NG0, KVG, FNG, BADA, BKV, CW, SINK, FLAG, NV = 0, 48, 56, 64, 208, 224, 248, 264, 265


class Res:
    __slots__ = ("w", "r")

    def __init__(self):
        self.w = None
        self.r = {}


class Tracker:
    ENG = ("pe", "act", "dve", "pool", "sp")

    def __init__(self, sems):
        self.sems = sems
        self.q = {e: [] for e in self.ENG}
        self.cnt = {e: 0 for e in self.ENG}
        self.waited = {e: {} for e in self.ENG}

    def _waits(self, eng, reads, writes, strict=False):
        need = {}

        def add(ev, same_ok):
            if ev is None:
                return
            sem, val, src = ev
            if src == eng and not same_ok and not strict:
                return
            k = id(sem)
            if k not in need or need[k][1] < val:
                need[k] = (sem, val)

        for r in reads:
            add(r.w, True)
        for w in writes:
            add(w.w, False)
            for ev in w.r.values():
                add(ev, False)
        out = []
        for k, (sem, val) in need.items():
            if self.waited[eng].get(k, 0) >= val:
                continue
            self.waited[eng][k] = val
            out.append((sem, val))
        return out

    def op(self, eng, fn, reads=(), writes=()):
        waits = self._waits(eng, reads, writes, strict=(eng != "pe"))
        self.cnt[eng] += 1
        sem = self.sems[eng]
        ev = (sem, self.cnt[eng], eng)

        def run(e):
            for s, v in waits:
                e.wait_ge(s, v)
            fn(e).then_inc(sem, 1)

        self.q[eng].append(run)
        for r in reads:
            r.r[eng] = ev
        for w in writes:
            w.w = ev
            w.r = {}
        return ev

    def mm(self, mms, reads, writes):
        def fn(e):
            ins = None
            for (o, l, r, st, sp) in mms:
                ins = e.matmul(o, l, r, start=st, stop=sp)
            return ins

        return self.op("pe", fn, reads, writes)

    def dma(self, qeng, pairs, dsem, reads=(), writes=()):
        waits = self._waits(qeng, reads, writes, strict=True)
        dsem[1] += 16 * len(pairs)
        sem = dsem[0]
        ev = (sem, dsem[1], "dma")

        def run(e):
            for s, v in waits:
                e.wait_ge(s, v)
            for (o, i) in pairs:
                e.dma_start(out=o, in_=i).then_inc(sem, 16)

        self.q[qeng].append(run)
        for r in reads:
            r.r["dma" + str(id(sem))] = ev
        for w in writes:
            w.w = ev
            w.r = {}
        return ev

    def wait_only(self, eng, evs):
        ws = []
        for (sem, val, _) in evs:
            ws.append((sem, val))

        def run(e):
            for s, v in ws:
                e.wait_ge(s, v)

        self.q[eng].append(run)


class Buf:
    def __init__(self, t, mk_sem):
        self.t = t
        self.res = Res()
        self._mk = mk_sem
        self._ds = None

    @property
    def ds(self):
        if self._ds is None:
            self._ds = [self._mk(), 0]
        return self._ds

    @property
    def ds_pool(self):
        if getattr(self, "_ds2", None) is None:
            self._ds2 = [self._mk(), 0]
        return self._ds2


class Rot:
    def __init__(self, bufs):
        self.bufs = bufs
        self.i = 0

    def get(self):
        b = self.bufs[self.i % len(self.bufs)]
        self.i += 1
        return b


def build():
    nc = bass.Bass("TRN2", target_bir_lowering=False)
    es = ExitStack()
    with es:
        import os as _os0
        KDBG = _os0.environ.get("KDBG", "")
        nsem = [0]

        def mk_sem():
            nsem[0] += 1
            return es.enter_context(nc.semaphore("s%d" % nsem[0]))

        def din(name, shape):
            return nc.dram_tensor(name, shape, F32, kind="ExternalInput").ap()

        def dout(name, shape):
            return nc.dram_tensor(name, shape, F32, kind="ExternalOutput").ap()

        xin = din("xin", [T, D])
        cin = din("cin", [17, D])
        scin = din("scin", [32, D])
        kcache = din("kcache", [16, 128, 256])
        vcache = din("vcache", [16, 128, 256])
        ropec = din("ropec", [128, TKV])
        ropes = din("ropes", [128, TKV])
        masks_d = din("masks", [128, 6, 128])
        vec_d = din("vec", [128, NV])
        ident_d = din("ident", [128, 128])
        w_ada = din("w_ada", [2, D, 9 * D])
        wg_d = din("wg", [2, 2, D, DFF])
        wu_d = din("wu", [2, 2, D, DFF])
        wd_d = din("wd", [2, 2, DFF, D])
        cwin = din("cwin", [D, 3 * D])
        cwout = din("cwout", [D, D])
        wadakv = din("wadakv", [D, 2 * D])
        wk_d = din("wk", [D, 256])
        wksw_d = din("wksw", [D, 256])
        wv_d = din("wv", [D, 256])
        wq_d = din("wq", [D, D])
        wqsw_d = din("wqsw", [D, D])
        wo_d = din("wo", [D, D])

        y_p = dout("y_p", [2048, D])
        y_s = dout("y_s", [128, D])
        cso_p = dout("cso_p", [2, D])
        cso_s = dout("cso_s", [32, D])
        kout_p = dout("kout_p", [128, 256])
        vout_p = dout("vout_p", [128, 256])
        kout_s = dout("kout_s", [16, 128, 256])
        vout_s = dout("vout_s", [16, 128, 256])

        kt_scr = nc.dram_tensor("kt_scr", [4, 128, TKV], BF16).ap()
        v_scr = nc.dram_tensor("v_scr", [18, 128, 256], BF16).ap()

        def sb(name, shape, dt):
            return es.enter_context(nc.sbuf_tensor(name, shape, dt))

        def mkbuf(name, shape, dt):
            return Buf(sb(name, shape, dt), mk_sem)

        X = sb("X", [128, 8, T], F32)
        Hb = sb("Hb", [128, 8, T], BF16)
        RING = [mkbuf("ring%d" % i, [128, SLOT], BF16) for i in range(2)]
        MOD = mkbuf("MOD", [128, 72, 17], F32)
        MODKV = mkbuf("MODKV", [128, 16, 17], F32)
        SC = mkbuf("SC", [128, 8, 17], BF16)
        Ab = mkbuf("Ab", [128, 8, 17], F32)
        Gb = mkbuf("Gb", [128, 8, 17], F32)
        VEC = mkbuf("VEC", [128, NV], F32)
        IDF = mkbuf("IDF", [128, 128], F32)
        IDB = mkbuf("IDB", [128, 128], BF16)
        ONES = mkbuf("ONES", [128, 128], BF16)
        MASKS = mkbuf("MASKS", [128, 6, 128], BF16)
        ESINK = mkbuf("ESINK", [128, 8], F32)
        ACTP = Rot([mkbuf("act%d" % i, [128, 4, 512], BF16) for i in range(2)])
        T32 = Rot([mkbuf("t32_%d" % i, [128, 520], F32) for i in range(4)])
        RS = Rot([mkbuf("rs%d" % i, [128, 512], F32) for i in range(2)])
        TB = Rot([mkbuf("tb%d" % i, [128, 512], BF16) for i in range(4)])
        P2 = Rot([mkbuf("p2_%d" % i, [128, 2, 512], BF16) for i in range(4)])
        KTC = mkbuf("KTC", [128, 16, 128], BF16)
        CARRY = mkbuf("CARRY", [128, 2, 2], F32)
        CSOP = mkbuf("CSOP", [128, 8, 2], F32)
        CSOS = mkbuf("CSOS", [128, 8, 32], F32)
        SCS = mkbuf("SCS", [128, 8, 32], F32)
        KVO = Rot([mkbuf("kvo%d" % i, [128, 256], F32) for i in range(2)])

        banks = []
        for i in range(8):
            pt = es.enter_context(nc.psum_tensor("bk%d" % i, [128, 512], F32))
            banks.append((pt, Res()))
        bank_i = [0]

        def bank():
            b = banks[bank_i[0] % 6]
            bank_i[0] += 1
            return b

        nbank_i = [0]

        def nbank():
            b = banks[6 + nbank_i[0] % 2]
            nbank_i[0] += 1
            return b

        sems = {e: mk_sem() for e in Tracker.ENG}
        tr = Tracker(sems)

        TILES0 = [(0, 0, 130, "halo")] + [(1 + k, 130 + 512 * k, 512, "prompt") for k in range(4)] + [(5, 2178, 128, "sample")]
        TILES1 = TILES0[1:]
        XR = [[Res() for _ in range(8)] for _ in range(6)]
        HR = [[Res() for _ in range(8)] for _ in range(6)]
        KTSCR = [[Res() for _ in range(6)] for _ in range(4)]
        VSCR = [Res() for _ in range(6)]
        out_events = []

        def ACT(out, in_, func, reads, writes, bias=None, scale=None):
            kw = {}
            if bias is not None:
                kw["bias"] = bias
            if scale is not None:
                kw["scale"] = scale
            tr.op("act", lambda e: e.activation(out=out, in_=in_, func=func, **kw), reads, writes)

        def TT(out, in0, in1, op, reads, writes):
            tr.op("dve", lambda e: e.tensor_tensor(out=out, in0=in0, in1=in1, op=op), reads, writes)

        def PTT(out, in0, in1, op, reads, writes):
            tr.op("pool", lambda e: e.tensor_tensor(out=out, in0=in0, in1=in1, op=op), reads, writes)

        def TS(out, in0, s1, s2, op0, op1, reads, writes):
            if s2 is None:
                tr.op("dve", lambda e: e.tensor_scalar(out=out, in0=in0, scalar1=s1, scalar2=None, op0=op0), reads, writes)
            else:
                tr.op("dve", lambda e: e.tensor_scalar(out=out, in0=in0, scalar1=s1, scalar2=s2, op0=op0, op1=op1), reads, writes)

        def STT(out, in0, scalar, in1, op0, op1, reads, writes):
            tr.op("dve", lambda e: e.scalar_tensor_tensor(out=out, in0=in0, scalar=scalar, in1=in1, op0=op0, op1=op1), reads, writes)

        def CPY(eng, out, in_, reads, writes):
            if eng == "act":
                tr.op("act", lambda e: e.copy(out=out, in_=in_), reads, writes)
            else:
                tr.op("dve", lambda e: e.tensor_copy(out=out, in_=in_), reads, writes)

        def TRANSP(out, in_, ident, reads, writes):
            tr.op("pe", lambda e: e.transpose(out, in_, ident), reads, writes)

        cp_i = [0]

        def evac(out, in_, reads, writes):
            cp_i[0] += 1
            CPY("act" if cp_i[0] % 2 else "dve", out, in_, reads, writes)

        def s3(ap):
            return ap.rearrange("p (b i) -> p b i", b=16)

        def bc8(ap16):
            return ap16.unsqueeze(2).broadcast_to([128, 16, 8])

        tr.dma("sp", [(VEC.t[:, :], vec_d)], VEC.ds, writes=[VEC.res])
        tr.dma("sp", [(IDF.t[:, :], ident_d)], IDF.ds, writes=[IDF.res])
        tr.dma("pool", [(IDB.t[:, :], ident_d)], IDB.ds, writes=[IDB.res])
        tr.dma("pool", [(MASKS.t[:, :, :], masks_d)], MASKS.ds, writes=[MASKS.res])
        tr.op("dve", lambda e: e.memset(ONES.t[:, :], 1.0), [], [ONES.res])
        ACT(ESINK.t[:, :], VEC.t[:, SINK:SINK + 8], AF.Exp, [VEC.res], [ESINK.res])

        def f32view(b):
            return b.t[:, :, :].rearrange("p a b -> p (a b)").bitcast(F32)

        cb = ACTP.get()
        cv_ = f32view(cb)
        tr.dma("sp", [(cv_[0:17, :], cin)], cb.ds, writes=[cb.res])
        ACT(cv_[0:17, :], cv_[0:17, :], AF.Silu, [cb.res], [cb.res])
        bk, rk = bank()
        for j in range(8):
            TRANSP(bk[:, j * 17:(j + 1) * 17], cv_[0:17, j * 128:(j + 1) * 128], IDF.t[0:17, 0:17], [cb.res, IDF.res], [rk])
        CPY("dve", SC.t[:, :, :], bk[:, 0:136].rearrange("p (j s) -> p j s", j=8), [rk], [SC.res])
        cb = ACTP.get()
        cv_ = f32view(cb)
        tr.dma("sp", [(cv_[0:32, :], scin)], cb.ds, writes=[cb.res])
        bk, rk = bank()
        for j in range(8):
            TRANSP(bk[:, j * 32:(j + 1) * 32], cv_[0:32, j * 128:(j + 1) * 128], IDF.t[0:32, 0:32], [cb.res, IDF.res], [rk])
        CPY("dve", SCS.t[:, :, :], bk[:, 0:256].rearrange("p (j s) -> p j s", j=8), [rk], [SCS.res])
        rowtiles = [(0, 2, 0, 0), (2, 128, 0, 2)] + [(130 + 128 * k, 128, 1 + k // 4, 130 + 128 * k) for k in range(16)] + [(2178, 128, 5, 2178)]
        for (r0, nr, ti, c0) in rowtiles:
            cb = ACTP.get()
            cv_ = f32view(cb)
            tr.dma("sp", [(cv_[0:nr, :], xin[r0:r0 + nr, :])], cb.ds, writes=[cb.res])
            for half in range(2):
                bk, rk = bank()
                for jj in range(4):
                    j = half * 4 + jj
                    TRANSP(bk[:, jj * 128:jj * 128 + nr], cv_[0:nr, j * 128:(j + 1) * 128], IDF.t[0:nr, 0:nr], [cb.res, IDF.res], [rk])
                evac(X[:, half * 4:half * 4 + 4, c0:c0 + nr], bk[:, :].rearrange("p (j c) -> p j c", j=4)[:, :, 0:nr],
                     [rk], [XR[ti][half * 4 + jj] for jj in range(4)])
        dsk = [mk_sem(), 0]
        out_events.append(tr.dma("sp", [(kout_s[:, 0:120, :], kcache[:, 8:128, :]), (vout_s[:, 0:120, :], vcache[:, 8:128, :])], dsk))

        parts = []
        ring_n = [0]

        def modv(m, j):
            return MOD.t[:, m * 8 + j, :]

        def gate_acc(kind_src):
            if kind_src[0] == "G":
                return (lambda i: Gb.t[:, i, 0:1]), (lambda i: Gb.t[:, i, 1:17]), Gb.res
            m = kind_src[1]
            return (lambda i: MOD.t[:, m * 8 + i, 0:1]), (lambda i: MOD.t[:, m * 8 + i, 1:17]), MOD.res

        def resid(tile, i, bk, rk, gacc):
            ti, c0, n, kind = tile
            gp, gs, gres = gacc
            xs = X[:, i, c0:c0 + n]
            if kind != "sample":
                STT(xs, bk[:, :n], gp(i), xs, ALU.mult, ALU.add, [rk, gres, XR[ti][i]], [XR[ti][i]])
            else:
                tb_ = T32.get()
                TT(s3(tb_.t[:, 0:128]), s3(bk[:, 0:128]), bc8(gs(i)), ALU.mult, [rk, gres], [tb_.res])
                TT(xs, xs, tb_.t[:, 0:128], ALU.add, [tb_.res, XR[ti][i]], [XR[ti][i]])

        def prep_mod(l, sub, ffn):
            ng = VEC.t[:, NG0 + (l * 3 + sub) * 8: NG0 + (l * 3 + sub) * 8 + 8]
            sc = MOD.t[:, (3 * sub + 1) * 8:(3 * sub + 1) * 8 + 8, :]
            TS(Ab.t[:, :, :], sc, 1.0, None, ALU.add, None, [MOD.res], [Ab.res])
            TT(Ab.t[:, :, :], Ab.t[:, :, :], ng.unsqueeze(2).broadcast_to([128, 8, 17]), ALU.mult, [Ab.res, VEC.res], [Ab.res])
            if ffn:
                gt = MOD.t[:, (3 * sub + 2) * 8:(3 * sub + 2) * 8 + 8, :]
                TS(Gb.t[:, :, :], gt, 0.5, None, ALU.mult, None, [MOD.res], [Gb.res])

        def prep_mod_kv():
            ng = VEC.t[:, KVG:KVG + 8]
            TS(Ab.t[:, :, :], MODKV.t[:, 8:16, :], 1.0, None, ALU.add, None, [MODKV.res], [Ab.res])
            TT(Ab.t[:, :, :], Ab.t[:, :, :], ng.unsqueeze(2).broadcast_to([128, 8, 17]), ALU.mult, [Ab.res, VEC.res], [Ab.res])

        def norm_items(tiles, bview, bres, pre=None, final=False):
            items = []
            first = [True]
            for tile in tiles:
                ti, c0, n, kind = tile
                st = {}

                def Afn(tile=tile, st=st):
                    ti, c0, n, kind = tile
                    if first[0] and pre is not None:
                        pre()
                    first[0] = False
                    bk, rk = nbank()
                    sqb = ACTP.get()
                    for hh in range(2):
                        ACT(sqb.t[:, :, :n], X[:, hh * 4:hh * 4 + 4, c0:c0 + n], AF.Square, [XR[ti][hh * 4 + jj] for jj in range(4)], [sqb.res])
                        tr.mm([(bk[:, :n], ONES.t[:, :], sqb.t[:, jj, :n], hh == 0 and jj == 0, hh == 1 and jj == 3) for jj in range(4)], [sqb.res, ONES.res], [rk])
                    st["bk"] = (bk, rk)

                def Bfn(tile=tile, st=st):
                    ti, c0, n, kind = tile
                    bk, rk = st["bk"]
                    rs = RS.get()
                    ACT(rs.t[:, :n], bk[:, :n], AF.Ln, [rk], [rs.res], bias=EPS, scale=1.0 / D)
                    ACT(rs.t[:, :n], rs.t[:, :n], AF.Exp, [rs.res], [rs.res], scale=-0.5)
                    st["rs"] = rs
                    if final:
                        return
                    for j in range(8):
                        xs = X[:, j, c0:c0 + n]
                        tmp = T32.get()
                        if kind != "sample":
                            STT(tmp.t[:, :n], xs, Ab.t[:, j, 0:1], rs.t[:, :n], ALU.mult, ALU.mult, [XR[ti][j], Ab.res, rs.res], [tmp.res])
                            ACT(Hb[:, j, c0:c0 + n], tmp.t[:, :n], AF.Identity, [tmp.res, bres], [HR[ti][j]], bias=bview(j)[:, 0:1])
                        else:
                            TT(tmp.t[:, :n], xs, rs.t[:, :n], ALU.mult, [XR[ti][j], rs.res], [tmp.res])
                            TT(s3(tmp.t[:, :n]), s3(tmp.t[:, :n]), bc8(Ab.t[:, j, 1:17]), ALU.mult, [tmp.res, Ab.res], [tmp.res])
                            TT(s3(Hb[:, j, c0:c0 + n]), s3(tmp.t[:, :n]), bc8(bview(j)[:, 1:17]), ALU.add, [tmp.res, bres], [HR[ti][j]])

                items.append((Afn, Bfn, st))
            return items

        def add_plain(items):
            parts.append(dict(dma=None, items=lambda slot: [(a, b) for (a, b, *_r) in items]))

        def ada_part(src, c0, ncol, dstbuf, ch0, bcol0):
            nch = ncol // 128

            def dma(slot):
                v = slot.t[:, 0:8 * ncol].rearrange("p (k f) -> p k f", k=8)
                return [(v, src.rearrange("(k p) f -> p k f", p=128)[:, :, c0:c0 + ncol])]

            def items(slot):
                W = slot.t[:, 0:8 * ncol].rearrange("p (k f) -> p k f", k=8)

                def Afn():
                    bk, rk = bank()
                    for oc in range(nch):
                        tr.mm([(bk[:, oc * 17:(oc + 1) * 17], W[:, kk, oc * 128:(oc + 1) * 128], SC.t[:, kk, :], kk == 0, kk == 7) for kk in range(8)],
                              [slot.res, SC.res], [rk])
                    TT(dstbuf.t[:, ch0:ch0 + nch, :], bk[:, 0:nch * 17].rearrange("p (c s) -> p c s", c=nch),
                       VEC.t[:, bcol0:bcol0 + nch].unsqueeze(2).broadcast_to([128, nch, 17]), ALU.add, [rk, VEC.res], [dstbuf.res])

                return [(Afn, None)]

            parts.append(dict(dma=dma, items=items))

        def ada_micros(src, ncols, dstbuf, ch0, bcol0):
            return [(src, c * 128, dstbuf, ch0 + c, bcol0 + c) for c in range(ncols // 128)]

        class Side:
            def __init__(self, micros, per_item):
                self.todo = list(micros)
                self.pending = []
                self.k = per_item

            def _issue(self):
                (src, col0, dstbuf, ch, bcol) = self.todo.pop(0)
                pb_ = P2.get()
                v = pb_.t[:, :, :].rearrange("p a b -> p (a b)").rearrange("p (k f) -> p k f", k=8)
                tr.dma("pool", [(v, src.rearrange("(k p) f -> p k f", p=128)[:, :, col0:col0 + 128])], pb_.ds_pool, writes=[pb_.res])
                self.pending.append((pb_, v, dstbuf, ch, bcol))

            def _compute(self):
                (pb_, v, dstbuf, ch, bcol) = self.pending.pop(0)
                bk, rk = bank()
                tr.mm([(bk[:, 0:17], v[:, kk, :], SC.t[:, kk, :], kk == 0, kk == 7) for kk in range(8)], [pb_.res, SC.res], [rk])
                TT(dstbuf.t[:, ch, :], bk[:, 0:17], VEC.t[:, bcol:bcol + 1].broadcast_to([128, 17]), ALU.add, [rk, VEC.res], [dstbuf.res])

            def step(self):
                for _ in range(self.k):
                    if self.pending:
                        self._compute()
                for _ in range(self.k):
                    if self.todo:
                        self._issue()

            def flush(self):
                while self.pending or self.todo:
                    while self.pending:
                        self._compute()
                    for _ in range(3):
                        if self.todo:
                            self._issue()

        def ffn_parts(l, w, tiles, side=None):
            f0 = 0
            while f0 < 22:
                F = min(4, 22 - f0)

                def dma(slot, f0=f0, F=F):
                    c0 = f0 * 128
                    nc_ = F * 128
                    g = slot.t[:, 0:8 * nc_].rearrange("p (k f) -> p k f", k=8)
                    u = slot.t[:, 8 * nc_:16 * nc_].rearrange("p (k f) -> p k f", k=8)
                    dd = slot.t[:, 16 * nc_:16 * nc_ + F * D].rearrange("p (f d) -> p f d", f=F)
                    return [(g, wg_d[l, w].rearrange("(k p) f -> p k f", p=128)[:, :, c0:c0 + nc_]),
                            (u, wu_d[l, w].rearrange("(k p) f -> p k f", p=128)[:, :, c0:c0 + nc_]),
                            (dd, wd_d[l, w][c0:c0 + nc_, :].rearrange("(f p) d -> p f d", p=128))]

                def items(slot, F=F):
                    nc_ = F * 128
                    Wg = slot.t[:, 0:8 * nc_].rearrange("p (k f) -> p k f", k=8)
                    Wu = slot.t[:, 8 * nc_:16 * nc_].rearrange("p (k f) -> p k f", k=8)
                    Wd = slot.t[:, 16 * nc_:16 * nc_ + F * D].rearrange("p (f d) -> p f d", f=F)
                    gacc = gate_acc(("G",))
                    its = []
                    for tile in tiles:
                        st = {}

                        def Afn(tile=tile, st=st):
                            ti, c0, n, kind = tile
                            if side is not None:
                                side.step()
                            ab = ACTP.get()
                            st["ab"] = ab
                            for f in range(F):
                                bg, rg = bank()
                                bu, ru = bank()
                                tr.mm([(bg[:, :n], Wg[:, kk, f * 128:(f + 1) * 128], Hb[:, kk, c0:c0 + n], kk == 0, kk == 7) for kk in range(8)],
                                      [slot.res] + HR[ti], [rg])
                                tr.mm([(bu[:, :n], Wu[:, kk, f * 128:(f + 1) * 128], Hb[:, kk, c0:c0 + n], kk == 0, kk == 7) for kk in range(8)],
                                      [slot.res] + HR[ti], [ru])
                                sg = T32.get()
                                ACT(sg.t[:, :n], bg[:, :n], AF.Silu, [rg], [sg.res])
                                TT(ab.t[:, f, :n], sg.t[:, :n], bu[:, :n], ALU.mult, [sg.res, ru], [ab.res])

                        def Bfn(tile=tile, st=st):
                            ti, c0, n, kind = tile
                            ab = st["ab"]
                            for i in range(8):
                                bd, rd = bank()
                                tr.mm([(bd[:, :n], Wd[:, f, i * 128:(i + 1) * 128], ab.t[:, f, :n], f == 0, f == F - 1) for f in range(F)],
                                      [slot.res, ab.res], [rd])
                                resid(tile, i, bd, rd, gacc)

                        its.append((Afn, Bfn))
                    return its

                parts.append(dict(dma=dma, items=items))
                f0 += F

        def conv_parts(tiles):
            for p in range(4):
                def dma(slot, p=p):
                    prs = []
                    for q in range(3):
                        v = slot.t[:, q * 2048:(q + 1) * 2048].rearrange("p (k f) -> p k f", k=8)
                        prs.append((v, cwin.rearrange("(k p) f -> p k f", p=128)[:, :, q * D + p * 256:q * D + p * 256 + 256]))
                    v = slot.t[:, 6144:8192].rearrange("p (c d) -> p c d", c=2)
                    prs.append((v, cwout[p * 256:(p + 1) * 256, :].rearrange("(c p) d -> p c d", p=128)))
                    return prs

                def items(slot, p=p):
                    Wb = slot.t[:, 0:2048].rearrange("p (k f) -> p k f", k=8)
                    Wc = slot.t[:, 2048:4096].rearrange("p (k f) -> p k f", k=8)
                    Wv_ = slot.t[:, 4096:6144].rearrange("p (k f) -> p k f", k=8)
                    Wo_ = slot.t[:, 6144:8192].rearrange("p (c d) -> p c d", c=2)
                    gacc = gate_acc(("M", 5))
                    its = []
                    for tile in tiles:
                        st = {}

                        def Afn(tile=tile, st=st):
                            ti, c0, n, kind = tile
                            zb = P2.get()
                            st["zb"] = zb
                            if kind == "halo":
                                tr.op("dve", lambda e: e.memset(CARRY.t[:, :, :], 0.0), [], [CARRY.res])
                            for jl in range(2):
                                j = 2 * p + jl
                                bb, rb = bank()
                                bc, rc = bank()
                                bv, rv = bank()
                                for (bkk, rkk, W) in ((bb, rb, Wb), (bc, rc, Wc), (bv, rv, Wv_)):
                                    tr.mm([(bkk[:, :n], W[:, kk, jl * 128:(jl + 1) * 128], Hb[:, kk, c0:c0 + n], kk == 0, kk == 7) for kk in range(8)],
                                          [slot.res] + HR[ti], [rkk])
                                csb = T32.get()
                                ACT(csb.t[:, :n], bc[:, :n], AF.Identity, [rc], [csb.res])
                                U = T32.get()
                                cvb = T32.get()
                                w0 = VEC.t[:, CW + j:CW + j + 1]
                                w1 = VEC.t[:, CW + 8 + j:CW + 8 + j + 1]
                                w2 = VEC.t[:, CW + 16 + j:CW + 16 + j + 1]
                                if kind != "sample":
                                    CPY("dve", U.t[:, 0:2], CARRY.t[:, jl, :], [CARRY.res], [U.res])
                                    TT(U.t[:, 2:2 + n], csb.t[:, :n], bv[:, :n], ALU.mult, [csb.res, rv], [U.res])
                                    if kind == "halo":
                                        TS(U.t[:, 2:2 + n], U.t[:, 2:2 + n], VEC.t[:, FLAG:FLAG + 1], None, ALU.mult, None, [U.res, VEC.res], [U.res])
                                    CPY("dve", CARRY.t[:, jl, :], U.t[:, n:n + 2], [U.res], [CARRY.res])
                                    if ti == 4:
                                        CPY("dve", CSOP.t[:, j, :], U.t[:, n:n + 2], [U.res], [CSOP.res])
                                    TS(cvb.t[:, :n], U.t[:, 0:n], w0, None, ALU.mult, None, [U.res, VEC.res], [cvb.res])
                                    STT(cvb.t[:, :n], U.t[:, 1:n + 1], w1, cvb.t[:, :n], ALU.mult, ALU.add, [U.res, cvb.res], [cvb.res])
                                    STT(cvb.t[:, :n], U.t[:, 2:n + 2], w2, cvb.t[:, :n], ALU.mult, ALU.add, [U.res, cvb.res], [cvb.res])
                                    TT(zb.t[:, jl, :n], bb[:, :n], cvb.t[:, :n], ALU.mult, [rb, cvb.res], [zb.res])
                                else:
                                    U3 = U.t[:, 0:160].rearrange("p (b i) -> p b i", b=16)
                                    CPY("dve", U3[:, :, 0:2], SCS.t[:, j, :].rearrange("p (b i) -> p b i", b=16), [SCS.res], [U.res])
                                    TT(U3[:, :, 2:10], s3(csb.t[:, :n]), s3(bv[:, :n]), ALU.mult, [csb.res, rv], [U.res])
                                    CPY("dve", CSOS.t[:, j, :].rearrange("p (b i) -> p b i", b=16), U3[:, :, 8:10], [U.res], [CSOS.res])
                                    c3 = s3(cvb.t[:, :n])
                                    TS(c3, U3[:, :, 0:8], w0, None, ALU.mult, None, [U.res, VEC.res], [cvb.res])
                                    STT(c3, U3[:, :, 1:9], w1, c3, ALU.mult, ALU.add, [U.res, cvb.res], [cvb.res])
                                    STT(c3, U3[:, :, 2:10], w2, c3, ALU.mult, ALU.add, [U.res, cvb.res], [cvb.res])
                                    TT(zb.t[:, jl, :n], bb[:, :n], cvb.t[:, :n], ALU.mult, [rb, cvb.res], [zb.res])

                        def Bfn(tile=tile, st=st):
                            ti, c0, n, kind = tile
                            zb = st["zb"]
                            for i in range(8):
                                bd, rd = bank()
                                tr.mm([(bd[:, :n], Wo_[:, c, i * 128:(i + 1) * 128], zb.t[:, c, :n], c == 0, c == 1) for c in range(2)],
                                      [slot.res, zb.res], [rd])
                                resid(tile, i, bd, rd, gacc)

                        its.append((Afn, Bfn))
                    return its

                parts.append(dict(dma=dma, items=items))

        KVT = [(0, 2, 128, "halo")] + TILES1

        def kv_part():
            def dma(slot):
                prs = []
                for (off, src) in ((0, wk_d), (4096, wksw_d)):
                    v = slot.t[:, off:off + 4096].rearrange("p (k g e d) -> p k g e d", k=8, g=4, e=2)
                    s = src.rearrange("(k p) (g d) -> p k g d", p=128, g=4)
                    for e in range(2):
                        for g_ in range(4):
                            prs.append((v[:, :, g_, e, :], s[:, :, g_, :]))
                v = slot.t[:, 8192:8192 + 2048].rearrange("p (k d) -> p k d", k=8)
                prs.append((v, wv_d.rearrange("(k p) d -> p k d", p=128)))
                return prs

            def items(slot):
                Wk = slot.t[:, 0:4096].rearrange("p (k g m) -> p k g m", k=8, g=4)
                Wks = slot.t[:, 4096:8192].rearrange("p (k g m) -> p k g m", k=8, g=4)
                Wv_ = slot.t[:, 8192:8192 + 2048].rearrange("p (k d) -> p k d", k=8)
                its = []
                for tile in KVT:
                    def Afn(tile=tile):
                        ti, c0, n, kind = tile
                        rp = ACTP.get()
                        rv_ = f32view(rp).rearrange("p (a b) -> p a b", a=2)
                        tr.dma("sp", [(rv_[:, 0, 0:n], ropec[:, c0 - 2:c0 - 2 + n]), (rv_[:, 1, 0:n], ropes[:, c0 - 2:c0 - 2 + n])], rp.ds, writes=[rp.res])
                        is_out = (ti == 4) or (ti == 5)
                        for g in range(4):
                            bk_, rk_ = bank()
                            bs_, rs_ = bank()
                            tr.mm([(bk_[:, :n], Wk[:, kk, g, :], Hb[:, kk, c0:c0 + n], kk == 0, kk == 7) for kk in range(8)], [slot.res] + HR[ti], [rk_])
                            tr.mm([(bs_[:, :n], Wks[:, kk, g, :], Hb[:, kk, c0:c0 + n], kk == 0, kk == 7) for kk in range(8)], [slot.res] + HR[ti], [rs_])
                            t1 = T32.get()
                            t2 = T32.get()
                            TT(t1.t[:, :n], bk_[:, :n], rv_[:, 0, 0:n], ALU.mult, [rk_, rp.res], [t1.res])
                            TT(t2.t[:, :n], bs_[:, :n], rv_[:, 1, 0:n], ALU.mult, [rs_, rp.res], [t2.res])
                            TT(t1.t[:, :n], t1.t[:, :n], t2.t[:, :n], ALU.add, [t1.res, t2.res], [t1.res])
                            kb = TB.get()
                            ACT(kb.t[:, :n], t1.t[:, :n], AF.Identity, [t1.res], [kb.res])
                            if "a" not in KDBG:
                                tr.dma("sp", [(kt_scr[g, :, c0 - 2:c0 - 2 + n], kb.t[:, :n])], kb.ds, reads=[kb.res], writes=[KTSCR[g][ti]])
                            if is_out and "b" not in KDBG:
                                bt_, rt_ = bank()
                                TRANSP(bt_[:, 0:128], t1.t[:, n - 128:n], IDF.t[:, :], [t1.res, IDF.res], [rt_])
                                if ti not in st_k:
                                    st_k[ti] = KVO.get()
                                ko = st_k[ti]
                                evac(ko.t[:, g * 64:(g + 1) * 64], bt_[:, 0:64], [rt_], [ko.res])
                        if is_out and "b" not in KDBG:
                            ko = st_k[ti]
                            if ti == 4:
                                out_events.append(tr.dma("sp", [(kout_p, ko.t[:, :])], ko.ds, reads=[ko.res]))
                            else:
                                out_events.append(tr.dma("sp", [(kout_s[b, 120:128, :], ko.t[b * 8:(b + 1) * 8, :]) for b in range(16)], ko.ds, reads=[ko.res]))

                    def Bfn(tile=tile):
                        ti, c0, n, kind = tile
                        if "c" in KDBG:
                            return
                        nb = n // 128
                        vb = P2.get()
                        vv = vb.t[:, :, :].rearrange("p a b -> p (a b)").rearrange("p (b d) -> p b d", b=4)
                        is_out = (ti == 4) or (ti == 5)
                        for b_ in range(nb):
                            bk_, rk_ = bank()
                            tr.mm([(bk_[:, 0:256], Hb[:, kk, c0 + b_ * 128:c0 + (b_ + 1) * 128], Wv_[:, kk, :], kk == 0, kk == 7) for kk in range(8)],
                                  [slot.res] + HR[ti], [rk_])
                            if not (is_out and b_ == nb - 1):
                                evac(vv[:, b_, :], bk_[:, 0:256], [rk_], [vb.res])
                            else:
                                vo = KVO.get()
                                CPY("dve", vo.t[:, :], bk_[:, 0:256], [rk_], [vo.res])
                                CPY("act", vv[:, b_, :], vo.t[:, :], [vo.res], [vb.res])
                                if ti == 4:
                                    out_events.append(tr.dma("sp", [(vout_p, vo.t[:, :])], vo.ds, reads=[vo.res]))
                                else:
                                    out_events.append(tr.dma("sp", [(vout_s[b, 120:128, :], vo.t[b * 8:(b + 1) * 8, :]) for b in range(16)], vo.ds, reads=[vo.res]))
                        blk0 = (c0 - 2) // 128
                        if "d" not in KDBG:
                            tr.dma("sp", [(v_scr[blk0:blk0 + nb, :, :].rearrange("b p d -> p b d"), vv[:, 0:nb, :])], vb.ds, reads=[vb.res], writes=[VSCR[ti]])

                    its.append((Afn, Bfn))
                return its

            parts.append(dict(dma=dma, items=items))

        st_k = {}

        def attn_parts(tiles):
            for g in range(4):
                def dma(slot, g=g):
                    prs = []
                    prs.append((slot.t[:, 0:2048].rearrange("p (k f) -> p k f", k=8), wq_d.rearrange("(k p) f -> p k f", p=128)[:, :, g * 256:(g + 1) * 256]))
                    prs.append((slot.t[:, 2048:4096].rearrange("p (k f) -> p k f", k=8), wqsw_d.rearrange("(k p) f -> p k f", p=128)[:, :, g * 256:(g + 1) * 256]))
                    prs.append((slot.t[:, 4096:6144].rearrange("p (c d) -> p c d", c=2), wo_d[g * 256:(g + 1) * 256, :].rearrange("(c p) d -> p c d", p=128)))
                    prs.append((slot.t[:, 6144:6144 + TKV], kt_scr[g, :, :]))
                    prs.append((slot.t[:, 8448:8448 + 1152].rearrange("p (b d) -> p b d", b=18), v_scr[:, :, g * 64:(g + 1) * 64].rearrange("b p d -> p b d")))
                    prs.append((slot.t[:, 9600:9600 + 1024].rearrange("p (b d) -> p b d", b=16), kcache[:, :, g * 64:(g + 1) * 64].rearrange("b p d -> p b d")))
                    prs.append((slot.t[:, 10624:10624 + 1024].rearrange("p (b d) -> p b d", b=16), vcache[:, :, g * 64:(g + 1) * 64].rearrange("b p d -> p b d")))
                    return prs

                def items(slot, g=g):
                    Wq = slot.t[:, 0:2048].rearrange("p (k f) -> p k f", k=8)
                    Wqs = slot.t[:, 2048:4096].rearrange("p (k f) -> p k f", k=8)
                    Wo_ = slot.t[:, 4096:6144].rearrange("p (c d) -> p c d", c=2)
                    KT = slot.t[:, 6144:6144 + TKV]
                    Vg = slot.t[:, 8448:8448 + 1152].rearrange("p (b d) -> p b d", b=18)
                    kc = slot.t[:, 9600:9600 + 1024].rearrange("p (b d) -> p b d", b=16)
                    vc = slot.t[:, 10624:10624 + 1024].rearrange("p (b d) -> p b d", b=16)
                    gacc = gate_acc(("M", 5))
                    esk = ESINK.t[:, g * 2:(g + 1) * 2].unsqueeze(2).broadcast_to([128, 2, 128])
                    its = []

                    def Pfn():
                        for q4 in range(4):
                            bk_, rk_ = bank()
                            for bl in range(4):
                                b = q4 * 4 + bl
                                for e in range(2):
                                    tr.mm([(bk_[e * 64:(e + 1) * 64, bl * 128:(bl + 1) * 128], kc[:, b, :], IDB.t[:, :], True, True)], [slot.res, IDB.res], [rk_])
                            evac(KTC.t[:, q4 * 4:(q4 + 1) * 4, :], bk_[:, :].rearrange("p (b k) -> p b k", b=4), [rk_], [KTC.res])

                    for tile in tiles:
                        st = {}

                        def Afn(tile=tile, st=st):
                            ti, c0, n, kind = tile
                            if kind == "sample":
                                Pfn()
                            rp = ACTP.get()
                            rv_ = f32view(rp).rearrange("p (a b) -> p a b", a=2)
                            tr.dma("sp", [(rv_[:, 0, 0:n], ropec[:, c0 - 2:c0 - 2 + n]), (rv_[:, 1, 0:n], ropes[:, c0 - 2:c0 - 2 + n])], rp.ds, writes=[rp.res])
                            qt = P2.get()
                            st["qt"] = qt
                            for c in range(2):
                                bq, rq = bank()
                                bs_, rs_ = bank()
                                tr.mm([(bq[:, :n], Wq[:, kk, c * 128:(c + 1) * 128], Hb[:, kk, c0:c0 + n], kk == 0, kk == 7) for kk in range(8)], [slot.res] + HR[ti], [rq])
                                tr.mm([(bs_[:, :n], Wqs[:, kk, c * 128:(c + 1) * 128], Hb[:, kk, c0:c0 + n], kk == 0, kk == 7) for kk in range(8)], [slot.res] + HR[ti], [rs_])
                                t1 = T32.get()
                                t2 = T32.get()
                                TT(t1.t[:, :n], bq[:, :n], rv_[:, 0, 0:n], ALU.mult, [rq, rp.res], [t1.res])
                                TT(t2.t[:, :n], bs_[:, :n], rv_[:, 1, 0:n], ALU.mult, [rs_, rp.res], [t2.res])
                                PTT(qt.t[:, c, :n], t1.t[:, :n], t2.t[:, :n], ALU.add, [t1.res, t2.res], [qt.res])

                        def Bfn(tile=tile, st=st):
                            ti, c0, n, kind = tile
                            if "q" in KDBG or ("s" in KDBG and kind == "sample") or ("m" in KDBG and kind != "sample"):
                                return
                            qt = st["qt"]
                            ot = P2.get()
                            bst = {}

                            def stage1(blk):
                                q0 = blk * 128
                                kown = c0 - 2 + q0
                                be = [bank(), bank()]
                                pe_ = [TB.get(), TB.get()]
                                mlist = []
                                if kind != "sample":
                                    for (c_lo, k_lo) in ((0, kown - 128), (256, kown)):
                                        for e in range(2):
                                            es_ = slice(e * 64, (e + 1) * 64)
                                            mlist.append((be[e][0][:, c_lo:c_lo + 256].rearrange("p (c q) -> p c q", c=2), KT[es_, k_lo:k_lo + 128], qt.t[es_, :, q0:q0 + 128], True, True))
                                else:
                                    for b in range(16):
                                        for c_ in range(2):
                                            for e in range(2):
                                                es_ = slice(e * 64, (e + 1) * 64)
                                                o = be[e][0][:, c_ * 128 + b * 8:c_ * 128 + b * 8 + 8]
                                                mlist.append((o, KTC.t[es_, b, :], qt.t[es_, c_, b * 8:(b + 1) * 8], True, True))
                                    for e in range(2):
                                        es_ = slice(e * 64, (e + 1) * 64)
                                        mlist.append((be[e][0][:, 256:512].rearrange("p (c q) -> p c q", c=2), KT[es_, kown:kown + 128], qt.t[es_, :, q0:q0 + 128], True, True))
                                tr.mm(mlist, [slot.res, KTC.res, qt.res], [be[0][1], be[1][1]])
                                if kind != "sample":
                                    mi = 2 if (ti == 1 and blk == 0) else 0
                                else:
                                    mi = 4
                                msk = MASKS.t[:, mi:mi + 2, :].unsqueeze(2).broadcast_to([128, 2, 2, 128])
                                for e in range(2):
                                    bke, rke = be[e]
                                    ACT(pe_[e].t[:, :], bke[:, :], AF.Exp, [rke], [pe_[e].res], scale=0.125)
                                    p4 = pe_[e].t[:, :].rearrange("p (t c q) -> p t c q", t=2, c=2)
                                    PTT(p4, p4, msk, ALU.mult, [pe_[e].res, MASKS.res], [pe_[e].res])
                                bst[blk] = pe_

                            def stage2(blk):
                                q0 = blk * 128
                                kown = c0 - 2 + q0
                                vown = kown // 128
                                pe_ = bst.pop(blk)
                                bd_, rd_ = bank()
                                tr.mm([(bd_[e * 64:(e + 1) * 64, 0:256], ONES.t[:, 0:64], pe_[e].t[:, t_ * 256:(t_ + 1) * 256], t_ == 0, t_ == 1) for t_ in range(2) for e in range(2)],
                                      [pe_[0].res, pe_[1].res, ONES.res], [rd_])
                                rden = RS.get()
                                r3 = rden.t[:, 0:256].rearrange("p (h q) -> p h q", h=2)
                                TT(r3, bd_[:, 0:256].rearrange("p (h q) -> p h q", h=2), esk, ALU.add, [rd_, ESINK.res], [rden.res])
                                tr.op("dve", lambda e, rden=rden: e.reciprocal(out=rden.t[:, 0:256], in_=rden.t[:, 0:256]), [rden.res], [rden.res])
                                bo, ro = bank()
                                mlist = []
                                if kind != "sample":
                                    for (vb_, lo, st_, sp_) in ((vown - 1, 0, True, False), (vown, 256, False, True)):
                                        for e in range(2):
                                            es_ = slice(e * 64, (e + 1) * 64)
                                            mlist.append((bo[es_, 0:256], Vg[:, vb_, :], pe_[e].t[:, lo:lo + 256], st_, sp_))
                                else:
                                    for e in range(2):
                                        es_ = slice(e * 64, (e + 1) * 64)
                                        mlist.append((bo[es_, 0:256], Vg[:, vown, :], pe_[e].t[:, 256:512], True, False))
                                    for b in range(16):
                                        for c_ in range(2):
                                            for e in range(2):
                                                es_ = slice(e * 64, (e + 1) * 64)
                                                o = bo[es_, c_ * 128 + b * 8:c_ * 128 + b * 8 + 8]
                                                r_ = pe_[e].t[:, c_ * 128 + b * 8:c_ * 128 + b * 8 + 8]
                                                mlist.append((o, vc[:, b, :], r_, False, (b == 15 and c_ == 1)))
                                tr.mm(mlist, [slot.res, pe_[0].res, pe_[1].res], [ro])
                                TT(ot.t[:, :, q0:q0 + 128], bo[:, 0:256].rearrange("p (c q) -> p c q", c=2),
                                   rden.t[:, 0:256].rearrange("p (c q) -> p c q", c=2), ALU.mult, [ro, rden.res], [ot.res])

                            nblk = n // 128
                            for blk in range(nblk):
                                stage1(blk)
                                if blk >= 1:
                                    stage2(blk - 1)
                            stage2(nblk - 1)
                            for i in range(8):
                                bd, rd = bank()
                                tr.mm([(bd[:, :n], Wo_[:, c, i * 128:(i + 1) * 128], ot.t[:, c, :n], c == 0, c == 1) for c in range(2)], [slot.res, ot.res], [rd])
                                resid(tile, i, bd, rd, gacc)

                        its.append((Afn, Bfn))
                    return its

                parts.append(dict(dma=dma, items=items, extra=[KTSCR[g][t_] for t_ in range(6)] + VSCR))

        def final_items(tiles):
            base = norm_items(tiles, None, None, final=True)
            its = []
            for (Afn, Bfn, st), tile in zip(base, tiles):
                def B2(tile=tile, st=st, Bfn=Bfn):
                    ti, c0, n, kind = tile
                    Bfn()
                    rs = st["rs"]
                    for blk in range(n // 128):
                        q0 = blk * 128
                        sg_ = ACTP.get()
                        sv_ = f32view(sg_)
                        for half in range(2):
                            bk_, rk_ = bank()
                            for jj in range(4):
                                j = half * 4 + jj
                                yt = T32.get()
                                STT(yt.t[:, 0:128], X[:, j, c0 + q0:c0 + q0 + 128], VEC.t[:, FNG + j:FNG + j + 1], rs.t[:, q0:q0 + 128], ALU.mult, ALU.mult,
                                    [XR[ti][j], VEC.res, rs.res], [yt.res])
                                TRANSP(bk_[:, jj * 128:(jj + 1) * 128], yt.t[:, 0:128], IDF.t[:, :], [yt.res, IDF.res], [rk_])
                            evac(sv_[:, half * 512:(half + 1) * 512], bk_[:, :], [rk_], [sg_.res])
                        if kind == "sample":
                            dst = y_s
                        else:
                            r0 = c0 - 130 + q0
                            dst = y_p[r0:r0 + 128, :]
                        out_events.append(tr.dma("sp", [(dst, sv_[:, :])], sg_.ds, reads=[sg_.res]))

                its.append((Afn, B2))
            parts.append(dict(dma=None, items=lambda slot: its))

        def cso_out():
            def Afn():
                sg_ = ACTP.get()
                sv_ = f32view(sg_)
                for half in range(2):
                    bk_, rk_ = bank()
                    for jj in range(4):
                        TRANSP(bk_[0:32, jj * 128:(jj + 1) * 128], CSOS.t[:, half * 4 + jj, :], IDF.t[:, :], [CSOS.res, IDF.res], [rk_])
                    evac(sv_[0:32, half * 512:(half + 1) * 512], bk_[0:32, :], [rk_], [sg_.res])
                out_events.append(tr.dma("sp", [(cso_s, sv_[0:32, :])], sg_.ds, reads=[sg_.res]))
                sg2 = ACTP.get()
                sv2 = f32view(sg2)
                for half in range(2):
                    bk3, rk3 = bank()
                    for jj in range(4):
                        TRANSP(bk3[0:2, jj * 128:(jj + 1) * 128], CSOP.t[:, half * 4 + jj, :], IDF.t[:, :], [CSOP.res, IDF.res], [rk3])
                    evac(sv2[0:2, half * 512:(half + 1) * 512], bk3[0:2, :], [rk3], [sg2.res])
                out_events.append(tr.dma("sp", [(cso_p, sv2[0:2, :])], sg2.ds, reads=[sg2.res]))

            parts.append(dict(dma=None, items=lambda slot: [(Afn, None)]))

        side_l0 = Side(ada_micros(w_ada[0], 9 * D, MOD, 0, BADA)[24:], 2)
        side_l1 = Side(ada_micros(wadakv, 2 * D, MODKV, 0, BKV) + ada_micros(w_ada[1], 9 * D, MOD, 0, BADA + 72), 3)

        def flush_part(sd):
            parts.append(dict(dma=None, items=lambda slot: [(sd.flush, None)]))

        ada_part(w_ada[0], 0, 1536, MOD, 0, BADA + 0)
        ada_part(w_ada[0], 1536, 1536, MOD, 12, BADA + 12)
        add_plain(norm_items(TILES0, lambda j: modv(0, j), MOD.res, pre=lambda: prep_mod(0, 0, True)))
        ffn_parts(0, 0, TILES0, side=side_l0)
        flush_part(side_l0)
        add_plain(norm_items(TILES0, lambda j: modv(3, j), MOD.res, pre=lambda: prep_mod(0, 1, False)))
        conv_parts(TILES0)
        cso_out()
        add_plain(norm_items(TILES0, lambda j: modv(6, j), MOD.res, pre=lambda: prep_mod(0, 2, True)))
        ffn_parts(0, 1, TILES0, side=side_l1)
        flush_part(side_l1)
        add_plain(norm_items(TILES0, lambda j: MODKV.t[:, j, :], MODKV.res, pre=prep_mod_kv))
        kv_part()
        add_plain(norm_items(TILES1, lambda j: modv(0, j), MOD.res, pre=lambda: prep_mod(1, 0, True)))
        ffn_parts(1, 0, TILES1)
        add_plain(norm_items(TILES1, lambda j: modv(3, j), MOD.res, pre=lambda: prep_mod(1, 1, False)))
        attn_parts(TILES1)
        add_plain(norm_items(TILES1, lambda j: modv(6, j), MOD.res, pre=lambda: prep_mod(1, 2, True)))
        ffn_parts(1, 1, TILES1)
        final_items(TILES1)

        import os as _os
        _ks = _os.environ.get("KSTOP")
        if _ks is not None:
            parts = parts[:int(_ks)]
        dparts = [p for p in parts if p["dma"] is not None]
        for k, p in enumerate(dparts):
            p["slot"] = RING[k % 2]
            p["didx"] = k

        def issue(k):
            if k < len(dparts):
                p = dparts[k]
                slot = p["slot"]
                tr.dma("pool", p["dma"](slot), slot.ds, reads=p.get("extra", []), writes=[slot.res])

        issue(0)
        prevB = [None]
        for p in parts:
            slot = p.get("slot")
            its = p["items"](slot)
            for n_, it in enumerate(its):
                a, b = it[0], it[1]
                if a:
                    a()
                if prevB[0]:
                    prevB[0]()
                prevB[0] = b
                if n_ == 0 and p["dma"] is not None:
                    issue(p["didx"] + 1)
        if prevB[0]:
            prevB[0]()

        tr.wait_only("sp", out_events)

        block = es.enter_context(nc.Block())

        @block.tensor
        def _(e):
            for f in tr.q["pe"]:
                f(e)

        @block.scalar
        def _(e):
            for f in tr.q["act"]:
                f(e)

        @block.vector
        def _(e):
            for f in tr.q["dve"]:
                f(e)

        @block.gpsimd
        def _(e):
            for f in tr.q["pool"]:
                f(e)

        @block.sync
        def _(e):
            for f in tr.q["sp"]:
                f(e)
    return nc


_NC = [None]


def _rope_tables(pos):
    inv = (np.float32(500000.0) ** (-np.arange(0, 16, 2, dtype=np.float32) / np.float32(16))).astype(np.float32)
    ang = pos.astype(np.float32)[None, :] * inv[:, None]
    cos = np.cos(ang).astype(np.float32)
    sin = np.sin(ang).astype(np.float32)
    n = pos.shape[0]
    C = np.ones((64, n), np.float32)
    S = np.zeros((64, n), np.float32)
    C[0:8] = cos
    C[8:16] = cos
    S[0:8] = -sin
    S[8:16] = sin
    return C, S


def _swap_cols(w, nheads):
    perm = []
    for h in range(nheads):
        base = h * 64
        perm += [base + 8 + d for d in range(8)] + [base + d for d in range(8)] + [base + d for d in range(16, 64)]
    return np.ascontiguousarray(w[:, perm])


def kernel(x_prompt, x_sample, state_conv, cache_k_win, cache_v_win, c_prompt, c_sample,
           norm_g, w_ada, b_ada, w_ffn_gate, w_ffn_up, w_ffn_down,
           conv_w_in, conv_w, conv_w_out, kv_norm_g, w_ada_kv, b_ada_kv, w_k, w_v,
           attn_w_q, attn_sinks, attn_w_o, final_norm_g):
    f = lambda a: np.ascontiguousarray(np.asarray(a, dtype=np.float32))
    x_prompt, x_sample, state_conv = f(x_prompt), f(x_sample), f(state_conv)
    cache_k_win, cache_v_win, c_prompt, c_sample = f(cache_k_win), f(cache_v_win), f(c_prompt), f(c_sample)
    if _NC[0] is None:
        _NC[0] = build()
    nc = _NC[0]

    def fm(v):
        v = f(v)
        lead = v.shape[:-1]
        return np.moveaxis(v.reshape(lead + (8, 128)), -1, 0)

    vec_common = np.zeros((128, NV), np.float32)
    vec_common[:, NG0:NG0 + 48] = fm(norm_g).reshape(128, 48)
    vec_common[:, KVG:KVG + 8] = fm(kv_norm_g).reshape(128, 8)
    vec_common[:, FNG:FNG + 8] = fm(final_norm_g).reshape(128, 8)
    ba = f(b_ada).reshape(2, 72, 128)
    vec_common[:, BADA:BADA + 144] = np.moveaxis(ba, -1, 0).reshape(128, 144)
    vec_common[:, BKV:BKV + 16] = np.moveaxis(f(b_ada_kv).reshape(16, 128), -1, 0)
    vec_common[:, CW:CW + 24] = fm(f(conv_w)[0]).reshape(128, 24)
    sk = f(attn_sinks)[0]
    for e in range(2):
        sperm = [4 * g + 2 * c + e for g in range(4) for c in range(2)]
        vec_common[e * 64:(e + 1) * 64, SINK:SINK + 8] = np.broadcast_to(sk[sperm][None, :], (64, 8))

    ident = np.eye(128, dtype=np.float32)
    jj = np.arange(128)[:, None]
    ii = np.arange(128)[None, :]
    mA = (jj > ii).astype(np.float32)
    mB = (jj <= ii).astype(np.float32)
    mAs = (jj > (ii % 8)).astype(np.float32)
    mBs = (((jj // 8) == (ii // 8)) & ((jj % 8) <= (ii % 8))).astype(np.float32)

    shared = dict(
        ident=ident, w_ada=f(w_ada), wg=f(w_ffn_gate), wu=f(w_ffn_up), wd=f(w_ffn_down),
        cwin=f(conv_w_in)[0], cwout=f(conv_w_out)[0], wadakv=f(w_ada_kv),
        wk=f(w_k), wksw=_swap_cols(f(w_k), 4), wv=f(w_v),
        wq=f(attn_w_q)[0], wqsw=_swap_cols(f(attn_w_q)[0], 16), wo=f(attn_w_o)[0],
    )
    in_maps = []
    for c in range(NCORE):
        b, q = c // 4, c % 4
        s0 = q * 2048
        xin = np.zeros((T, D), np.float32)
        if q > 0:
            xin[0:130] = x_prompt[b, s0 - 130:s0]
        xin[130:2178] = x_prompt[b, s0:s0 + 2048]
        xin[2178:2306] = x_sample[16 * c:16 * c + 16].reshape(128, D)
        cin = np.concatenate([c_prompt[b:b + 1], c_sample[16 * c:16 * c + 16]], axis=0)
        scin = state_conv[0, 16 * c:16 * c + 16].reshape(32, D)
        pos = np.concatenate([np.arange(s0 - 128, s0 + 2048), np.tile(16384 + np.arange(8), 16)]).astype(np.int64)
        C, S = _rope_tables(pos)
        masks = np.stack([mA, mB, mA if q > 0 else np.zeros_like(mA), mB, mAs, mBs], axis=1)
        vec = vec_common.copy()
        vec[:, FLAG] = 1.0 if q > 0 else 0.0
        m = dict(shared)
        m.update(xin=xin, cin=np.ascontiguousarray(cin), scin=np.ascontiguousarray(scin),
                 kcache=np.ascontiguousarray(cache_k_win[16 * c:16 * c + 16].reshape(16, 128, 256)),
                 vcache=np.ascontiguousarray(cache_v_win[16 * c:16 * c + 16].reshape(16, 128, 256)),
                 ropec=np.ascontiguousarray(np.concatenate([C, C], axis=0)), ropes=np.ascontiguousarray(np.concatenate([S, S], axis=0)),
                 masks=np.ascontiguousarray(masks), vec=vec)
        in_maps.append(m)
    res = run_bass_kernel_spmd(nc, in_maps, core_ids=list(range(NCORE)))
    R = res.results
    y_prompt = np.stack([np.concatenate([R[4 * b + q]["y_p"] for q in range(4)], axis=0) for b in range(2)], axis=0)
    y_sample = np.concatenate([R[c]["y_s"].reshape(16, 8, D) for c in range(NCORE)], axis=0)
    conv_p = np.stack([R[3]["cso_p"], R[7]["cso_p"]], axis=0)[None]
    conv_s = np.concatenate([R[c]["cso_s"].reshape(16, 2, D) for c in range(NCORE)], axis=0)[None]
    k_p = np.stack([R[3]["kout_p"], R[7]["kout_p"]], axis=0).reshape(2, 128, 4, 64)
    v_p = np.stack([R[3]["vout_p"], R[7]["vout_p"]], axis=0).reshape(2, 128, 4, 64)
    k_s = np.concatenate([R[c]["kout_s"] for c in range(NCORE)], axis=0).reshape(128, 128, 4, 64)
    v_s = np.concatenate([R[c]["vout_s"] for c in range(NCORE)], axis=0).reshape(128, 128, 4, 64)
    outs = (y_prompt, y_sample, conv_p, conv_s, k_p, v_p, k_s, v_s)
    return tuple(np.ascontiguousarray(o.astype(np.float32)) for o in outs)
```

```python
import numpy as np
from contextlib import ExitStack
import concourse.bass as bass
import concourse.mybir as mybir
from concourse.bass_utils import run_bass_kernel_spmd

F32 = mybir.dt.float32
BF16 = mybir.dt.bfloat16
AF = mybir.ActivationFunctionType
ALU = mybir.AluOpType

D = 1024
DFF = 2816
NCORE = 8
T = 2306
TKV = 2304
SLOT = 12288
EPS = 1e-6
NG0, KVG, FNG, BADA, BKV, CW, SINK, FLAG, NV = 0, 48, 56, 64, 208, 224, 248, 264, 265


class Res:
    __slots__ = ("w", "r")

    def __init__(self):
        self.w = None
        self.r = {}


class Tracker:
    ENG = ("pe", "act", "dve", "pool", "sp")

    def __init__(self, sems):
        self.sems = sems
        self.q = {e: [] for e in self.ENG}
        self.cnt = {e: 0 for e in self.ENG}
        self.waited = {e: {} for e in self.ENG}

    def _waits(self, eng, reads, writes, strict=False):
        need = {}

        def add(ev, same_ok):
            if ev is None:
                return
            sem, val, src = ev
            if src == eng and not same_ok and not strict:
                return
            k = id(sem)
            if k not in need or need[k][1] < val:
                need[k] = (sem, val)

        for r in reads:
            add(r.w, True)
        for w in writes:
            add(w.w, False)
            for ev in w.r.values():
                add(ev, False)
        out = []
        for k, (sem, val) in need.items():
            if self.waited[eng].get(k, 0) >= val:
                continue
            self.waited[eng][k] = val
            out.append((sem, val))
        return out

    def op(self, eng, fn, reads=(), writes=()):
        waits = self._waits(eng, reads, writes, strict=(eng != "pe"))
        self.cnt[eng] += 1
        sem = self.sems[eng]
        ev = (sem, self.cnt[eng], eng)

        def run(e):
            for s, v in waits:
                e.wait_ge(s, v)
            fn(e).then_inc(sem, 1)

        self.q[eng].append(run)
        for r in reads:
            r.r[eng] = ev
        for w in writes:
            w.w = ev
            w.r = {}
        return ev

    def mm(self, mms, reads, writes):
        def fn(e):
            ins = None
            for (o, l, r, st, sp) in mms:
                ins = e.matmul(o, l, r, start=st, stop=sp)
            return ins

        return self.op("pe", fn, reads, writes)

    def dma(self, qeng, pairs, dsem, reads=(), writes=()):
        waits = self._waits(qeng, reads, writes, strict=True)
        dsem[1] += 16 * len(pairs)
        sem = dsem[0]
        ev = (sem, dsem[1], "dma")

        def run(e):
            for s, v in waits:
                e.wait_ge(s, v)
            for (o, i) in pairs:
                e.dma_start(out=o, in_=i).then_inc(sem, 16)

        self.q[qeng].append(run)
        for r in reads:
            r.r["dma" + str(id(sem))] = ev
        for w in writes:
            w.w = ev
            w.r = {}
        return ev

    def wait_only(self, eng, evs):
        ws = []
        for (sem, val, _) in evs:
            ws.append((sem, val))

        def run(e):
            for s, v in ws:
                e.wait_ge(s, v)

        self.q[eng].append(run)


class Buf:
    def __init__(self, t, mk_sem):
        self.t = t
        self.res = Res()
        self._mk = mk_sem
        self._ds = None

    @property
    def ds(self):
        if self._ds is None:
            self._ds = [self._mk(), 0]
        return self._ds

    @property
    def ds_pool(self):
        if getattr(self, "_ds2", None) is None:
            self._ds2 = [self._mk(), 0]
        return self._ds2


class Rot:
    def __init__(self, bufs):
        self.bufs = bufs
        self.i = 0

    def get(self):
        b = self.bufs[self.i % len(self.bufs)]
        self.i += 1
        return b


def build():
    nc = bass.Bass("TRN2", target_bir_lowering=False)
    es = ExitStack()
    with es:
        import os as _os0
        KDBG = _os0.environ.get("KDBG", "")
        nsem = [0]

        def mk_sem():
            nsem[0] += 1
            return es.enter_context(nc.semaphore("s%d" % nsem[0]))

        def din(name, shape):
            return nc.dram_tensor(name, shape, F32, kind="ExternalInput").ap()

        def dout(name, shape):
            return nc.dram_tensor(name, shape, F32, kind="ExternalOutput").ap()

        xin = din("xin", [T, D])
        cin = din("cin", [17, D])
        scin = din("scin", [32, D])
        kcache = din("kcache", [16, 128, 256])
        vcache = din("vcache", [16, 128, 256])
        ropec = din("ropec", [128, TKV])
        ropes = din("ropes", [128, TKV])
        masks_d = din("masks", [128, 6, 128])
        vec_d = din("vec", [128, NV])
        ident_d = din("ident", [128, 128])
        w_ada = din("w_ada", [2, D, 9 * D])
        wg_d = din("wg", [2, 2, D, DFF])
        wu_d = din("wu", [2, 2, D, DFF])
        wd_d = din("wd", [2, 2, DFF, D])
        cwin = din("cwin", [D, 3 * D])
        cwout = din("cwout", [D, D])
        wadakv = din("wadakv", [D, 2 * D])
        wk_d = din("wk", [D, 256])
        wksw_d = din("wksw", [D, 256])
        wv_d = din("wv", [D, 256])
        wq_d = din("wq", [D, D])
        wqsw_d = din("wqsw", [D, D])
        wo_d = din("wo", [D, D])

        y_p = dout("y_p", [2048, D])
        y_s = dout("y_s", [128, D])
        cso_p = dout("cso_p", [2, D])
        cso_s = dout("cso_s", [32, D])
        kout_p = dout("kout_p", [128, 256])
        vout_p = dout("vout_p", [128, 256])
        kout_s = dout("kout_s", [16, 128, 256])
        vout_s = dout("vout_s", [16, 128, 256])

        kt_scr = nc.dram_tensor("kt_scr", [4, 128, TKV], BF16).ap()
        v_scr = nc.dram_tensor("v_scr", [18, 128, 256], BF16).ap()

        def sb(name, shape, dt):
            return es.enter_context(nc.sbuf_tensor(name, shape, dt))

        def mkbuf(name, shape, dt):
            return Buf(sb(name, shape, dt), mk_sem)

        X = sb("X", [128, 8, T], F32)
        Hb = sb("Hb", [128, 8, T], BF16)
        RING = [mkbuf("ring%d" % i, [128, SLOT], BF16) for i in range(2)]
        MOD = mkbuf("MOD", [128, 72, 17], F32)
        MODKV = mkbuf("MODKV", [128, 16, 17], F32)
        SC = mkbuf("SC", [128, 8, 17], BF16)
        Ab = mkbuf("Ab", [128, 8, 17], F32)
        Gb = mkbuf("Gb", [128, 8, 17], F32)
        VEC = mkbuf("VEC", [128, NV], F32)
        IDF = mkbuf("IDF", [128, 128], F32)
        IDB = mkbuf("IDB", [128, 128], BF16)
        ONES = mkbuf("ONES", [128, 128], BF16)
        MASKS = mkbuf("MASKS", [128, 6, 128], BF16)
        ESINK = mkbuf("ESINK", [128, 8], F32)
        ACTP = Rot([mkbuf("act%d" % i, [128, 4, 512], BF16) for i in range(2)])
        T32 = Rot([mkbuf("t32_%d" % i, [128, 520], F32) for i in range(4)])
        RS = Rot([mkbuf("rs%d" % i, [128, 512], F32) for i in range(2)])
        TB = Rot([mkbuf("tb%d" % i, [128, 512], BF16) for i in range(4)])
        P2 = Rot([mkbuf("p2_%d" % i, [128, 2, 512], BF16) for i in range(4)])
        KTC = mkbuf("KTC", [128, 16, 128], BF16)
        CARRY = mkbuf("CARRY", [128, 2, 2], F32)
        CSOP = mkbuf("CSOP", [128, 8, 2], F32)
        CSOS = mkbuf("CSOS", [128, 8, 32], F32)
        SCS = mkbuf("SCS", [128, 8, 32], F32)
        KVO = Rot([mkbuf("kvo%d" % i, [128, 256], F32) for i in range(2)])

        banks = []
        for i in range(8):
            pt = es.enter_context(nc.psum_tensor("bk%d" % i, [128, 512], F32))
            banks.append((pt, Res()))
        bank_i = [0]

        def bank():
            b = banks[bank_i[0] % 6]
            bank_i[0] += 1
            return b

        nbank_i = [0]

        def nbank():
            b = banks[6 + nbank_i[0] % 2]
            nbank_i[0] += 1
            return b

        sems = {e: mk_sem() for e in Tracker.ENG}
        tr = Tracker(sems)

        TILES0 = [(0, 0, 130, "halo")] + [(1 + k, 130 + 512 * k, 512, "prompt") for k in range(4)] + [(5, 2178, 128, "sample")]
        TILES1 = TILES0[1:]
        XR = [[Res() for _ in range(8)] for _ in range(6)]
        HR = [[Res() for _ in range(8)] for _ in range(6)]
        KTSCR = [[Res() for _ in range(6)] for _ in range(4)]
        VSCR = [Res() for _ in range(6)]
        out_events = []

        def ACT(out, in_, func, reads, writes, bias=None, scale=None):
            kw = {}
            if bias is not None:
                kw["bias"] = bias
            if scale is not None:
                kw["scale"] = scale
            tr.op("act", lambda e: e.activation(out=out, in_=in_, func=func, **kw), reads, writes)

        def TT(out, in0, in1, op, reads, writes):
            tr.op("dve", lambda e: e.tensor_tensor(out=out, in0=in0, in1=in1, op=op), reads, writes)

        def PTT(out, in0, in1, op, reads, writes):
            tr.op("pool", lambda e: e.tensor_tensor(out=out, in0=in0, in1=in1, op=op), reads, writes)

        def TS(out, in0, s1, s2, op0, op1, reads, writes):
            if s2 is None:
                tr.op("dve", lambda e: e.tensor_scalar(out=out, in0=in0, scalar1=s1, scalar2=None, op0=op0), reads, writes)
            else:
                tr.op("dve", lambda e: e.tensor_scalar(out=out, in0=in0, scalar1=s1, scalar2=s2, op0=op0, op1=op1), reads, writes)

        def STT(out, in0, scalar, in1, op0, op1, reads, writes):
            tr.op("dve", lambda e: e.scalar_tensor_tensor(out=out, in0=in0, scalar=scalar, in1=in1, op0=op0, op1=op1), reads, writes)

        def CPY(eng, out, in_, reads, writes):
            if eng == "act":
                tr.op("act", lambda e: e.copy(out=out, in_=in_), reads, writes)
            else:
                tr.op("dve", lambda e: e.tensor_copy(out=out, in_=in_), reads, writes)

        def TRANSP(out, in_, ident, reads, writes):
            tr.op("pe", lambda e: e.transpose(out, in_, ident), reads, writes)

        cp_i = [0]

        def evac(out, in_, reads, writes):
            cp_i[0] += 1
            CPY("act" if cp_i[0] % 2 else "dve", out, in_, reads, writes)

        def s3(ap):
            return ap.rearrange("p (b i) -> p b i", b=16)

        def bc8(ap16):
            return ap16.unsqueeze(2).broadcast_to([128, 16, 8])

        tr.dma("sp", [(VEC.t[:, :], vec_d)], VEC.ds, writes=[VEC.res])
        tr.dma("sp", [(IDF.t[:, :], ident_d)], IDF.ds, writes=[IDF.res])
        tr.dma("pool", [(IDB.t[:, :], ident_d)], IDB.ds, writes=[IDB.res])
        tr.dma("pool", [(MASKS.t[:, :, :], masks_d)], MASKS.ds, writes=[MASKS.res])
        tr.op("dve", lambda e: e.memset(ONES.t[:, :], 1.0), [], [ONES.res])
        ACT(ESINK.t[:, :], VEC.t[:, SINK:SINK + 8], AF.Exp, [VEC.res], [ESINK.res])

        def f32view(b):
            return b.t[:, :, :].rearrange("p a b -> p (a b)").bitcast(F32)

        cb = ACTP.get()
        cv_ = f32view(cb)
        tr.dma("sp", [(cv_[0:17, :], cin)], cb.ds, writes=[cb.res])
        ACT(cv_[0:17, :], cv_[0:17, :], AF.Silu, [cb.res], [cb.res])
        bk, rk = bank()
        for j in range(8):
            TRANSP(bk[:, j * 17:(j + 1) * 17], cv_[0:17, j * 128:(j + 1) * 128], IDF.t[0:17, 0:17], [cb.res, IDF.res], [rk])
        CPY("dve", SC.t[:, :, :], bk[:, 0:136].rearrange("p (j s) -> p j s", j=8), [rk], [SC.res])
        cb = ACTP.get()
        cv_ = f32view(cb)
        tr.dma("sp", [(cv_[0:32, :], scin)], cb.ds, writes=[cb.res])
        bk, rk = bank()
        for j in range(8):
            TRANSP(bk[:, j * 32:(j + 1) * 32], cv_[0:32, j * 128:(j + 1) * 128], IDF.t[0:32, 0:32], [cb.res, IDF.res], [rk])
        CPY("dve", SCS.t[:, :, :], bk[:, 0:256].rearrange("p (j s) -> p j s", j=8), [rk], [SCS.res])
        rowtiles = [(0, 2, 0, 0), (2, 128, 0, 2)] + [(130 + 128 * k, 128, 1 + k // 4, 130 + 128 * k) for k in range(16)] + [(2178, 128, 5, 2178)]
        for (r0, nr, ti, c0) in rowtiles:
            cb = ACTP.get()
            cv_ = f32view(cb)
            tr.dma("sp", [(cv_[0:nr, :], xin[r0:r0 + nr, :])], cb.ds, writes=[cb.res])
            for half in range(2):
                bk, rk = bank()
                for jj in range(4):
                    j = half * 4 + jj
                    TRANSP(bk[:, jj * 128:jj * 128 + nr], cv_[0:nr, j * 128:(j + 1) * 128], IDF.t[0:nr, 0:nr], [cb.res, IDF.res], [rk])
                evac(X[:, half * 4:half * 4 + 4, c0:c0 + nr], bk[:, :].rearrange("p (j c) -> p j c", j=4)[:, :, 0:nr],
                     [rk], [XR[ti][half * 4 + jj] for jj in range(4)])
        dsk = [mk_sem(), 0]
        out_events.append(tr.dma("sp", [(kout_s[:, 0:120, :], kcache[:, 8:128, :]), (vout_s[:, 0:120, :], vcache[:, 8:128, :])], dsk))

        parts = []
        ring_n = [0]

        def modv(m, j):
            return MOD.t[:, m * 8 + j, :]

        def gate_acc(kind_src):
            if kind_src[0] == "G":
                return (lambda i: Gb.t[:, i, 0:1]), (lambda i: Gb.t[:, i, 1:17]), Gb.res
            m = kind_src[1]
            return (lambda i: MOD.t[:, m * 8 + i, 0:1]), (lambda i: MOD.t[:, m * 8 + i, 1:17]), MOD.res

        def resid(tile, i, bk, rk, gacc):
            ti, c0, n, kind = tile
            gp, gs, gres = gacc
            xs = X[:, i, c0:c0 + n]
            if kind != "sample":
                STT(xs, bk[:, :n], gp(i), xs, ALU.mult, ALU.add, [rk, gres, XR[ti][i]], [XR[ti][i]])
            else:
                tb_ = T32.get()
                TT(s3(tb_.t[:, 0:128]), s3(bk[:, 0:128]), bc8(gs(i)), ALU.mult, [rk, gres], [tb_.res])
                TT(xs, xs, tb_.t[:, 0:128], ALU.add, [tb_.res, XR[ti][i]], [XR[ti][i]])

        def prep_mod(l, sub, ffn):
            ng = VEC.t[:, NG0 + (l * 3 + sub) * 8: NG0 + (l * 3 + sub) * 8 + 8]
            sc = MOD.t[:, (3 * sub + 1) * 8:(3 * sub + 1) * 8 + 8, :]
            TS(Ab.t[:, :, :], sc, 1.0, None, ALU.add, None, [MOD.res], [Ab.res])
            TT(Ab.t[:, :, :], Ab.t[:, :, :], ng.unsqueeze(2).broadcast_to([128, 8, 17]), ALU.mult, [Ab.res, VEC.res], [Ab.res])
            if ffn:
                gt = MOD.t[:, (3 * sub + 2) * 8:(3 * sub + 2) * 8 + 8, :]
                TS(Gb.t[:, :, :], gt, 0.5, None, ALU.mult, None, [MOD.res], [Gb.res])

        def prep_mod_kv():
            ng = VEC.t[:, KVG:KVG + 8]
            TS(Ab.t[:, :, :], MODKV.t[:, 8:16, :], 1.0, None, ALU.add, None, [MODKV.res], [Ab.res])
            TT(Ab.t[:, :, :], Ab.t[:, :, :], ng.unsqueeze(2).broadcast_to([128, 8, 17]), ALU.mult, [Ab.res, VEC.res], [Ab.res])

        def norm_items(tiles, bview, bres, pre=None, final=False, tmp_p2=False):
            items = []
            first = [True]
            for tile in tiles:
                ti, c0, n, kind = tile
                st = {}

                def Afn(tile=tile, st=st):
                    ti, c0, n, kind = tile
                    if first[0] and pre is not None:
                        pre()
                    first[0] = False
                    bk, rk = nbank()
                    for j in range(8):
                        sq = TB.get()
                        ACT(sq.t[:, :n], X[:, j, c0:c0 + n], AF.Square, [XR[ti][j]], [sq.res])
                        tr.mm([(bk[:, :n], ONES.t[:, :], sq.t[:, :n], j == 0, j == 7)], [sq.res, ONES.res], [rk])
                    st["bk"] = (bk, rk)

                def Bfn(tile=tile, st=st):
                    ti, c0, n, kind = tile
                    bk, rk = st["bk"]
                    rs = RS.get()
                    ACT(rs.t[:, :n], bk[:, :n], AF.Ln, [rk], [rs.res], bias=EPS, scale=1.0 / D)
                    ACT(rs.t[:, :n], rs.t[:, :n], AF.Exp, [rs.res], [rs.res], scale=-0.5)
                    st["rs"] = rs
                    if final:
                        return
                    for j in range(8):
                        xs = X[:, j, c0:c0 + n]
                        if tmp_p2:
                            pb_ = P2.get()
                            tmp = Buf.__new__(Buf)
                            tmp.t = pb_.t[:, :, :].rearrange("p a b -> p (a b)").bitcast(F32)
                            tmp.res = pb_.res
                        else:
                            tmp = T32.get()
                        if kind != "sample":
                            STT(tmp.t[:, :n], xs, Ab.t[:, j, 0:1], rs.t[:, :n], ALU.mult, ALU.mult, [XR[ti][j], Ab.res, rs.res], [tmp.res])
                            ACT(Hb[:, j, c0:c0 + n], tmp.t[:, :n], AF.Identity, [tmp.res, bres], [HR[ti][j]], bias=bview(j)[:, 0:1])
                        else:
                            TT(tmp.t[:, :n], xs, rs.t[:, :n], ALU.mult, [XR[ti][j], rs.res], [tmp.res])
                            TT(s3(tmp.t[:, :n]), s3(tmp.t[:, :n]), bc8(Ab.t[:, j, 1:17]), ALU.mult, [tmp.res, Ab.res], [tmp.res])
                            TT(s3(Hb[:, j, c0:c0 + n]), s3(tmp.t[:, :n]), bc8(bview(j)[:, 1:17]), ALU.add, [tmp.res, bres], [HR[ti][j]])

                items.append((Afn, Bfn, st))
            return items

        def add_plain(items, zip_prev=False):
            parts.append(dict(dma=None, zip_prev=zip_prev, items=lambda slot: [(a, b) for (a, b, *_r) in items]))

        def ada_part(src, c0, ncol, dstbuf, ch0, bcol0):
            nch = ncol // 128

            def dma(slot):
                v = slot.t[:, 0:8 * ncol].rearrange("p (k f) -> p k f", k=8)
                return [(v, src.rearrange("(k p) f -> p k f", p=128)[:, :, c0:c0 + ncol])]

            def items(slot):
                W = slot.t[:, 0:8 * ncol].rearrange("p (k f) -> p k f", k=8)

                def Afn():
                    bk, rk = bank()
                    for oc in range(nch):
                        tr.mm([(bk[:, oc * 17:(oc + 1) * 17], W[:, kk, oc * 128:(oc + 1) * 128], SC.t[:, kk, :], kk == 0, kk == 7) for kk in range(8)],
                              [slot.res, SC.res], [rk])
                    TT(dstbuf.t[:, ch0:ch0 + nch, :], bk[:, 0:nch * 17].rearrange("p (c s) -> p c s", c=nch),
                       VEC.t[:, bcol0:bcol0 + nch].unsqueeze(2).broadcast_to([128, nch, 17]), ALU.add, [rk, VEC.res], [dstbuf.res])

                return [(Afn, None)]

            parts.append(dict(dma=dma, items=items))

        def ada_micros(src, ncols, dstbuf, ch0, bcol0):
            return [(src, c * 128, dstbuf, ch0 + c, bcol0 + c) for c in range(ncols // 128)]

        class Side:
            def __init__(self, micros, per_item):
                self.todo = list(micros)
                self.pending = []
                self.k = per_item

            def _issue(self):
                (src, col0, dstbuf, ch, bcol) = self.todo.pop(0)
                pb_ = P2.get()
                v = pb_.t[:, :, :].rearrange("p a b -> p (a b)").rearrange("p (k f) -> p k f", k=8)
                tr.dma("pool", [(v, src.rearrange("(k p) f -> p k f", p=128)[:, :, col0:col0 + 128])], pb_.ds_pool, writes=[pb_.res])
                self.pending.append((pb_, v, dstbuf, ch, bcol))

            def _compute(self):
                (pb_, v, dstbuf, ch, bcol) = self.pending.pop(0)
                bk, rk = bank()
                tr.mm([(bk[:, 0:17], v[:, kk, :], SC.t[:, kk, :], kk == 0, kk == 7) for kk in range(8)], [pb_.res, SC.res], [rk])
                TT(dstbuf.t[:, ch, :], bk[:, 0:17], VEC.t[:, bcol:bcol + 1].broadcast_to([128, 17]), ALU.add, [rk, VEC.res], [dstbuf.res])

            def step(self):
                for _ in range(self.k):
                    if self.pending:
                        self._compute()
                for _ in range(self.k):
                    if self.todo:
                        self._issue()

            def flush(self):
                while self.pending or self.todo:
                    while self.pending:
                        self._compute()
                    for _ in range(3):
                        if self.todo:
                            self._issue()

        def ffn_parts(l, w, tiles, side=None):
            f0 = 0
            while f0 < 22:
                F = min(4, 22 - f0)

                def dma(slot, f0=f0, F=F):
                    c0 = f0 * 128
                    nc_ = F * 128
                    g = slot.t[:, 0:8 * nc_].rearrange("p (k f) -> p k f", k=8)
                    u = slot.t[:, 8 * nc_:16 * nc_].rearrange("p (k f) -> p k f", k=8)
                    dd = slot.t[:, 16 * nc_:16 * nc_ + F * D].rearrange("p (f d) -> p f d", f=F)
                    return [(g, wg_d[l, w].rearrange("(k p) f -> p k f", p=128)[:, :, c0:c0 + nc_]),
                            (u, wu_d[l, w].rearrange("(k p) f -> p k f", p=128)[:, :, c0:c0 + nc_]),
                            (dd, wd_d[l, w][c0:c0 + nc_, :].rearrange("(f p) d -> p f d", p=128))]

                def items(slot, F=F):
                    nc_ = F * 128
                    Wg = slot.t[:, 0:8 * nc_].rearrange("p (k f) -> p k f", k=8)
                    Wu = slot.t[:, 8 * nc_:16 * nc_].rearrange("p (k f) -> p k f", k=8)
                    Wd = slot.t[:, 16 * nc_:16 * nc_ + F * D].rearrange("p (f d) -> p f d", f=F)
                    gacc = gate_acc(("G",))
                    its = []
                    for tile in tiles:
                        st = {}

                        def Afn(tile=tile, st=st):
                            ti, c0, n, kind = tile
                            if side is not None:
                                side.step()
                            ab = ACTP.get()
                            st["ab"] = ab
                            for f in range(F):
                                bg, rg = bank()
                                bu, ru = bank()
                                tr.mm([(bg[:, :n], Wg[:, kk, f * 128:(f + 1) * 128], Hb[:, kk, c0:c0 + n], kk == 0, kk == 7) for kk in range(8)],
                                      [slot.res] + HR[ti], [rg])
                                tr.mm([(bu[:, :n], Wu[:, kk, f * 128:(f + 1) * 128], Hb[:, kk, c0:c0 + n], kk == 0, kk == 7) for kk in range(8)],
                                      [slot.res] + HR[ti], [ru])
                                sg = T32.get()
                                ACT(sg.t[:, :n], bg[:, :n], AF.Silu, [rg], [sg.res])
                                TT(ab.t[:, f, :n], sg.t[:, :n], bu[:, :n], ALU.mult, [sg.res, ru], [ab.res])

                        def Bfn(tile=tile, st=st):
                            ti, c0, n, kind = tile
                            ab = st["ab"]
                            for i in range(8):
                                bd, rd = bank()
                                tr.mm([(bd[:, :n], Wd[:, f, i * 128:(i + 1) * 128], ab.t[:, f, :n], f == 0, f == F - 1) for f in range(F)],
                                      [slot.res, ab.res], [rd])
                                resid(tile, i, bd, rd, gacc)

                        its.append((Afn, Bfn))
                    return its

                parts.append(dict(dma=dma, items=items))
                f0 += F

        def conv_parts(tiles):
            for p in range(4):
                def dma(slot, p=p):
                    prs = []
                    for q in range(3):
                        v = slot.t[:, q * 2048:(q + 1) * 2048].rearrange("p (k f) -> p k f", k=8)
                        prs.append((v, cwin.rearrange("(k p) f -> p k f", p=128)[:, :, q * D + p * 256:q * D + p * 256 + 256]))
                    v = slot.t[:, 6144:8192].rearrange("p (c d) -> p c d", c=2)
                    prs.append((v, cwout[p * 256:(p + 1) * 256, :].rearrange("(c p) d -> p c d", p=128)))
                    return prs

                def items(slot, p=p):
                    Wb = slot.t[:, 0:2048].rearrange("p (k f) -> p k f", k=8)
                    Wc = slot.t[:, 2048:4096].rearrange("p (k f) -> p k f", k=8)
                    Wv_ = slot.t[:, 4096:6144].rearrange("p (k f) -> p k f", k=8)
                    Wo_ = slot.t[:, 6144:8192].rearrange("p (c d) -> p c d", c=2)
                    gacc = gate_acc(("M", 5))
                    its = []
                    for tile in tiles:
                        st = {}

                        def Afn(tile=tile, st=st):
                            ti, c0, n, kind = tile
                            zb = P2.get()
                            st["zb"] = zb
                            if kind == "halo":
                                tr.op("dve", lambda e: e.memset(CARRY.t[:, :, :], 0.0), [], [CARRY.res])
                            for jl in range(2):
                                j = 2 * p + jl
                                bb, rb = bank()
                                bc, rc = bank()
                                bv, rv = bank()
                                for (bkk, rkk, W) in ((bb, rb, Wb), (bc, rc, Wc), (bv, rv, Wv_)):
                                    tr.mm([(bkk[:, :n], W[:, kk, jl * 128:(jl + 1) * 128], Hb[:, kk, c0:c0 + n], kk == 0, kk == 7) for kk in range(8)],
                                          [slot.res] + HR[ti], [rkk])
                                csb = T32.get()
                                ACT(csb.t[:, :n], bc[:, :n], AF.Identity, [rc], [csb.res])
                                U = T32.get()
                                cvb = T32.get()
                                w0 = VEC.t[:, CW + j:CW + j + 1]
                                w1 = VEC.t[:, CW + 8 + j:CW + 8 + j + 1]
                                w2 = VEC.t[:, CW + 16 + j:CW + 16 + j + 1]
                                if kind != "sample":
                                    CPY("dve", U.t[:, 0:2], CARRY.t[:, jl, :], [CARRY.res], [U.res])
                                    TT(U.t[:, 2:2 + n], csb.t[:, :n], bv[:, :n], ALU.mult, [csb.res, rv], [U.res])
                                    if kind == "halo":
                                        TS(U.t[:, 2:2 + n], U.t[:, 2:2 + n], VEC.t[:, FLAG:FLAG + 1], None, ALU.mult, None, [U.res, VEC.res], [U.res])
                                    CPY("dve", CARRY.t[:, jl, :], U.t[:, n:n + 2], [U.res], [CARRY.res])
                                    if ti == 4:
                                        CPY("dve", CSOP.t[:, j, :], U.t[:, n:n + 2], [U.res], [CSOP.res])
                                    TS(cvb.t[:, :n], U.t[:, 0:n], w0, None, ALU.mult, None, [U.res, VEC.res], [cvb.res])
                                    STT(cvb.t[:, :n], U.t[:, 1:n + 1], w1, cvb.t[:, :n], ALU.mult, ALU.add, [U.res, cvb.res], [cvb.res])
                                    STT(cvb.t[:, :n], U.t[:, 2:n + 2], w2, cvb.t[:, :n], ALU.mult, ALU.add, [U.res, cvb.res], [cvb.res])
                                    TT(zb.t[:, jl, :n], bb[:, :n], cvb.t[:, :n], ALU.mult, [rb, cvb.res], [zb.res])
                                else:
                                    U3 = U.t[:, 0:160].rearrange("p (b i) -> p b i", b=16)
                                    CPY("dve", U3[:, :, 0:2], SCS.t[:, j, :].rearrange("p (b i) -> p b i", b=16), [SCS.res], [U.res])
                                    TT(U3[:, :, 2:10], s3(csb.t[:, :n]), s3(bv[:, :n]), ALU.mult, [csb.res, rv], [U.res])
                                    CPY("dve", CSOS.t[:, j, :].rearrange("p (b i) -> p b i", b=16), U3[:, :, 8:10], [U.res], [CSOS.res])
                                    c3 = s3(cvb.t[:, :n])
                                    TS(c3, U3[:, :, 0:8], w0, None, ALU.mult, None, [U.res, VEC.res], [cvb.res])
                                    STT(c3, U3[:, :, 1:9], w1, c3, ALU.mult, ALU.add, [U.res, cvb.res], [cvb.res])
                                    STT(c3, U3[:, :, 2:10], w2, c3, ALU.mult, ALU.add, [U.res, cvb.res], [cvb.res])
                                    TT(zb.t[:, jl, :n], bb[:, :n], cvb.t[:, :n], ALU.mult, [rb, cvb.res], [zb.res])

                        def Bfn(tile=tile, st=st):
                            ti, c0, n, kind = tile
                            zb = st["zb"]
                            for i in range(8):
                                bd, rd = bank()
                                tr.mm([(bd[:, :n], Wo_[:, c, i * 128:(i + 1) * 128], zb.t[:, c, :n], c == 0, c == 1) for c in range(2)],
                                      [slot.res, zb.res], [rd])
                                resid(tile, i, bd, rd, gacc)

                        its.append((Afn, Bfn))
                    return its

                parts.append(dict(dma=dma, items=items))

        KVT = [(0, 2, 128, "halo")] + TILES1

        def kv_part():
            def dma(slot):
                prs = []
                for (off, src) in ((0, wk_d), (4096, wksw_d)):
                    v = slot.t[:, off:off + 4096].rearrange("p (k g e d) -> p k g e d", k=8, g=4, e=2)
                    s = src.rearrange("(k p) (g d) -> p k g d", p=128, g=4)
                    for e in range(2):
                        for g_ in range(4):
                            prs.append((v[:, :, g_, e, :], s[:, :, g_, :]))
                v = slot.t[:, 8192:8192 + 2048].rearrange("p (k d) -> p k d", k=8)
                prs.append((v, wv_d.rearrange("(k p) d -> p k d", p=128)))
                return prs

            def items(slot):
                Wk = slot.t[:, 0:4096].rearrange("p (k g m) -> p k g m", k=8, g=4)
                Wks = slot.t[:, 4096:8192].rearrange("p (k g m) -> p k g m", k=8, g=4)
                Wv_ = slot.t[:, 8192:8192 + 2048].rearrange("p (k d) -> p k d", k=8)
                its = []
                for tile in KVT:
                    def Afn(tile=tile):
                        ti, c0, n, kind = tile
                        rp = ACTP.get()
                        rv_ = f32view(rp).rearrange("p (a b) -> p a b", a=2)
                        tr.dma("sp", [(rv_[:, 0, 0:n], ropec[:, c0 - 2:c0 - 2 + n]), (rv_[:, 1, 0:n], ropes[:, c0 - 2:c0 - 2 + n])], rp.ds, writes=[rp.res])
                        is_out = (ti == 4) or (ti == 5)
                        for g in range(4):
                            bk_, rk_ = bank()
                            bs_, rs_ = bank()
                            tr.mm([(bk_[:, :n], Wk[:, kk, g, :], Hb[:, kk, c0:c0 + n], kk == 0, kk == 7) for kk in range(8)], [slot.res] + HR[ti], [rk_])
                            tr.mm([(bs_[:, :n], Wks[:, kk, g, :], Hb[:, kk, c0:c0 + n], kk == 0, kk == 7) for kk in range(8)], [slot.res] + HR[ti], [rs_])
                            t1 = T32.get()
                            t2 = T32.get()
                            TT(t1.t[:, :n], bk_[:, :n], rv_[:, 0, 0:n], ALU.mult, [rk_, rp.res], [t1.res])
                            TT(t2.t[:, :n], bs_[:, :n], rv_[:, 1, 0:n], ALU.mult, [rs_, rp.res], [t2.res])
                            TT(t1.t[:, :n], t1.t[:, :n], t2.t[:, :n], ALU.add, [t1.res, t2.res], [t1.res])
                            kb = TB.get()
                            ACT(kb.t[:, :n], t1.t[:, :n], AF.Identity, [t1.res], [kb.res])
                            if "a" not in KDBG:
                                tr.dma("sp", [(kt_scr[g, :, c0 - 2:c0 - 2 + n], kb.t[:, :n])], kb.ds, reads=[kb.res], writes=[KTSCR[g][ti]])
                            if is_out and "b" not in KDBG:
                                bt_, rt_ = bank()
                                TRANSP(bt_[:, 0:128], t1.t[:, n - 128:n], IDF.t[:, :], [t1.res, IDF.res], [rt_])
                                if ti not in st_k:
                                    st_k[ti] = KVO.get()
                                ko = st_k[ti]
                                evac(ko.t[:, g * 64:(g + 1) * 64], bt_[:, 0:64], [rt_], [ko.res])
                        if is_out and "b" not in KDBG:
                            ko = st_k[ti]
                            if ti == 4:
                                out_events.append(tr.dma("sp", [(kout_p, ko.t[:, :])], ko.ds, reads=[ko.res]))
                            else:
                                out_events.append(tr.dma("sp", [(kout_s[b, 120:128, :], ko.t[b * 8:(b + 1) * 8, :]) for b in range(16)], ko.ds, reads=[ko.res]))

                    def Bfn(tile=tile):
                        ti, c0, n, kind = tile
                        if "c" in KDBG:
                            return
                        nb = n // 128
                        vb = P2.get()
                        vv = vb.t[:, :, :].rearrange("p a b -> p (a b)").rearrange("p (b d) -> p b d", b=4)
                        is_out = (ti == 4) or (ti == 5)
                        for b_ in range(nb):
                            bk_, rk_ = bank()
                            tr.mm([(bk_[:, 0:256], Hb[:, kk, c0 + b_ * 128:c0 + (b_ + 1) * 128], Wv_[:, kk, :], kk == 0, kk == 7) for kk in range(8)],
                                  [slot.res] + HR[ti], [rk_])
                            if not (is_out and b_ == nb - 1):
                                evac(vv[:, b_, :], bk_[:, 0:256], [rk_], [vb.res])
                            else:
                                vo = KVO.get()
                                CPY("dve", vo.t[:, :], bk_[:, 0:256], [rk_], [vo.res])
                                CPY("act", vv[:, b_, :], vo.t[:, :], [vo.res], [vb.res])
                                if ti == 4:
                                    out_events.append(tr.dma("sp", [(vout_p, vo.t[:, :])], vo.ds, reads=[vo.res]))
                                else:
                                    out_events.append(tr.dma("sp", [(vout_s[b, 120:128, :], vo.t[b * 8:(b + 1) * 8, :]) for b in range(16)], vo.ds, reads=[vo.res]))
                        blk0 = (c0 - 2) // 128
                        if "d" not in KDBG:
                            tr.dma("sp", [(v_scr[blk0:blk0 + nb, :, :].rearrange("b p d -> p b d"), vv[:, 0:nb, :])], vb.ds, reads=[vb.res], writes=[VSCR[ti]])

                    its.append((Afn, Bfn))
                return its

            parts.append(dict(dma=dma, items=items))

        st_k = {}

        def attn_parts(tiles):
            for g in range(4):
                def dma(slot, g=g):
                    prs = []
                    prs.append((slot.t[:, 0:2048].rearrange("p (k f) -> p k f", k=8), wq_d.rearrange("(k p) f -> p k f", p=128)[:, :, g * 256:(g + 1) * 256]))
                    prs.append((slot.t[:, 2048:4096].rearrange("p (k f) -> p k f", k=8), wqsw_d.rearrange("(k p) f -> p k f", p=128)[:, :, g * 256:(g + 1) * 256]))
                    prs.append((slot.t[:, 4096:6144].rearrange("p (c d) -> p c d", c=2), wo_d[g * 256:(g + 1) * 256, :].rearrange("(c p) d -> p c d", p=128)))
                    prs.append((slot.t[:, 6144:6144 + TKV], kt_scr[g, :, :]))
                    prs.append((slot.t[:, 8448:8448 + 1152].rearrange("p (b d) -> p b d", b=18), v_scr[:, :, g * 64:(g + 1) * 64].rearrange("b p d -> p b d")))
                    prs.append((slot.t[:, 9600:9600 + 1024].rearrange("p (b d) -> p b d", b=16), kcache[:, :, g * 64:(g + 1) * 64].rearrange("b p d -> p b d")))
                    prs.append((slot.t[:, 10624:10624 + 1024].rearrange("p (b d) -> p b d", b=16), vcache[:, :, g * 64:(g + 1) * 64].rearrange("b p d -> p b d")))
                    return prs

                def items(slot, g=g):
                    Wq = slot.t[:, 0:2048].rearrange("p (k f) -> p k f", k=8)
                    Wqs = slot.t[:, 2048:4096].rearrange("p (k f) -> p k f", k=8)
                    Wo_ = slot.t[:, 4096:6144].rearrange("p (c d) -> p c d", c=2)
                    KT = slot.t[:, 6144:6144 + TKV]
                    Vg = slot.t[:, 8448:8448 + 1152].rearrange("p (b d) -> p b d", b=18)
                    kc = slot.t[:, 9600:9600 + 1024].rearrange("p (b d) -> p b d", b=16)
                    vc = slot.t[:, 10624:10624 + 1024].rearrange("p (b d) -> p b d", b=16)
                    gacc = gate_acc(("M", 5))
                    esk = ESINK.t[:, g * 2:(g + 1) * 2].unsqueeze(2).broadcast_to([128, 2, 128])
                    its = []

                    def Pfn():
                        for q4 in range(4):
                            bk_, rk_ = bank()
                            for bl in range(4):
                                b = q4 * 4 + bl
                                for e in range(2):
                                    tr.mm([(bk_[e * 64:(e + 1) * 64, bl * 128:(bl + 1) * 128], kc[:, b, :], IDB.t[:, :], True, True)], [slot.res, IDB.res], [rk_])
                            evac(KTC.t[:, q4 * 4:(q4 + 1) * 4, :], bk_[:, :].rearrange("p (b k) -> p b k", b=4), [rk_], [KTC.res])

                    for tile in tiles:
                        st = {}

                        def Afn(tile=tile, st=st):
                            ti, c0, n, kind = tile
                            if kind == "sample":
                                Pfn()
                            rp = ACTP.get()
                            rv_ = f32view(rp).rearrange("p (a b) -> p a b", a=2)
                            tr.dma("sp", [(rv_[:, 0, 0:n], ropec[:, c0 - 2:c0 - 2 + n]), (rv_[:, 1, 0:n], ropes[:, c0 - 2:c0 - 2 + n])], rp.ds, writes=[rp.res])
                            qt = P2.get()
                            st["qt"] = qt
                            for c in range(2):
                                bq, rq = bank()
                                bs_, rs_ = bank()
                                tr.mm([(bq[:, :n], Wq[:, kk, c * 128:(c + 1) * 128], Hb[:, kk, c0:c0 + n], kk == 0, kk == 7) for kk in range(8)], [slot.res] + HR[ti], [rq])
                                tr.mm([(bs_[:, :n], Wqs[:, kk, c * 128:(c + 1) * 128], Hb[:, kk, c0:c0 + n], kk == 0, kk == 7) for kk in range(8)], [slot.res] + HR[ti], [rs_])
                                t1 = T32.get()
                                t2 = T32.get()
                                TT(t1.t[:, :n], bq[:, :n], rv_[:, 0, 0:n], ALU.mult, [rq, rp.res], [t1.res])
                                TT(t2.t[:, :n], bs_[:, :n], rv_[:, 1, 0:n], ALU.mult, [rs_, rp.res], [t2.res])
                                PTT(qt.t[:, c, :n], t1.t[:, :n], t2.t[:, :n], ALU.add, [t1.res, t2.res], [qt.res])

                        def Bfn(tile=tile, st=st):
                            ti, c0, n, kind = tile
                            if "q" in KDBG or ("s" in KDBG and kind == "sample") or ("m" in KDBG and kind != "sample"):
                                return
                            qt = st["qt"]
                            ot = P2.get()
                            bst = {}

                            def stage1(blk):
                                q0 = blk * 128
                                kown = c0 - 2 + q0
                                be = [bank(), bank()]
                                pe_ = [TB.get(), TB.get()]
                                mlist = []
                                if kind != "sample":
                                    for (c_lo, k_lo) in ((0, kown - 128), (256, kown)):
                                        for e in range(2):
                                            es_ = slice(e * 64, (e + 1) * 64)
                                            mlist.append((be[e][0][:, c_lo:c_lo + 256].rearrange("p (c q) -> p c q", c=2), KT[es_, k_lo:k_lo + 128], qt.t[es_, :, q0:q0 + 128], True, True))
                                else:
                                    for b in range(16):
                                        for c_ in range(2):
                                            for e in range(2):
                                                es_ = slice(e * 64, (e + 1) * 64)
                                                o = be[e][0][:, c_ * 128 + b * 8:c_ * 128 + b * 8 + 8]
                                                mlist.append((o, KTC.t[es_, b, :], qt.t[es_, c_, b * 8:(b + 1) * 8], True, True))
                                    for e in range(2):
                                        es_ = slice(e * 64, (e + 1) * 64)
                                        mlist.append((be[e][0][:, 256:512].rearrange("p (c q) -> p c q", c=2), KT[es_, kown:kown + 128], qt.t[es_, :, q0:q0 + 128], True, True))
                                tr.mm(mlist, [slot.res, KTC.res, qt.res], [be[0][1], be[1][1]])
                                if kind != "sample":
                                    mi = 2 if (ti == 1 and blk == 0) else 0
                                else:
                                    mi = 4
                                msk = MASKS.t[:, mi:mi + 2, :].unsqueeze(2).broadcast_to([128, 2, 2, 128])
                                for e in range(2):
                                    bke, rke = be[e]
                                    ACT(pe_[e].t[:, :], bke[:, :], AF.Exp, [rke], [pe_[e].res], scale=0.125)
                                    p4 = pe_[e].t[:, :].rearrange("p (t c q) -> p t c q", t=2, c=2)
                                    PTT(p4, p4, msk, ALU.mult, [pe_[e].res, MASKS.res], [pe_[e].res])
                                bst[blk] = pe_

                            def stage2(blk):
                                q0 = blk * 128
                                kown = c0 - 2 + q0
                                vown = kown // 128
                                pe_ = bst.pop(blk)
                                bd_, rd_ = bank()
                                tr.mm([(bd_[e * 64:(e + 1) * 64, 0:256], ONES.t[:, 0:64], pe_[e].t[:, t_ * 256:(t_ + 1) * 256], t_ == 0, t_ == 1) for t_ in range(2) for e in range(2)],
                                      [pe_[0].res, pe_[1].res, ONES.res], [rd_])
                                rden = RS.get()
                                r3 = rden.t[:, 0:256].rearrange("p (h q) -> p h q", h=2)
                                TT(r3, bd_[:, 0:256].rearrange("p (h q) -> p h q", h=2), esk, ALU.add, [rd_, ESINK.res], [rden.res])
                                tr.op("dve", lambda e, rden=rden: e.reciprocal(out=rden.t[:, 0:256], in_=rden.t[:, 0:256]), [rden.res], [rden.res])
                                bo, ro = bank()
                                mlist = []
                                if kind != "sample":
                                    for (vb_, lo, st_, sp_) in ((vown - 1, 0, True, False), (vown, 256, False, True)):
                                        for e in range(2):
                                            es_ = slice(e * 64, (e + 1) * 64)
                                            mlist.append((bo[es_, 0:256], Vg[:, vb_, :], pe_[e].t[:, lo:lo + 256], st_, sp_))
                                else:
                                    for e in range(2):
                                        es_ = slice(e * 64, (e + 1) * 64)
                                        mlist.append((bo[es_, 0:256], Vg[:, vown, :], pe_[e].t[:, 256:512], True, False))
                                    for b in range(16):
                                        for c_ in range(2):
                                            for e in range(2):
                                                es_ = slice(e * 64, (e + 1) * 64)
                                                o = bo[es_, c_ * 128 + b * 8:c_ * 128 + b * 8 + 8]
                                                r_ = pe_[e].t[:, c_ * 128 + b * 8:c_ * 128 + b * 8 + 8]
                                                mlist.append((o, vc[:, b, :], r_, False, (b == 15 and c_ == 1)))
                                tr.mm(mlist, [slot.res, pe_[0].res, pe_[1].res], [ro])
                                TT(ot.t[:, :, q0:q0 + 128], bo[:, 0:256].rearrange("p (c q) -> p c q", c=2),
                                   rden.t[:, 0:256].rearrange("p (c q) -> p c q", c=2), ALU.mult, [ro, rden.res], [ot.res])

                            nblk = n // 128
                            for blk in range(nblk):
                                stage1(blk)
                                if blk >= 1:
                                    stage2(blk - 1)
                            stage2(nblk - 1)
                            for i in range(8):
                                bd, rd = bank()
                                tr.mm([(bd[:, :n], Wo_[:, c, i * 128:(i + 1) * 128], ot.t[:, c, :n], c == 0, c == 1) for c in range(2)], [slot.res, ot.res], [rd])
                                resid(tile, i, bd, rd, gacc)

                        its.append((Afn, Bfn))
                    return its

                parts.append(dict(dma=dma, items=items, extra=[KTSCR[g][t_] for t_ in range(6)] + VSCR))

        def final_items(tiles):
            base = norm_items(tiles, None, None, final=True)
            its = []
            for (Afn, Bfn, st), tile in zip(base, tiles):
                def B2(tile=tile, st=st, Bfn=Bfn):
                    ti, c0, n, kind = tile
                    Bfn()
                    rs = st["rs"]
                    for blk in range(n // 128):
                        q0 = blk * 128
                        sg_ = ACTP.get()
                        sv_ = f32view(sg_)
                        for half in range(2):
                            bk_, rk_ = bank()
                            for jj in range(4):
                                j = half * 4 + jj
                                yt = T32.get()
                                STT(yt.t[:, 0:128], X[:, j, c0 + q0:c0 + q0 + 128], VEC.t[:, FNG + j:FNG + j + 1], rs.t[:, q0:q0 + 128], ALU.mult, ALU.mult,
                                    [XR[ti][j], VEC.res, rs.res], [yt.res])
                                TRANSP(bk_[:, jj * 128:(jj + 1) * 128], yt.t[:, 0:128], IDF.t[:, :], [yt.res, IDF.res], [rk_])
                            evac(sv_[:, half * 512:(half + 1) * 512], bk_[:, :], [rk_], [sg_.res])
                        if kind == "sample":
                            dst = y_s
                        else:
                            r0 = c0 - 130 + q0
                            dst = y_p[r0:r0 + 128, :]
                        out_events.append(tr.dma("sp", [(dst, sv_[:, :])], sg_.ds, reads=[sg_.res]))

                its.append((Afn, B2))
            parts.append(dict(dma=None, items=lambda slot: its))

        def cso_out():
            def Afn():
                sg_ = ACTP.get()
                sv_ = f32view(sg_)
                for half in range(2):
                    bk_, rk_ = bank()
                    for jj in range(4):
                        TRANSP(bk_[0:32, jj * 128:(jj + 1) * 128], CSOS.t[:, half * 4 + jj, :], IDF.t[:, :], [CSOS.res, IDF.res], [rk_])
                    evac(sv_[0:32, half * 512:(half + 1) * 512], bk_[0:32, :], [rk_], [sg_.res])
                out_events.append(tr.dma("sp", [(cso_s, sv_[0:32, :])], sg_.ds, reads=[sg_.res]))
                sg2 = ACTP.get()
                sv2 = f32view(sg2)
                for half in range(2):
                    bk3, rk3 = bank()
                    for jj in range(4):
                        TRANSP(bk3[0:2, jj * 128:(jj + 1) * 128], CSOP.t[:, half * 4 + jj, :], IDF.t[:, :], [CSOP.res, IDF.res], [rk3])
                    evac(sv2[0:2, half * 512:(half + 1) * 512], bk3[0:2, :], [rk3], [sg2.res])
                out_events.append(tr.dma("sp", [(cso_p, sv2[0:2, :])], sg2.ds, reads=[sg2.res]))

            parts.append(dict(dma=None, items=lambda slot: [(Afn, None)]))

        side_l0 = Side(ada_micros(w_ada[0], 9 * D, MOD, 0, BADA)[24:], 2)
        side_l1 = Side(ada_micros(wadakv, 2 * D, MODKV, 0, BKV) + ada_micros(w_ada[1], 9 * D, MOD, 0, BADA + 72), 3)

        def flush_part(sd):
            parts.append(dict(dma=None, items=lambda slot: [(sd.flush, None)]))

        ada_part(w_ada[0], 0, 1536, MOD, 0, BADA + 0)
        ada_part(w_ada[0], 1536, 1536, MOD, 12, BADA + 12)
        add_plain(norm_items(TILES0, lambda j: modv(0, j), MOD.res, pre=lambda: prep_mod(0, 0, True)))
        ffn_parts(0, 0, TILES0, side=side_l0)
        flush_part(side_l0)
        add_plain(norm_items(TILES0, lambda j: modv(3, j), MOD.res, pre=lambda: prep_mod(0, 1, False)))
        conv_parts(TILES0)
        cso_out()
        add_plain(norm_items(TILES0, lambda j: modv(6, j), MOD.res, pre=lambda: prep_mod(0, 2, True)))
        ffn_parts(0, 1, TILES0, side=side_l1)
        flush_part(side_l1)
        add_plain(norm_items(TILES0, lambda j: MODKV.t[:, j, :], MODKV.res, pre=prep_mod_kv))
        kv_part()
        add_plain(norm_items(TILES1, lambda j: modv(0, j), MOD.res, pre=lambda: prep_mod(1, 0, True), tmp_p2=True), zip_prev=True)
        ffn_parts(1, 0, TILES1)
        add_plain(norm_items(TILES1, lambda j: modv(3, j), MOD.res, pre=lambda: prep_mod(1, 1, False)))
        attn_parts(TILES1)
        add_plain(norm_items(TILES1, lambda j: modv(6, j), MOD.res, pre=lambda: prep_mod(1, 2, True)))
        ffn_parts(1, 1, TILES1)
        final_items(TILES1)

        import os as _os
        _ks = _os.environ.get("KSTOP")
        if _ks is not None:
            parts = parts[:int(_ks)]
        dparts = [p for p in parts if p["dma"] is not None]
        for k, p in enumerate(dparts):
            p["slot"] = RING[k % 2]
            p["didx"] = k

        def issue(k):
            if k < len(dparts):
                p = dparts[k]
                slot = p["slot"]
                tr.dma("pool", p["dma"](slot), slot.ds, reads=p.get("extra", []), writes=[slot.res])

        issue(0)
        flat = []
        pi = 0
        while pi < len(parts):
            p = parts[pi]
            its = p["items"](p.get("slot"))
            post0 = (lambda k=p.get("didx", -1) + 1: issue(k)) if p["dma"] is not None else None
            L = [(it[0], it[1], post0 if n_ == 0 else None) for n_, it in enumerate(its)]
            if pi + 1 < len(parts) and parts[pi + 1].get("zip_prev") and len(L) >= 2:
                Nn = [(it[0], it[1], None) for it in parts[pi + 1]["items"](None)]
                m_, n_n = len(L), len(Nn)
                off = m_ - n_n
                merged = L[:off + 1]
                for j in range(n_n - 1):
                    merged.append(L[off + 1 + j])
                    merged.append(Nn[j])
                merged.append(Nn[n_n - 1])
                flat.extend(merged)
                pi += 2
                continue
            flat.extend(L)
            pi += 1
        prevB = None
        for (a, b, post) in flat:
            if a:
                a()
            if prevB:
                prevB()
            prevB = b
            if post:
                post()
        if prevB:
            prevB()

        tr.wait_only("sp", out_events)

        block = es.enter_context(nc.Block())

        @block.tensor
        def _(e):
            for f in tr.q["pe"]:
                f(e)

        @block.scalar
        def _(e):
            for f in tr.q["act"]:
                f(e)

        @block.vector
        def _(e):
            for f in tr.q["dve"]:
                f(e)

        @block.gpsimd
        def _(e):
            for f in tr.q["pool"]:
                f(e)

        @block.sync
        def _(e):
            for f in tr.q["sp"]:
                f(e)
    return nc


_NC = [None]


def _rope_tables(pos):
    inv = (np.float32(500000.0) ** (-np.arange(0, 16, 2, dtype=np.float32) / np.float32(16))).astype(np.float32)
    ang = pos.astype(np.float32)[None, :] * inv[:, None]
    cos = np.cos(ang).astype(np.float32)
    sin = np.sin(ang).astype(np.float32)
    n = pos.shape[0]
    C = np.ones((64, n), np.float32)
    S = np.zeros((64, n), np.float32)
    C[0:8] = cos
    C[8:16] = cos
    S[0:8] = -sin
    S[8:16] = sin
    return C, S


def _swap_cols(w, nheads):
    perm = []
    for h in range(nheads):
        base = h * 64
        perm += [base + 8 + d for d in range(8)] + [base + d for d in range(8)] + [base + d for d in range(16, 64)]
    return np.ascontiguousarray(w[:, perm])


def kernel(x_prompt, x_sample, state_conv, cache_k_win, cache_v_win, c_prompt, c_sample,
           norm_g, w_ada, b_ada, w_ffn_gate, w_ffn_up, w_ffn_down,
           conv_w_in, conv_w, conv_w_out, kv_norm_g, w_ada_kv, b_ada_kv, w_k, w_v,
           attn_w_q, attn_sinks, attn_w_o, final_norm_g):
    f = lambda a: np.ascontiguousarray(np.asarray(a, dtype=np.float32))
    x_prompt, x_sample, state_conv = f(x_prompt), f(x_sample), f(state_conv)
    cache_k_win, cache_v_win, c_prompt, c_sample = f(cache_k_win), f(cache_v_win), f(c_prompt), f(c_sample)
    if _NC[0] is None:
        _NC[0] = build()
    nc = _NC[0]

    def fm(v):
        v = f(v)
        lead = v.shape[:-1]
        return np.moveaxis(v.reshape(lead + (8, 128)), -1, 0)

    vec_common = np.zeros((128, NV), np.float32)
    vec_common[:, NG0:NG0 + 48] = fm(norm_g).reshape(128, 48)
    vec_common[:, KVG:KVG + 8] = fm(kv_norm_g).reshape(128, 8)
    vec_common[:, FNG:FNG + 8] = fm(final_norm_g).reshape(128, 8)
    ba = f(b_ada).reshape(2, 72, 128)
    vec_common[:, BADA:BADA + 144] = np.moveaxis(ba, -1, 0).reshape(128, 144)
    vec_common[:, BKV:BKV + 16] = np.moveaxis(f(b_ada_kv).reshape(16, 128), -1, 0)
    vec_common[:, CW:CW + 24] = fm(f(conv_w)[0]).reshape(128, 24)
    sk = f(attn_sinks)[0]
    for e in range(2):
        sperm = [4 * g + 2 * c + e for g in range(4) for c in range(2)]
        vec_common[e * 64:(e + 1) * 64, SINK:SINK + 8] = np.broadcast_to(sk[sperm][None, :], (64, 8))

    ident = np.eye(128, dtype=np.float32)
    jj = np.arange(128)[:, None]
    ii = np.arange(128)[None, :]
    mA = (jj > ii).astype(np.float32)
    mB = (jj <= ii).astype(np.float32)
    mAs = (jj > (ii % 8)).astype(np.float32)
    mBs = (((jj // 8) == (ii // 8)) & ((jj % 8) <= (ii % 8))).astype(np.float32)

    shared = dict(
        ident=ident, w_ada=f(w_ada), wg=f(w_ffn_gate), wu=f(w_ffn_up), wd=f(w_ffn_down),
        cwin=f(conv_w_in)[0], cwout=f(conv_w_out)[0], wadakv=f(w_ada_kv),
        wk=f(w_k), wksw=_swap_cols(f(w_k), 4), wv=f(w_v),
        wq=f(attn_w_q)[0], wqsw=_swap_cols(f(attn_w_q)[0], 16), wo=f(attn_w_o)[0],
    )
    in_maps = []
    for c in range(NCORE):
        b, q = c // 4, c % 4
        s0 = q * 2048
        xin = np.zeros((T, D), np.float32)
        if q > 0:
            xin[0:130] = x_prompt[b, s0 - 130:s0]
        xin[130:2178] = x_prompt[b, s0:s0 + 2048]
        xin[2178:2306] = x_sample[16 * c:16 * c + 16].reshape(128, D)
        cin = np.concatenate([c_prompt[b:b + 1], c_sample[16 * c:16 * c + 16]], axis=0)
        scin = state_conv[0, 16 * c:16 * c + 16].reshape(32, D)
        pos = np.concatenate([np.arange(s0 - 128, s0 + 2048), np.tile(16384 + np.arange(8), 16)]).astype(np.int64)
        C, S = _rope_tables(pos)
        masks = np.stack([mA, mB, mA if q > 0 else np.zeros_like(mA), mB, mAs, mBs], axis=1)
        vec = vec_common.copy()
        vec[:, FLAG] = 1.0 if q > 0 else 0.0
        m = dict(shared)
        m.update(xin=xin, cin=np.ascontiguousarray(cin), scin=np.ascontiguousarray(scin),
                 kcache=np.ascontiguousarray(cache_k_win[16 * c:16 * c + 16].reshape(16, 128, 256)),
                 vcache=np.ascontiguousarray(cache_v_win[16 * c:16 * c + 16].reshape(16, 128, 256)),
                 ropec=np.ascontiguousarray(np.concatenate([C, C], axis=0)), ropes=np.ascontiguousarray(np.concatenate([S, S], axis=0)),
                 masks=np.ascontiguousarray(masks), vec=vec)
        in_maps.append(m)
    res = run_bass_kernel_spmd(nc, in_maps, core_ids=list(range(NCORE)))
    R = res.results
    y_prompt = np.stack([np.concatenate([R[4 * b + q]["y_p"] for q in range(4)], axis=0) for b in range(2)], axis=0)
    y_sample = np.concatenate([R[c]["y_s"].reshape(16, 8, D) for c in range(NCORE)], axis=0)
    conv_p = np.stack([R[3]["cso_p"], R[7]["cso_p"]], axis=0)[None]
    conv_s = np.concatenate([R[c]["cso_s"].reshape(16, 2, D) for c in range(NCORE)], axis=0)[None]
    k_p = np.stack([R[3]["kout_p"], R[7]["kout_p"]], axis=0).reshape(2, 128, 4, 64)
    v_p = np.stack([R[3]["vout_p"], R[7]["vout_p"]], axis=0).reshape(2, 128, 4, 64)
    k_s = np.concatenate([R[c]["kout_s"] for c in range(NCORE)], axis=0).reshape(128, 128, 4, 64)
    v_s = np.concatenate([R[c]["vout_s"] for c in range(NCORE)], axis=0).reshape(128, 128, 4, 64)
    outs = (y_prompt, y_sample, conv_p, conv_s, k_p, v_p, k_s, v_s)
    return tuple(np.ascontiguousarray(o.astype(np.float32)) for o in outs)
```

```python
import numpy as np
from contextlib import ExitStack
import concourse.bass as bass
import concourse.mybir as mybir
from concourse.bass_utils import run_bass_kernel_spmd

F32 = mybir.dt.float32
BF16 = mybir.dt.bfloat16
AF = mybir.ActivationFunctionType
ALU = mybir.AluOpType

D = 1024
DFF = 2816
NCORE = 8
T = 2306
TKV = 2304
SLOT = 12288
EPS = 1e-6
NG0, KVG, FNG, BADA, BKV, CW, SINK, FLAG, NV = 0, 48, 56, 64, 208, 224, 248, 264, 265


class Res:
    __slots__ = ("w", "r")

    def __init__(self):
        self.w = None
        self.r = {}


class Tracker:
    ENG = ("pe", "act", "dve", "pool", "sp")

    def __init__(self, sems):
        self.sems = sems
        self.q = {e: [] for e in self.ENG}
        self.cnt = {e: 0 for e in self.ENG}
        self.waited = {e: {} for e in self.ENG}

    def _waits(self, eng, reads, writes, strict=False):
        need = {}

        def add(ev, same_ok):
            if ev is None:
                return
            sem, val, src = ev
            if src == eng and not same_ok and not strict:
                return
            k = id(sem)
            if k not in need or need[k][1] < val:
                need[k] = (sem, val)

        for r in reads:
            add(r.w, True)
        for w in writes:
            add(w.w, False)
            for ev in w.r.values():
                add(ev, False)
        out = []
        for k, (sem, val) in need.items():
            if self.waited[eng].get(k, 0) >= val:
                continue
            self.waited[eng][k] = val
            out.append((sem, val))
        return out

    def op(self, eng, fn, reads=(), writes=()):
        waits = self._waits(eng, reads, writes, strict=(eng != "pe"))
        self.cnt[eng] += 1
        sem = self.sems[eng]
        ev = (sem, self.cnt[eng], eng)

        def run(e):
            for s, v in waits:
                e.wait_ge(s, v)
            fn(e).then_inc(sem, 1)

        self.q[eng].append(run)
        for r in reads:
            r.r[eng] = ev
        for w in writes:
            w.w = ev
            w.r = {}
        return ev

    def mm(self, mms, reads, writes):
        def fn(e):
            ins = None
            for (o, l, r, st, sp) in mms:
                ins = e.matmul(o, l, r, start=st, stop=sp)
            return ins

        return self.op("pe", fn, reads, writes)

    def dma(self, qeng, pairs, dsem, reads=(), writes=()):
        waits = self._waits(qeng, reads, writes, strict=True)
        dsem[1] += 16 * len(pairs)
        sem = dsem[0]
        ev = (sem, dsem[1], "dma")

        def run(e):
            for s, v in waits:
                e.wait_ge(s, v)
            for (o, i) in pairs:
                e.dma_start(out=o, in_=i).then_inc(sem, 16)

        self.q[qeng].append(run)
        for r in reads:
            r.r["dma" + str(id(sem))] = ev
        for w in writes:
            w.w = ev
            w.r = {}
        return ev

    def wait_only(self, eng, evs):
        ws = []
        for (sem, val, _) in evs:
            ws.append((sem, val))

        def run(e):
            for s, v in ws:
                e.wait_ge(s, v)

        self.q[eng].append(run)


class Buf:
    def __init__(self, t, mk_sem):
        self.t = t
        self.res = Res()
        self._mk = mk_sem
        self._ds = None

    @property
    def ds(self):
        if self._ds is None:
            self._ds = [self._mk(), 0]
        return self._ds

    @property
    def ds_pool(self):
        if getattr(self, "_ds2", None) is None:
            self._ds2 = [self._mk(), 0]
        return self._ds2


class Rot:
    def __init__(self, bufs):
        self.bufs = bufs
        self.i = 0

    def get(self):
        b = self.bufs[self.i % len(self.bufs)]
        self.i += 1
        return b


def build():
    nc = bass.Bass("TRN2", target_bir_lowering=False)
    es = ExitStack()
    with es:
        import os as _os0
        KDBG = _os0.environ.get("KDBG", "")
        nsem = [0]

        def mk_sem():
            nsem[0] += 1
            return es.enter_context(nc.semaphore("s%d" % nsem[0]))

        def din(name, shape):
            return nc.dram_tensor(name, shape, F32, kind="ExternalInput").ap()

        def dout(name, shape):
            return nc.dram_tensor(name, shape, F32, kind="ExternalOutput").ap()

        xin = din("xin", [T, D])
        cin = din("cin", [17, D])
        scin = din("scin", [32, D])
        kcache = din("kcache", [16, 128, 256])
        vcache = din("vcache", [16, 128, 256])
        ropec = din("ropec", [128, TKV])
        ropes = din("ropes", [128, TKV])
        masks_d = din("masks", [128, 6, 128])
        vec_d = din("vec", [128, NV])
        ident_d = din("ident", [128, 128])
        w_ada = din("w_ada", [2, D, 9 * D])
        wg_d = din("wg", [2, 2, D, DFF])
        wu_d = din("wu", [2, 2, D, DFF])
        wd_d = din("wd", [2, 2, DFF, D])
        cwin = din("cwin", [D, 3 * D])
        cwout = din("cwout", [D, D])
        wadakv = din("wadakv", [D, 2 * D])
        wk_d = din("wk", [D, 256])
        wksw_d = din("wksw", [D, 256])
        wv_d = din("wv", [D, 256])
        wq_d = din("wq", [D, D])
        wqsw_d = din("wqsw", [D, D])
        wo_d = din("wo", [D, D])

        y_p = dout("y_p", [2048, D])
        y_s = dout("y_s", [128, D])
        cso_p = dout("cso_p", [2, D])
        cso_s = dout("cso_s", [32, D])
        kout_p = dout("kout_p", [128, 256])
        vout_p = dout("vout_p", [128, 256])
        kout_s = dout("kout_s", [16, 128, 256])
        vout_s = dout("vout_s", [16, 128, 256])

        kt_scr = nc.dram_tensor("kt_scr", [4, 128, TKV], BF16).ap()
        v_scr = nc.dram_tensor("v_scr", [18, 128, 256], BF16).ap()

        def sb(name, shape, dt):
            return es.enter_context(nc.sbuf_tensor(name, shape, dt))

        def mkbuf(name, shape, dt):
            return Buf(sb(name, shape, dt), mk_sem)

        X = sb("X", [128, 8, T], F32)
        Hb = sb("Hb", [128, 8, T], BF16)
        RING = [mkbuf("ring%d" % i, [128, SLOT], BF16) for i in range(2)]
        MOD = mkbuf("MOD", [128, 72, 17], F32)
        MODKV = mkbuf("MODKV", [128, 16, 17], F32)
        SC = mkbuf("SC", [128, 8, 17], BF16)
        Ab = mkbuf("Ab", [128, 8, 17], F32)
        Gb = mkbuf("Gb", [128, 8, 17], F32)
        VEC = mkbuf("VEC", [128, NV], F32)
        IDF = mkbuf("IDF", [128, 128], F32)
        IDB = mkbuf("IDB", [128, 128], BF16)
        ONES = mkbuf("ONES", [128, 128], BF16)
        MASKS = mkbuf("MASKS", [128, 6, 128], BF16)
        ESINK = mkbuf("ESINK", [128, 8], F32)
        ACTP = Rot([mkbuf("act%d" % i, [128, 4, 512], BF16) for i in range(2)])
        T32 = Rot([mkbuf("t32_%d" % i, [128, 520], F32) for i in range(4)])
        RS = Rot([mkbuf("rs%d" % i, [128, 512], F32) for i in range(2)])
        TB = Rot([mkbuf("tb%d" % i, [128, 512], BF16) for i in range(4)])
        P2 = Rot([mkbuf("p2_%d" % i, [128, 2, 512], BF16) for i in range(4)])
        KTC = mkbuf("KTC", [128, 16, 128], BF16)
        CARRY = mkbuf("CARRY", [128, 2, 2], F32)
        CSOP = mkbuf("CSOP", [128, 8, 2], F32)
        CSOS = mkbuf("CSOS", [128, 8, 32], F32)
        SCS = mkbuf("SCS", [128, 8, 32], F32)
        KVO = Rot([mkbuf("kvo%d" % i, [128, 256], F32) for i in range(2)])

        banks = []
        for i in range(8):
            pt = es.enter_context(nc.psum_tensor("bk%d" % i, [128, 512], F32))
            banks.append((pt, Res()))
        bank_i = [0]

        def bank():
            b = banks[bank_i[0] % 6]
            bank_i[0] += 1
            return b

        nbank_i = [0]

        def nbank():
            b = banks[6 + nbank_i[0] % 2]
            nbank_i[0] += 1
            return b

        sems = {e: mk_sem() for e in Tracker.ENG}
        tr = Tracker(sems)

        TILES0 = [(0, 0, 130, "halo")] + [(1 + k, 130 + 512 * k, 512, "prompt") for k in range(4)] + [(5, 2178, 128, "sample")]
        TILES1 = TILES0[1:]
        XR = [[Res() for _ in range(8)] for _ in range(6)]
        HR = [[Res() for _ in range(8)] for _ in range(6)]
        KTSCR = [[Res() for _ in range(6)] for _ in range(4)]
        VSCR = [Res() for _ in range(6)]
        out_events = []

        def ACT(out, in_, func, reads, writes, bias=None, scale=None):
            kw = {}
            if bias is not None:
                kw["bias"] = bias
            if scale is not None:
                kw["scale"] = scale
            tr.op("act", lambda e: e.activation(out=out, in_=in_, func=func, **kw), reads, writes)

        def TT(out, in0, in1, op, reads, writes):
            tr.op("dve", lambda e: e.tensor_tensor(out=out, in0=in0, in1=in1, op=op), reads, writes)

        def PTT(out, in0, in1, op, reads, writes):
            tr.op("pool", lambda e: e.tensor_tensor(out=out, in0=in0, in1=in1, op=op), reads, writes)

        def TS(out, in0, s1, s2, op0, op1, reads, writes):
            if s2 is None:
                tr.op("dve", lambda e: e.tensor_scalar(out=out, in0=in0, scalar1=s1, scalar2=None, op0=op0), reads, writes)
            else:
                tr.op("dve", lambda e: e.tensor_scalar(out=out, in0=in0, scalar1=s1, scalar2=s2, op0=op0, op1=op1), reads, writes)

        def STT(out, in0, scalar, in1, op0, op1, reads, writes):
            tr.op("dve", lambda e: e.scalar_tensor_tensor(out=out, in0=in0, scalar=scalar, in1=in1, op0=op0, op1=op1), reads, writes)

        def CPY(eng, out, in_, reads, writes):
            if eng == "act":
                tr.op("act", lambda e: e.copy(out=out, in_=in_), reads, writes)
            else:
                tr.op("dve", lambda e: e.tensor_copy(out=out, in_=in_), reads, writes)

        def TRANSP(out, in_, ident, reads, writes):
            tr.op("pe", lambda e: e.transpose(out, in_, ident), reads, writes)

        cp_i = [0]

        def evac(out, in_, reads, writes):
            cp_i[0] += 1
            CPY("act" if cp_i[0] % 2 else "dve", out, in_, reads, writes)

        def s3(ap):
            return ap.rearrange("p (b i) -> p b i", b=16)

        def bc8(ap16):
            return ap16.unsqueeze(2).broadcast_to([128, 16, 8])

        tr.dma("sp", [(VEC.t[:, :], vec_d)], VEC.ds, writes=[VEC.res])
        tr.dma("sp", [(IDF.t[:, :], ident_d)], IDF.ds, writes=[IDF.res])
        tr.dma("pool", [(IDB.t[:, :], ident_d)], IDB.ds, writes=[IDB.res])
        tr.dma("pool", [(MASKS.t[:, :, :], masks_d)], MASKS.ds, writes=[MASKS.res])
        tr.op("dve", lambda e: e.memset(ONES.t[:, :], 1.0), [], [ONES.res])
        ACT(ESINK.t[:, :], VEC.t[:, SINK:SINK + 8], AF.Exp, [VEC.res], [ESINK.res])

        def f32view(b):
            return b.t[:, :, :].rearrange("p a b -> p (a b)").bitcast(F32)

        cb = ACTP.get()
        cv_ = f32view(cb)
        tr.dma("sp", [(cv_[0:17, :], cin)], cb.ds, writes=[cb.res])
        ACT(cv_[0:17, :], cv_[0:17, :], AF.Silu, [cb.res], [cb.res])
        bk, rk = bank()
        for j in range(8):
            TRANSP(bk[:, j * 17:(j + 1) * 17], cv_[0:17, j * 128:(j + 1) * 128], IDF.t[0:17, 0:17], [cb.res, IDF.res], [rk])
        CPY("dve", SC.t[:, :, :], bk[:, 0:136].rearrange("p (j s) -> p j s", j=8), [rk], [SC.res])
        cb = ACTP.get()
        cv_ = f32view(cb)
        tr.dma("sp", [(cv_[0:32, :], scin)], cb.ds, writes=[cb.res])
        bk, rk = bank()
        for j in range(8):
            TRANSP(bk[:, j * 32:(j + 1) * 32], cv_[0:32, j * 128:(j + 1) * 128], IDF.t[0:32, 0:32], [cb.res, IDF.res], [rk])
        CPY("dve", SCS.t[:, :, :], bk[:, 0:256].rearrange("p (j s) -> p j s", j=8), [rk], [SCS.res])
        rowtiles = [(0, 2, 0, 0), (2, 128, 0, 2)] + [(130 + 128 * k, 128, 1 + k // 4, 130 + 128 * k) for k in range(16)] + [(2178, 128, 5, 2178)]
        for (r0, nr, ti, c0) in rowtiles:
            cb = ACTP.get()
            cv_ = f32view(cb)
            tr.dma("sp", [(cv_[0:nr, :], xin[r0:r0 + nr, :])], cb.ds, writes=[cb.res])
            for half in range(2):
                bk, rk = bank()
                for jj in range(4):
                    j = half * 4 + jj
                    TRANSP(bk[:, jj * 128:jj * 128 + nr], cv_[0:nr, j * 128:(j + 1) * 128], IDF.t[0:nr, 0:nr], [cb.res, IDF.res], [rk])
                evac(X[:, half * 4:half * 4 + 4, c0:c0 + nr], bk[:, :].rearrange("p (j c) -> p j c", j=4)[:, :, 0:nr],
                     [rk], [XR[ti][half * 4 + jj] for jj in range(4)])
        dsk = [mk_sem(), 0]
        out_events.append(tr.dma("sp", [(kout_s[:, 0:120, :], kcache[:, 8:128, :]), (vout_s[:, 0:120, :], vcache[:, 8:128, :])], dsk))

        parts = []
        ring_n = [0]

        def modv(m, j):
            return MOD.t[:, m * 8 + j, :]

        def gate_acc(kind_src):
            if kind_src[0] == "G":
                return (lambda i: Gb.t[:, i, 0:1]), (lambda i: Gb.t[:, i, 1:17]), Gb.res
            m = kind_src[1]
            return (lambda i: MOD.t[:, m * 8 + i, 0:1]), (lambda i: MOD.t[:, m * 8 + i, 1:17]), MOD.res

        def resid(tile, i, bk, rk, gacc):
            ti, c0, n, kind = tile
            gp, gs, gres = gacc
            xs = X[:, i, c0:c0 + n]
            if kind != "sample":
                STT(xs, bk[:, :n], gp(i), xs, ALU.mult, ALU.add, [rk, gres, XR[ti][i]], [XR[ti][i]])
            else:
                tb_ = T32.get()
                TT(s3(tb_.t[:, 0:128]), s3(bk[:, 0:128]), bc8(gs(i)), ALU.mult, [rk, gres], [tb_.res])
                TT(xs, xs, tb_.t[:, 0:128], ALU.add, [tb_.res, XR[ti][i]], [XR[ti][i]])

        def prep_mod(l, sub, ffn):
            ng = VEC.t[:, NG0 + (l * 3 + sub) * 8: NG0 + (l * 3 + sub) * 8 + 8]
            sc = MOD.t[:, (3 * sub + 1) * 8:(3 * sub + 1) * 8 + 8, :]
            TS(Ab.t[:, :, :], sc, 1.0, None, ALU.add, None, [MOD.res], [Ab.res])
            TT(Ab.t[:, :, :], Ab.t[:, :, :], ng.unsqueeze(2).broadcast_to([128, 8, 17]), ALU.mult, [Ab.res, VEC.res], [Ab.res])
            if ffn:
                gt = MOD.t[:, (3 * sub + 2) * 8:(3 * sub + 2) * 8 + 8, :]
                TS(Gb.t[:, :, :], gt, 0.5, None, ALU.mult, None, [MOD.res], [Gb.res])

        def prep_mod_kv():
            ng = VEC.t[:, KVG:KVG + 8]
            TS(Ab.t[:, :, :], MODKV.t[:, 8:16, :], 1.0, None, ALU.add, None, [MODKV.res], [Ab.res])
            TT(Ab.t[:, :, :], Ab.t[:, :, :], ng.unsqueeze(2).broadcast_to([128, 8, 17]), ALU.mult, [Ab.res, VEC.res], [Ab.res])

        def norm_items(tiles, bview, bres, pre=None, final=False):
            items = []
            first = [True]
            for tile in tiles:
                ti, c0, n, kind = tile
                st = {}

                def Afn(tile=tile, st=st):
                    ti, c0, n, kind = tile
                    if first[0] and pre is not None:
                        pre()
                    first[0] = False
                    bk, rk = nbank()
                    for j in range(8):
                        sq = TB.get()
                        ACT(sq.t[:, :n], X[:, j, c0:c0 + n], AF.Square, [XR[ti][j]], [sq.res])
                        tr.mm([(bk[:, :n], ONES.t[:, :], sq.t[:, :n], j == 0, j == 7)], [sq.res, ONES.res], [rk])
                    st["bk"] = (bk, rk)

                def Bfn(tile=tile, st=st):
                    ti, c0, n, kind = tile
                    bk, rk = st["bk"]
                    rs = RS.get()
                    ACT(rs.t[:, :n], bk[:, :n], AF.Ln, [rk], [rs.res], bias=EPS, scale=1.0 / D)
                    ACT(rs.t[:, :n], rs.t[:, :n], AF.Exp, [rs.res], [rs.res], scale=-0.5)
                    st["rs"] = rs
                    if final:
                        return
                    for j in range(8):
                        xs = X[:, j, c0:c0 + n]
                        tmp = T32.get()
                        if kind != "sample":
                            STT(tmp.t[:, :n], xs, Ab.t[:, j, 0:1], rs.t[:, :n], ALU.mult, ALU.mult, [XR[ti][j], Ab.res, rs.res], [tmp.res])
                            ACT(Hb[:, j, c0:c0 + n], tmp.t[:, :n], AF.Identity, [tmp.res, bres], [HR[ti][j]], bias=bview(j)[:, 0:1])
                        else:
                            TT(tmp.t[:, :n], xs, rs.t[:, :n], ALU.mult, [XR[ti][j], rs.res], [tmp.res])
                            TT(s3(tmp.t[:, :n]), s3(tmp.t[:, :n]), bc8(Ab.t[:, j, 1:17]), ALU.mult, [tmp.res, Ab.res], [tmp.res])
                            TT(s3(Hb[:, j, c0:c0 + n]), s3(tmp.t[:, :n]), bc8(bview(j)[:, 1:17]), ALU.add, [tmp.res, bres], [HR[ti][j]])

                items.append((Afn, Bfn, st))
            return items

        def add_plain(items):
            parts.append(dict(dma=None, items=lambda slot: [(a, b) for (a, b, *_r) in items]))

        def ada_part(src, c0, ncol, dstbuf, ch0, bcol0):
            nch = ncol // 128

            def dma(slot):
                v = slot.t[:, 0:8 * ncol].rearrange("p (k f) -> p k f", k=8)
                return [(v, src.rearrange("(k p) f -> p k f", p=128)[:, :, c0:c0 + ncol])]

            def items(slot):
                W = slot.t[:, 0:8 * ncol].rearrange("p (k f) -> p k f", k=8)

                def Afn():
                    bk, rk = bank()
                    for oc in range(nch):
                        tr.mm([(bk[:, oc * 17:(oc + 1) * 17], W[:, kk, oc * 128:(oc + 1) * 128], SC.t[:, kk, :], kk == 0, kk == 7) for kk in range(8)],
                              [slot.res, SC.res], [rk])
                    TT(dstbuf.t[:, ch0:ch0 + nch, :], bk[:, 0:nch * 17].rearrange("p (c s) -> p c s", c=nch),
                       VEC.t[:, bcol0:bcol0 + nch].unsqueeze(2).broadcast_to([128, nch, 17]), ALU.add, [rk, VEC.res], [dstbuf.res])

                return [(Afn, None)]

            parts.append(dict(dma=dma, items=items))

        def ada_micros(src, ncols, dstbuf, ch0, bcol0):
            return [(src, c * 128, dstbuf, ch0 + c, bcol0 + c) for c in range(ncols // 128)]

        class Side:
            def __init__(self, micros, per_item):
                self.todo = list(micros)
                self.pending = []
                self.k = per_item

            def _issue(self):
                (src, col0, dstbuf, ch, bcol) = self.todo.pop(0)
                pb_ = P2.get()
                v = pb_.t[:, :, :].rearrange("p a b -> p (a b)").rearrange("p (k f) -> p k f", k=8)
                tr.dma("pool", [(v, src.rearrange("(k p) f -> p k f", p=128)[:, :, col0:col0 + 128])], pb_.ds_pool, writes=[pb_.res])
                self.pending.append((pb_, v, dstbuf, ch, bcol))

            def _compute(self):
                (pb_, v, dstbuf, ch, bcol) = self.pending.pop(0)
                bk, rk = bank()
                tr.mm([(bk[:, 0:17], v[:, kk, :], SC.t[:, kk, :], kk == 0, kk == 7) for kk in range(8)], [pb_.res, SC.res], [rk])
                TT(dstbuf.t[:, ch, :], bk[:, 0:17], VEC.t[:, bcol:bcol + 1].broadcast_to([128, 17]), ALU.add, [rk, VEC.res], [dstbuf.res])

            def step(self):
                for _ in range(self.k):
                    if self.pending:
                        self._compute()
                for _ in range(self.k):
                    if self.todo:
                        self._issue()

            def flush(self):
                while self.pending or self.todo:
                    while self.pending:
                        self._compute()
                    for _ in range(3):
                        if self.todo:
                            self._issue()

        def ffn_parts(l, w, tiles, side=None):
            f0 = 0
            while f0 < 22:
                F = min(4, 22 - f0)

                def dma(slot, f0=f0, F=F):
                    c0 = f0 * 128
                    nc_ = F * 128
                    g = slot.t[:, 0:8 * nc_].rearrange("p (k f) -> p k f", k=8)
                    u = slot.t[:, 8 * nc_:16 * nc_].rearrange("p (k f) -> p k f", k=8)
                    dd = slot.t[:, 16 * nc_:16 * nc_ + F * D].rearrange("p (f d) -> p f d", f=F)
                    return [(g, wg_d[l, w].rearrange("(k p) f -> p k f", p=128)[:, :, c0:c0 + nc_]),
                            (u, wu_d[l, w].rearrange("(k p) f -> p k f", p=128)[:, :, c0:c0 + nc_]),
                            (dd, wd_d[l, w][c0:c0 + nc_, :].rearrange("(f p) d -> p f d", p=128))]

                def items(slot, F=F):
                    nc_ = F * 128
                    Wg = slot.t[:, 0:8 * nc_].rearrange("p (k f) -> p k f", k=8)
                    Wu = slot.t[:, 8 * nc_:16 * nc_].rearrange("p (k f) -> p k f", k=8)
                    Wd = slot.t[:, 16 * nc_:16 * nc_ + F * D].rearrange("p (f d) -> p f d", f=F)
                    gacc = gate_acc(("G",))
                    its = []
                    for tile in tiles:
                        st = {}

                        def Afn(tile=tile, st=st):
                            ti, c0, n, kind = tile
                            if side is not None:
                                side.step()
                            ab = ACTP.get()
                            st["ab"] = ab
                            for f in range(F):
                                bg, rg = bank()
                                bu, ru = bank()
                                tr.mm([(bg[:, :n], Wg[:, kk, f * 128:(f + 1) * 128], Hb[:, kk, c0:c0 + n], kk == 0, kk == 7) for kk in range(8)],
                                      [slot.res] + HR[ti], [rg])
                                tr.mm([(bu[:, :n], Wu[:, kk, f * 128:(f + 1) * 128], Hb[:, kk, c0:c0 + n], kk == 0, kk == 7) for kk in range(8)],
                                      [slot.res] + HR[ti], [ru])
                                sg = T32.get()
                                ACT(sg.t[:, :n], bg[:, :n], AF.Silu, [rg], [sg.res])
                                TT(ab.t[:, f, :n], sg.t[:, :n], bu[:, :n], ALU.mult, [sg.res, ru], [ab.res])

                        def Bfn(tile=tile, st=st):
                            ti, c0, n, kind = tile
                            ab = st["ab"]
                            for i in range(8):
                                bd, rd = bank()
                                tr.mm([(bd[:, :n], Wd[:, f, i * 128:(i + 1) * 128], ab.t[:, f, :n], f == 0, f == F - 1) for f in range(F)],
                                      [slot.res, ab.res], [rd])
                                resid(tile, i, bd, rd, gacc)

                        its.append((Afn, Bfn))
                    return its

                parts.append(dict(dma=dma, items=items))
                f0 += F

        def conv_parts(tiles):
            for p in range(4):
                def dma(slot, p=p):
                    prs = []
                    for q in range(3):
                        v = slot.t[:, q * 2048:(q + 1) * 2048].rearrange("p (k f) -> p k f", k=8)
                        prs.append((v, cwin.rearrange("(k p) f -> p k f", p=128)[:, :, q * D + p * 256:q * D + p * 256 + 256]))
                    v = slot.t[:, 6144:8192].rearrange("p (c d) -> p c d", c=2)
                    prs.append((v, cwout[p * 256:(p + 1) * 256, :].rearrange("(c p) d -> p c d", p=128)))
                    return prs

                def items(slot, p=p):
                    Wb = slot.t[:, 0:2048].rearrange("p (k f) -> p k f", k=8)
                    Wc = slot.t[:, 2048:4096].rearrange("p (k f) -> p k f", k=8)
                    Wv_ = slot.t[:, 4096:6144].rearrange("p (k f) -> p k f", k=8)
                    Wo_ = slot.t[:, 6144:8192].rearrange("p (c d) -> p c d", c=2)
                    gacc = gate_acc(("M", 5))
                    its = []
                    for tile in tiles:
                        st = {}

                        def Afn(tile=tile, st=st):
                            ti, c0, n, kind = tile
                            zb = P2.get()
                            st["zb"] = zb
                            if kind == "halo":
                                tr.op("dve", lambda e: e.memset(CARRY.t[:, :, :], 0.0), [], [CARRY.res])
                            for jl in range(2):
                                j = 2 * p + jl
                                bb, rb = bank()
                                bc, rc = bank()
                                bv, rv = bank()
                                for (bkk, rkk, W) in ((bb, rb, Wb), (bc, rc, Wc), (bv, rv, Wv_)):
                                    tr.mm([(bkk[:, :n], W[:, kk, jl * 128:(jl + 1) * 128], Hb[:, kk, c0:c0 + n], kk == 0, kk == 7) for kk in range(8)],
                                          [slot.res] + HR[ti], [rkk])
                                csb = T32.get()
                                ACT(csb.t[:, :n], bc[:, :n], AF.Identity, [rc], [csb.res])
                                U = T32.get()
                                cvb = T32.get()
                                w0 = VEC.t[:, CW + j:CW + j + 1]
                                w1 = VEC.t[:, CW + 8 + j:CW + 8 + j + 1]
                                w2 = VEC.t[:, CW + 16 + j:CW + 16 + j + 1]
                                if kind != "sample":
                                    CPY("dve", U.t[:, 0:2], CARRY.t[:, jl, :], [CARRY.res], [U.res])
                                    TT(U.t[:, 2:2 + n], csb.t[:, :n], bv[:, :n], ALU.mult, [csb.res, rv], [U.res])
                                    if kind == "halo":
                                        TS(U.t[:, 2:2 + n], U.t[:, 2:2 + n], VEC.t[:, FLAG:FLAG + 1], None, ALU.mult, None, [U.res, VEC.res], [U.res])
                                    CPY("dve", CARRY.t[:, jl, :], U.t[:, n:n + 2], [U.res], [CARRY.res])
                                    if ti == 4:
                                        CPY("dve", CSOP.t[:, j, :], U.t[:, n:n + 2], [U.res], [CSOP.res])
                                    TS(cvb.t[:, :n], U.t[:, 0:n], w0, None, ALU.mult, None, [U.res, VEC.res], [cvb.res])
                                    STT(cvb.t[:, :n], U.t[:, 1:n + 1], w1, cvb.t[:, :n], ALU.mult, ALU.add, [U.res, cvb.res], [cvb.res])
                                    STT(cvb.t[:, :n], U.t[:, 2:n + 2], w2, cvb.t[:, :n], ALU.mult, ALU.add, [U.res, cvb.res], [cvb.res])
                                    TT(zb.t[:, jl, :n], bb[:, :n], cvb.t[:, :n], ALU.mult, [rb, cvb.res], [zb.res])
                                else:
                                    U3 = U.t[:, 0:160].rearrange("p (b i) -> p b i", b=16)
                                    CPY("dve", U3[:, :, 0:2], SCS.t[:, j, :].rearrange("p (b i) -> p b i", b=16), [SCS.res], [U.res])
                                    TT(U3[:, :, 2:10], s3(csb.t[:, :n]), s3(bv[:, :n]), ALU.mult, [csb.res, rv], [U.res])
                                    CPY("dve", CSOS.t[:, j, :].rearrange("p (b i) -> p b i", b=16), U3[:, :, 8:10], [U.res], [CSOS.res])
                                    c3 = s3(cvb.t[:, :n])
                                    TS(c3, U3[:, :, 0:8], w0, None, ALU.mult, None, [U.res, VEC.res], [cvb.res])
                                    STT(c3, U3[:, :, 1:9], w1, c3, ALU.mult, ALU.add, [U.res, cvb.res], [cvb.res])
                                    STT(c3, U3[:, :, 2:10], w2, c3, ALU.mult, ALU.add, [U.res, cvb.res], [cvb.res])
                                    TT(zb.t[:, jl, :n], bb[:, :n], cvb.t[:, :n], ALU.mult, [rb, cvb.res], [zb.res])

                        def Bfn(tile=tile, st=st):
                            ti, c0, n, kind = tile
                            zb = st["zb"]
                            for i in range(8):
                                bd, rd = bank()
                                tr.mm([(bd[:, :n], Wo_[:, c, i * 128:(i + 1) * 128], zb.t[:, c, :n], c == 0, c == 1) for c in range(2)],
                                      [slot.res, zb.res], [rd])
                                resid(tile, i, bd, rd, gacc)

                        its.append((Afn, Bfn))
                    return its

                parts.append(dict(dma=dma, items=items))

        KVT = [(0, 2, 128, "halo")] + TILES1

        def kv_part():
            def dma(slot):
                prs = []
                for (off, src) in ((0, wk_d), (4096, wksw_d)):
                    v = slot.t[:, off:off + 4096].rearrange("p (k g e d) -> p k g e d", k=8, g=4, e=2)
                    s = src.rearrange("(k p) (g d) -> p k g d", p=128, g=4)
                    for e in range(2):
                        for g_ in range(4):
                            prs.append((v[:, :, g_, e, :], s[:, :, g_, :]))
                v = slot.t[:, 8192:8192 + 2048].rearrange("p (k d) -> p k d", k=8)
                prs.append((v, wv_d.rearrange("(k p) d -> p k d", p=128)))
                return prs

            def items(slot):
                Wk = slot.t[:, 0:4096].rearrange("p (k g m) -> p k g m", k=8, g=4)
                Wks = slot.t[:, 4096:8192].rearrange("p (k g m) -> p k g m", k=8, g=4)
                Wv_ = slot.t[:, 8192:8192 + 2048].rearrange("p (k d) -> p k d", k=8)
                its = []
                for tile in KVT:
                    def Afn(tile=tile):
                        ti, c0, n, kind = tile
                        rp = ACTP.get()
                        rv_ = f32view(rp).rearrange("p (a b) -> p a b", a=2)
                        tr.dma("sp", [(rv_[:, 0, 0:n], ropec[:, c0 - 2:c0 - 2 + n]), (rv_[:, 1, 0:n], ropes[:, c0 - 2:c0 - 2 + n])], rp.ds, writes=[rp.res])
                        is_out = (ti == 4) or (ti == 5)
                        for g in range(4):
                            bk_, rk_ = bank()
                            bs_, rs_ = bank()
                            tr.mm([(bk_[:, :n], Wk[:, kk, g, :], Hb[:, kk, c0:c0 + n], kk == 0, kk == 7) for kk in range(8)], [slot.res] + HR[ti], [rk_])
                            tr.mm([(bs_[:, :n], Wks[:, kk, g, :], Hb[:, kk, c0:c0 + n], kk == 0, kk == 7) for kk in range(8)], [slot.res] + HR[ti], [rs_])
                            t1 = T32.get()
                            t2 = T32.get()
                            TT(t1.t[:, :n], bk_[:, :n], rv_[:, 0, 0:n], ALU.mult, [rk_, rp.res], [t1.res])
                            TT(t2.t[:, :n], bs_[:, :n], rv_[:, 1, 0:n], ALU.mult, [rs_, rp.res], [t2.res])
                            TT(t1.t[:, :n], t1.t[:, :n], t2.t[:, :n], ALU.add, [t1.res, t2.res], [t1.res])
                            kb = TB.get()
                            ACT(kb.t[:, :n], t1.t[:, :n], AF.Identity, [t1.res], [kb.res])
                            if "a" not in KDBG:
                                tr.dma("sp", [(kt_scr[g, :, c0 - 2:c0 - 2 + n], kb.t[:, :n])], kb.ds, reads=[kb.res], writes=[KTSCR[g][ti]])
                            if is_out and "b" not in KDBG:
                                bt_, rt_ = bank()
                                TRANSP(bt_[:, 0:128], t1.t[:, n - 128:n], IDF.t[:, :], [t1.res, IDF.res], [rt_])
                                if ti not in st_k:
                                    st_k[ti] = KVO.get()
                                ko = st_k[ti]
                                evac(ko.t[:, g * 64:(g + 1) * 64], bt_[:, 0:64], [rt_], [ko.res])
                        if is_out and "b" not in KDBG:
                            ko = st_k[ti]
                            if ti == 4:
                                out_events.append(tr.dma("sp", [(kout_p, ko.t[:, :])], ko.ds, reads=[ko.res]))
                            else:
                                out_events.append(tr.dma("sp", [(kout_s[b, 120:128, :], ko.t[b * 8:(b + 1) * 8, :]) for b in range(16)], ko.ds, reads=[ko.res]))

                    def Bfn(tile=tile):
                        ti, c0, n, kind = tile
                        if "c" in KDBG:
                            return
                        nb = n // 128
                        vb = P2.get()
                        vv = vb.t[:, :, :].rearrange("p a b -> p (a b)").rearrange("p (b d) -> p b d", b=4)
                        is_out = (ti == 4) or (ti == 5)
                        for b_ in range(nb):
                            bk_, rk_ = bank()
                            tr.mm([(bk_[:, 0:256], Hb[:, kk, c0 + b_ * 128:c0 + (b_ + 1) * 128], Wv_[:, kk, :], kk == 0, kk == 7) for kk in range(8)],
                                  [slot.res] + HR[ti], [rk_])
                            if not (is_out and b_ == nb - 1):
                                evac(vv[:, b_, :], bk_[:, 0:256], [rk_], [vb.res])
                            else:
                                vo = KVO.get()
                                CPY("dve", vo.t[:, :], bk_[:, 0:256], [rk_], [vo.res])
                                CPY("act", vv[:, b_, :], vo.t[:, :], [vo.res], [vb.res])
                                if ti == 4:
                                    out_events.append(tr.dma("sp", [(vout_p, vo.t[:, :])], vo.ds, reads=[vo.res]))
                                else:
                                    out_events.append(tr.dma("sp", [(vout_s[b, 120:128, :], vo.t[b * 8:(b + 1) * 8, :]) for b in range(16)], vo.ds, reads=[vo.res]))
                        blk0 = (c0 - 2) // 128
                        if "d" not in KDBG:
                            tr.dma("sp", [(v_scr[blk0:blk0 + nb, :, :].rearrange("b p d -> p b d"), vv[:, 0:nb, :])], vb.ds, reads=[vb.res], writes=[VSCR[ti]])

                    its.append((Afn, Bfn))
                return its

            parts.append(dict(dma=dma, items=items))

        st_k = {}

        def attn_parts(tiles):
            for g in range(4):
                def dma(slot, g=g):
                    prs = []
                    prs.append((slot.t[:, 0:2048].rearrange("p (k f) -> p k f", k=8), wq_d.rearrange("(k p) f -> p k f", p=128)[:, :, g * 256:(g + 1) * 256]))
                    prs.append((slot.t[:, 2048:4096].rearrange("p (k f) -> p k f", k=8), wqsw_d.rearrange("(k p) f -> p k f", p=128)[:, :, g * 256:(g + 1) * 256]))
                    prs.append((slot.t[:, 4096:6144].rearrange("p (c d) -> p c d", c=2), wo_d[g * 256:(g + 1) * 256, :].rearrange("(c p) d -> p c d", p=128)))
                    prs.append((slot.t[:, 6144:6144 + TKV], kt_scr[g, :, :]))
                    prs.append((slot.t[:, 8448:8448 + 1152].rearrange("p (b d) -> p b d", b=18), v_scr[:, :, g * 64:(g + 1) * 64].rearrange("b p d -> p b d")))
                    prs.append((slot.t[:, 9600:9600 + 1024].rearrange("p (b d) -> p b d", b=16), kcache[:, :, g * 64:(g + 1) * 64].rearrange("b p d -> p b d")))
                    prs.append((slot.t[:, 10624:10624 + 1024].rearrange("p (b d) -> p b d", b=16), vcache[:, :, g * 64:(g + 1) * 64].rearrange("b p d -> p b d")))
                    return prs

                def items(slot, g=g):
                    Wq = slot.t[:, 0:2048].rearrange("p (k f) -> p k f", k=8)
                    Wqs = slot.t[:, 2048:4096].rearrange("p (k f) -> p k f", k=8)
                    Wo_ = slot.t[:, 4096:6144].rearrange("p (c d) -> p c d", c=2)
                    KT = slot.t[:, 6144:6144 + TKV]
                    Vg = slot.t[:, 8448:8448 + 1152].rearrange("p (b d) -> p b d", b=18)
                    kc = slot.t[:, 9600:9600 + 1024].rearrange("p (b d) -> p b d", b=16)
                    vc = slot.t[:, 10624:10624 + 1024].rearrange("p (b d) -> p b d", b=16)
                    gacc = gate_acc(("M", 5))
                    esk = ESINK.t[:, g * 2:(g + 1) * 2].unsqueeze(2).broadcast_to([128, 2, 128])
                    its = []

                    def Pfn():
                        for q4 in range(4):
                            bk_, rk_ = bank()
                            for bl in range(4):
                                b = q4 * 4 + bl
                                for e in range(2):
                                    tr.mm([(bk_[e * 64:(e + 1) * 64, bl * 128:(bl + 1) * 128], kc[:, b, :], IDB.t[:, :], True, True)], [slot.res, IDB.res], [rk_])
                            evac(KTC.t[:, q4 * 4:(q4 + 1) * 4, :], bk_[:, :].rearrange("p (b k) -> p b k", b=4), [rk_], [KTC.res])

                    for tile in tiles:
                        st = {}

                        def Afn(tile=tile, st=st):
                            ti, c0, n, kind = tile
                            if kind == "sample":
                                Pfn()
                            rp = ACTP.get()
                            rv_ = f32view(rp).rearrange("p (a b) -> p a b", a=2)
                            tr.dma("sp", [(rv_[:, 0, 0:n], ropec[:, c0 - 2:c0 - 2 + n]), (rv_[:, 1, 0:n], ropes[:, c0 - 2:c0 - 2 + n])], rp.ds, writes=[rp.res])
                            qt = P2.get()
                            st["qt"] = qt
                            for c in range(2):
                                bq, rq = bank()
                                bs_, rs_ = bank()
                                tr.mm([(bq[:, :n], Wq[:, kk, c * 128:(c + 1) * 128], Hb[:, kk, c0:c0 + n], kk == 0, kk == 7) for kk in range(8)], [slot.res] + HR[ti], [rq])
                                tr.mm([(bs_[:, :n], Wqs[:, kk, c * 128:(c + 1) * 128], Hb[:, kk, c0:c0 + n], kk == 0, kk == 7) for kk in range(8)], [slot.res] + HR[ti], [rs_])
                                t1 = T32.get()
                                t2 = T32.get()
                                TT(t1.t[:, :n], bq[:, :n], rv_[:, 0, 0:n], ALU.mult, [rq, rp.res], [t1.res])
                                TT(t2.t[:, :n], bs_[:, :n], rv_[:, 1, 0:n], ALU.mult, [rs_, rp.res], [t2.res])
                                PTT(qt.t[:, c, :n], t1.t[:, :n], t2.t[:, :n], ALU.add, [t1.res, t2.res], [qt.res])

                        def Bfn(tile=tile, st=st):
                            ti, c0, n, kind = tile
                            if "q" in KDBG or ("s" in KDBG and kind == "sample") or ("m" in KDBG and kind != "sample"):
                                return
                            qt = st["qt"]
                            ot = P2.get()
                            bst = {}

                            def stage1(blk):
                                q0 = blk * 128
                                kown = c0 - 2 + q0
                                be = [bank(), bank()]
                                pe_ = [TB.get(), TB.get()]
                                mlist = []
                                if kind != "sample":
                                    for (c_lo, k_lo) in ((0, kown - 128), (256, kown)):
                                        for e in range(2):
                                            es_ = slice(e * 64, (e + 1) * 64)
                                            mlist.append((be[e][0][:, c_lo:c_lo + 256].rearrange("p (c q) -> p c q", c=2), KT[es_, k_lo:k_lo + 128], qt.t[es_, :, q0:q0 + 128], True, True))
                                else:
                                    for b in range(16):
                                        for c_ in range(2):
                                            for e in range(2):
                                                es_ = slice(e * 64, (e + 1) * 64)
                                                o = be[e][0][:, c_ * 128 + b * 8:c_ * 128 + b * 8 + 8]
                                                mlist.append((o, KTC.t[es_, b, :], qt.t[es_, c_, b * 8:(b + 1) * 8], True, True))
                                    for e in range(2):
                                        es_ = slice(e * 64, (e + 1) * 64)
                                        mlist.append((be[e][0][:, 256:512].rearrange("p (c q) -> p c q", c=2), KT[es_, kown:kown + 128], qt.t[es_, :, q0:q0 + 128], True, True))
                                tr.mm(mlist, [slot.res, KTC.res, qt.res], [be[0][1], be[1][1]])
                                if kind != "sample":
                                    mi = 2 if (ti == 1 and blk == 0) else 0
                                else:
                                    mi = 4
                                msk = MASKS.t[:, mi:mi + 2, :].unsqueeze(2).broadcast_to([128, 2, 2, 128])
                                for e in range(2):
                                    bke, rke = be[e]
                                    ACT(pe_[e].t[:, :], bke[:, :], AF.Exp, [rke], [pe_[e].res], scale=0.125)
                                    p4 = pe_[e].t[:, :].rearrange("p (t c q) -> p t c q", t=2, c=2)
                                    (TT if e == 0 else PTT)(p4, p4, msk, ALU.mult, [pe_[e].res, MASKS.res], [pe_[e].res])
                                bst[blk] = pe_

                            def stage2(blk):
                                q0 = blk * 128
                                kown = c0 - 2 + q0
                                vown = kown // 128
                                pe_ = bst.pop(blk)
                                bd_, rd_ = bank()
                                tr.mm([(bd_[e * 64:(e + 1) * 64, 0:256], ONES.t[:, 0:64], pe_[e].t[:, t_ * 256:(t_ + 1) * 256], t_ == 0, t_ == 1) for t_ in range(2) for e in range(2)],
                                      [pe_[0].res, pe_[1].res, ONES.res], [rd_])
                                rden = RS.get()
                                r3 = rden.t[:, 0:256].rearrange("p (h q) -> p h q", h=2)
                                TT(r3, bd_[:, 0:256].rearrange("p (h q) -> p h q", h=2), esk, ALU.add, [rd_, ESINK.res], [rden.res])
                                tr.op("dve", lambda e, rden=rden: e.reciprocal(out=rden.t[:, 0:256], in_=rden.t[:, 0:256]), [rden.res], [rden.res])
                                bo, ro = bank()
                                mlist = []
                                if kind != "sample":
                                    for (vb_, lo, st_, sp_) in ((vown - 1, 0, True, False), (vown, 256, False, True)):
                                        for e in range(2):
                                            es_ = slice(e * 64, (e + 1) * 64)
                                            mlist.append((bo[es_, 0:256], Vg[:, vb_, :], pe_[e].t[:, lo:lo + 256], st_, sp_))
                                else:
                                    for e in range(2):
                                        es_ = slice(e * 64, (e + 1) * 64)
                                        mlist.append((bo[es_, 0:256], Vg[:, vown, :], pe_[e].t[:, 256:512], True, False))
                                    for b in range(16):
                                        for c_ in range(2):
                                            for e in range(2):
                                                es_ = slice(e * 64, (e + 1) * 64)
                                                o = bo[es_, c_ * 128 + b * 8:c_ * 128 + b * 8 + 8]
                                                r_ = pe_[e].t[:, c_ * 128 + b * 8:c_ * 128 + b * 8 + 8]
                                                mlist.append((o, vc[:, b, :], r_, False, (b == 15 and c_ == 1)))
                                tr.mm(mlist, [slot.res, pe_[0].res, pe_[1].res], [ro])
                                TT(ot.t[:, :, q0:q0 + 128], bo[:, 0:256].rearrange("p (c q) -> p c q", c=2),
                                   rden.t[:, 0:256].rearrange("p (c q) -> p c q", c=2), ALU.mult, [ro, rden.res], [ot.res])

                            nblk = n // 128
                            for blk in range(nblk):
                                stage1(blk)
                                if blk >= 1:
                                    stage2(blk - 1)
                            stage2(nblk - 1)
                            for i in range(8):
                                bd, rd = bank()
                                tr.mm([(bd[:, :n], Wo_[:, c, i * 128:(i + 1) * 128], ot.t[:, c, :n], c == 0, c == 1) for c in range(2)], [slot.res, ot.res], [rd])
                                resid(tile, i, bd, rd, gacc)

                        its.append((Afn, Bfn))
                    return its

                parts.append(dict(dma=dma, items=items, extra=[KTSCR[g][t_] for t_ in range(6)] + VSCR))

        def final_items(tiles):
            base = norm_items(tiles, None, None, final=True)
            its = []
            for (Afn, Bfn, st), tile in zip(base, tiles):
                def B2(tile=tile, st=st, Bfn=Bfn):
                    ti, c0, n, kind = tile
                    Bfn()
                    rs = st["rs"]
                    for blk in range(n // 128):
                        q0 = blk * 128
                        sg_ = ACTP.get()
                        sv_ = f32view(sg_)
                        for half in range(2):
                            bk_, rk_ = bank()
                            for jj in range(4):
                                j = half * 4 + jj
                                yt = T32.get()
                                STT(yt.t[:, 0:128], X[:, j, c0 + q0:c0 + q0 + 128], VEC.t[:, FNG + j:FNG + j + 1], rs.t[:, q0:q0 + 128], ALU.mult, ALU.mult,
                                    [XR[ti][j], VEC.res, rs.res], [yt.res])
                                TRANSP(bk_[:, jj * 128:(jj + 1) * 128], yt.t[:, 0:128], IDF.t[:, :], [yt.res, IDF.res], [rk_])
                            evac(sv_[:, half * 512:(half + 1) * 512], bk_[:, :], [rk_], [sg_.res])
                        if kind == "sample":
                            dst = y_s
                        else:
                            r0 = c0 - 130 + q0
                            dst = y_p[r0:r0 + 128, :]
                        out_events.append(tr.dma("sp", [(dst, sv_[:, :])], sg_.ds, reads=[sg_.res]))

                its.append((Afn, B2))
            parts.append(dict(dma=None, items=lambda slot: its))

        def cso_out():
            def Afn():
                sg_ = ACTP.get()
                sv_ = f32view(sg_)
                for half in range(2):
                    bk_, rk_ = bank()
                    for jj in range(4):
                        TRANSP(bk_[0:32, jj * 128:(jj + 1) * 128], CSOS.t[:, half * 4 + jj, :], IDF.t[:, :], [CSOS.res, IDF.res], [rk_])
                    evac(sv_[0:32, half * 512:(half + 1) * 512], bk_[0:32, :], [rk_], [sg_.res])
                out_events.append(tr.dma("sp", [(cso_s, sv_[0:32, :])], sg_.ds, reads=[sg_.res]))
                sg2 = ACTP.get()
                sv2 = f32view(sg2)
                for half in range(2):
                    bk3, rk3 = bank()
                    for jj in range(4):
                        TRANSP(bk3[0:2, jj * 128:(jj + 1) * 128], CSOP.t[:, half * 4 + jj, :], IDF.t[:, :], [CSOP.res, IDF.res], [rk3])
                    evac(sv2[0:2, half * 512:(half + 1) * 512], bk3[0:2, :], [rk3], [sg2.res])
                out_events.append(tr.dma("sp", [(cso_p, sv2[0:2, :])], sg2.ds, reads=[sg2.res]))

            parts.append(dict(dma=None, items=lambda slot: [(Afn, None)]))

        side_l0 = Side(ada_micros(w_ada[0], 9 * D, MOD, 0, BADA)[24:], 2)
        side_l1 = Side(ada_micros(wadakv, 2 * D, MODKV, 0, BKV) + ada_micros(w_ada[1], 9 * D, MOD, 0, BADA + 72), 3)

        def flush_part(sd):
            parts.append(dict(dma=None, items=lambda slot: [(sd.flush, None)]))

        ada_part(w_ada[0], 0, 1536, MOD, 0, BADA + 0)
        ada_part(w_ada[0], 1536, 1536, MOD, 12, BADA + 12)
        add_plain(norm_items(TILES0, lambda j: modv(0, j), MOD.res, pre=lambda: prep_mod(0, 0, True)))
        ffn_parts(0, 0, TILES0, side=side_l0)
        flush_part(side_l0)
        add_plain(norm_items(TILES0, lambda j: modv(3, j), MOD.res, pre=lambda: prep_mod(0, 1, False)))
        conv_parts(TILES0)
        cso_out()
        add_plain(norm_items(TILES0, lambda j: modv(6, j), MOD.res, pre=lambda: prep_mod(0, 2, True)))
        ffn_parts(0, 1, TILES0, side=side_l1)
        flush_part(side_l1)
        add_plain(norm_items(TILES0, lambda j: MODKV.t[:, j, :], MODKV.res, pre=prep_mod_kv))
        kv_part()
        add_plain(norm_items(TILES1, lambda j: modv(0, j), MOD.res, pre=lambda: prep_mod(1, 0, True)))
        ffn_parts(1, 0, TILES1)
        add_plain(norm_items(TILES1, lambda j: modv(3, j), MOD.res, pre=lambda: prep_mod(1, 1, False)))
        attn_parts(TILES1)
        add_plain(norm_items(TILES1, lambda j: modv(6, j), MOD.res, pre=lambda: prep_mod(1, 2, True)))
        ffn_parts(1, 1, TILES1)
        final_items(TILES1)

        import os as _os
        _ks = _os.environ.get("KSTOP")
        if _ks is not None:
            parts = parts[:int(_ks)]
        dparts = [p for p in parts if p["dma"] is not None]
        for k, p in enumerate(dparts):
            p["slot"] = RING[k % 2]
            p["didx"] = k

        def issue(k):
            if k < len(dparts):
                p = dparts[k]
                slot = p["slot"]
                tr.dma("pool", p["dma"](slot), slot.ds, reads=p.get("extra", []), writes=[slot.res])

        issue(0)
        prevB = [None]
        for p in parts:
            slot = p.get("slot")
            its = p["items"](slot)
            for n_, it in enumerate(its):
                a, b = it[0], it[1]
                if a:
                    a()
                if prevB[0]:
                    prevB[0]()
                prevB[0] = b
                if n_ == 0 and p["dma"] is not None:
                    issue(p["didx"] + 1)
        if prevB[0]:
            prevB[0]()

        tr.wait_only("sp", out_events)

        block = es.enter_context(nc.Block())

        @block.tensor
        def _(e):
            for f in tr.q["pe"]:
                f(e)

        @block.scalar
        def _(e):
            for f in tr.q["act"]:
                f(e)

        @block.vector
        def _(e):
            for f in tr.q["dve"]:
                f(e)

        @block.gpsimd
        def _(e):
            for f in tr.q["pool"]:
                f(e)

        @block.sync
        def _(e):
            for f in tr.q["sp"]:
                f(e)
    return nc


_NC = [None]


def _rope_tables(pos):
    inv = (np.float32(500000.0) ** (-np.arange(0, 16, 2, dtype=np.float32) / np.float32(16))).astype(np.float32)
    ang = pos.astype(np.float32)[None, :] * inv[:, None]
    cos = np.cos(ang).astype(np.float32)
    sin = np.sin(ang).astype(np.float32)
    n = pos.shape[0]
    C = np.ones((64, n), np.float32)
    S = np.zeros((64, n), np.float32)
    C[0:8] = cos
    C[8:16] = cos
    S[0:8] = -sin
    S[8:16] = sin
    return C, S


def _swap_cols(w, nheads):
    perm = []
    for h in range(nheads):
        base = h * 64
        perm += [base + 8 + d for d in range(8)] + [base + d for d in range(8)] + [base + d for d in range(16, 64)]
    return np.ascontiguousarray(w[:, perm])


def kernel(x_prompt, x_sample, state_conv, cache_k_win, cache_v_win, c_prompt, c_sample,
           norm_g, w_ada, b_ada, w_ffn_gate, w_ffn_up, w_ffn_down,
           conv_w_in, conv_w, conv_w_out, kv_norm_g, w_ada_kv, b_ada_kv, w_k, w_v,
           attn_w_q, attn_sinks, attn_w_o, final_norm_g):
    f = lambda a: np.ascontiguousarray(np.asarray(a, dtype=np.float32))
    x_prompt, x_sample, state_conv = f(x_prompt), f(x_sample), f(state_conv)
    cache_k_win, cache_v_win, c_prompt, c_sample = f(cache_k_win), f(cache_v_win), f(c_prompt), f(c_sample)
    if _NC[0] is None:
        _NC[0] = build()
    nc = _NC[0]

    def fm(v):
        v = f(v)
        lead = v.shape[:-1]
        return np.moveaxis(v.reshape(lead + (8, 128)), -1, 0)

    vec_common = np.zeros((128, NV), np.float32)
    vec_common[:, NG0:NG0 + 48] = fm(norm_g).reshape(128, 48)
    vec_common[:, KVG:KVG + 8] = fm(kv_norm_g).reshape(128, 8)
    vec_common[:, FNG:FNG + 8] = fm(final_norm_g).reshape(128, 8)
    ba = f(b_ada).reshape(2, 72, 128)
    vec_common[:, BADA:BADA + 144] = np.moveaxis(ba, -1, 0).reshape(128, 144)
    vec_common[:, BKV:BKV + 16] = np.moveaxis(f(b_ada_kv).reshape(16, 128), -1, 0)
    vec_common[:, CW:CW + 24] = fm(f(conv_w)[0]).reshape(128, 24)
    sk = f(attn_sinks)[0]
    for e in range(2):
        sperm = [4 * g + 2 * c + e for g in range(4) for c in range(2)]
        vec_common[e * 64:(e + 1) * 64, SINK:SINK + 8] = np.broadcast_to(sk[sperm][None, :], (64, 8))

    ident = np.eye(128, dtype=np.float32)
    jj = np.arange(128)[:, None]
    ii = np.arange(128)[None, :]
    mA = (jj > ii).astype(np.float32)
    mB = (jj <= ii).astype(np.float32)
    mAs = (jj > (ii % 8)).astype(np.float32)
    mBs = (((jj // 8) == (ii // 8)) & ((jj % 8) <= (ii % 8))).astype(np.float32)

    shared = dict(
        ident=ident, w_ada=f(w_ada), wg=f(w_ffn_gate), wu=f(w_ffn_up), wd=f(w_ffn_down),
        cwin=f(conv_w_in)[0], cwout=f(conv_w_out)[0], wadakv=f(w_ada_kv),
        wk=f(w_k), wksw=_swap_cols(f(w_k), 4), wv=f(w_v),
        wq=f(attn_w_q)[0], wqsw=_swap_cols(f(attn_w_q)[0], 16), wo=f(attn_w_o)[0],
    )
    in_maps = []
    for c in range(NCORE):
        b, q = c // 4, c % 4
        s0 = q * 2048
        xin = np.zeros((T, D), np.float32)
        if q > 0:
            xin[0:130] = x_prompt[b, s0 - 130:s0]
        xin[130:2178] = x_prompt[b, s0:s0 + 2048]
        xin[2178:2306] = x_sample[16 * c:16 * c + 16].reshape(128, D)
        cin = np.concatenate([c_prompt[b:b + 1], c_sample[16 * c:16 * c + 16]], axis=0)
        scin = state_conv[0, 16 * c:16 * c + 16].reshape(32, D)
        pos = np.concatenate([np.arange(s0 - 128, s0 + 2048), np.tile(16384 + np.arange(8), 16)]).astype(np.int64)
        C, S = _rope_tables(pos)
        masks = np.stack([mA, mB, mA if q > 0 else np.zeros_like(mA), mB, mAs, mBs], axis=1)
        vec = vec_common.copy()
        vec[:, FLAG] = 1.0 if q > 0 else 0.0
        m = dict(shared)
        m.update(xin=xin, cin=np.ascontiguousarray(cin), scin=np.ascontiguousarray(scin),
                 kcache=np.ascontiguousarray(cache_k_win[16 * c:16 * c + 16].reshape(16, 128, 256)),
                 vcache=np.ascontiguousarray(cache_v_win[16 * c:16 * c + 16].reshape(16, 128, 256)),
                 ropec=np.ascontiguousarray(np.concatenate([C, C], axis=0)), ropes=np.ascontiguousarray(np.concatenate([S, S], axis=0)),
                 masks=np.ascontiguousarray(masks), vec=vec)
        in_maps.append(m)
    res = run_bass_kernel_spmd(nc, in_maps, core_ids=list(range(NCORE)))
    R = res.results
    y_prompt = np.stack([np.concatenate([R[4 * b + q]["y_p"] for q in range(4)], axis=0) for b in range(2)], axis=0)
    y_sample = np.concatenate([R[c]["y_s"].reshape(16, 8, D) for c in range(NCORE)], axis=0)
    conv_p = np.stack([R[3]["cso_p"], R[7]["cso_p"]], axis=0)[None]
    conv_s = np.concatenate([R[c]["cso_s"].reshape(16, 2, D) for c in range(NCORE)], axis=0)[None]
    k_p = np.stack([R[3]["kout_p"], R[7]["kout_p"]], axis=0).reshape(2, 128, 4, 64)
    v_p = np.stack([R[3]["vout_p"], R[7]["vout_p"]], axis=0).reshape(2, 128, 4, 64)
    k_s = np.concatenate([R[c]["kout_s"] for c in range(NCORE)], axis=0).reshape(128, 128, 4, 64)
    v_s = np.concatenate([R[c]["vout_s"] for c in range(NCORE)], axis=0).reshape(128, 128, 4, 64)
    outs = (y_prompt, y_sample, conv_p, conv_s, k_p, v_p, k_s, v_s)
    return tuple(np.ascontiguousarray(o.astype(np.float32)) for o in outs)
```

```python
import numpy as np
from contextlib import ExitStack
import concourse.bass as bass
import concourse.mybir as mybir
from concourse.bass_utils import run_bass_kernel_spmd

F32 = mybir.dt.float32
BF16 = mybir.dt.bfloat16
AF = mybir.ActivationFunctionType
ALU = mybir.AluOpType

D = 1024
DFF = 2816
NCORE = 8
T = 2306
TKV = 2304
SLOT = 12288
EPS = 1e-6
NG0, KVG, FNG, BADA, BKV, CW, SINK, FLAG, NV = 0, 48, 56, 64, 208, 224, 248, 264, 265


class Res:
    __slots__ = ("w", "r")

    def __init__(self):
        self.w = None
        self.r = {}


class Tracker:
    ENG = ("pe", "act", "dve", "pool", "sp")

    def __init__(self, sems):
        self.sems = sems
        self.q = {e: [] for e in self.ENG}
        self.cnt = {e: 0 for e in self.ENG}
        self.waited = {e: {} for e in self.ENG}

    def _waits(self, eng, reads, writes, strict=False):
        need = {}

        def add(ev, same_ok):
            if ev is None:
                return
            sem, val, src = ev
            if src == eng and not same_ok and not strict:
                return
            k = id(sem)
            if k not in need or need[k][1] < val:
                need[k] = (sem, val)

        for r in reads:
            add(r.w, True)
        for w in writes:
            add(w.w, False)
            for ev in w.r.values():
                add(ev, False)
        out = []
        for k, (sem, val) in need.items():
            if self.waited[eng].get(k, 0) >= val:
                continue
            self.waited[eng][k] = val
            out.append((sem, val))
        return out

    def op(self, eng, fn, reads=(), writes=()):
        waits = self._waits(eng, reads, writes, strict=(eng != "pe"))
        self.cnt[eng] += 1
        sem = self.sems[eng]
        ev = (sem, self.cnt[eng], eng)

        def run(e):
            for s, v in waits:
                e.wait_ge(s, v)
            fn(e).then_inc(sem, 1)

        self.q[eng].append(run)
        for r in reads:
            r.r[eng] = ev
        for w in writes:
            w.w = ev
            w.r = {}
        return ev

    def mm(self, mms, reads, writes):
        def fn(e):
            ins = None
            for (o, l, r, st, sp) in mms:
                ins = e.matmul(o, l, r, start=st, stop=sp)
            return ins

        return self.op("pe", fn, reads, writes)

    def dma(self, qeng, pairs, dsem, reads=(), writes=()):
        waits = self._waits(qeng, reads, writes, strict=True)
        dsem[1] += 16 * len(pairs)
        sem = dsem[0]
        ev = (sem, dsem[1], "dma")

        def run(e):
            for s, v in waits:
                e.wait_ge(s, v)
            for (o, i) in pairs:
                e.dma_start(out=o, in_=i).then_inc(sem, 16)

        self.q[qeng].append(run)
        for r in reads:
            r.r["dma" + str(id(sem))] = ev
        for w in writes:
            w.w = ev
            w.r = {}
        return ev

    def wait_only(self, eng, evs):
        ws = []
        for (sem, val, _) in evs:
            ws.append((sem, val))

        def run(e):
            for s, v in ws:
                e.wait_ge(s, v)

        self.q[eng].append(run)


class Buf:
    def __init__(self, t, mk_sem):
        self.t = t
        self.res = Res()
        self._mk = mk_sem
        self._ds = None

    @property
    def ds(self):
        if self._ds is None:
            self._ds = [self._mk(), 0]
        return self._ds

    @property
    def ds_pool(self):
        if getattr(self, "_ds2", None) is None:
            self._ds2 = [self._mk(), 0]
        return self._ds2


class Rot:
    def __init__(self, bufs):
        self.bufs = bufs
        self.i = 0

    def get(self):
        b = self.bufs[self.i % len(self.bufs)]
        self.i += 1
        return b


def build():
    nc = bass.Bass("TRN2", target_bir_lowering=False)
    es = ExitStack()
    with es:
        import os as _os0
        KDBG = _os0.environ.get("KDBG", "")
        nsem = [0]

        def mk_sem():
            nsem[0] += 1
            return es.enter_context(nc.semaphore("s%d" % nsem[0]))

        def din(name, shape):
            return nc.dram_tensor(name, shape, F32, kind="ExternalInput").ap()

        def dout(name, shape):
            return nc.dram_tensor(name, shape, F32, kind="ExternalOutput").ap()

        xin = din("xin", [T, D])
        cin = din("cin", [17, D])
        scin = din("scin", [32, D])
        kcache = din("kcache", [16, 128, 256])
        vcache = din("vcache", [16, 128, 256])
        ropec = din("ropec", [128, TKV])
        ropes = din("ropes", [128, TKV])
        masks_d = din("masks", [128, 6, 128])
        vec_d = din("vec", [128, NV])
        ident_d = din("ident", [128, 128])
        w_ada = din("w_ada", [2, D, 9 * D])
        wg_d = din("wg", [2, 2, D, DFF])
        wu_d = din("wu", [2, 2, D, DFF])
        wd_d = din("wd", [2, 2, DFF, D])
        cwin = din("cwin", [D, 3 * D])
        cwout = din("cwout", [D, D])
        wadakv = din("wadakv", [D, 2 * D])
        wk_d = din("wk", [D, 256])
        wksw_d = din("wksw", [D, 256])
        wv_d = din("wv", [D, 256])
        wq_d = din("wq", [D, D])
        wqsw_d = din("wqsw", [D, D])
        wo_d = din("wo", [D, D])

        y_p = dout("y_p", [2048, D])
        y_s = dout("y_s", [128, D])
        cso_p = dout("cso_p", [2, D])
        cso_s = dout("cso_s", [32, D])
        kout_p = dout("kout_p", [128, 256])
        vout_p = dout("vout_p", [128, 256])
        kout_s = dout("kout_s", [16, 128, 256])
        vout_s = dout("vout_s", [16, 128, 256])

        kt_scr = nc.dram_tensor("kt_scr", [4, 128, TKV], BF16).ap()
        v_scr = nc.dram_tensor("v_scr", [18, 128, 256], BF16).ap()

        def sb(name, shape, dt):
            return es.enter_context(nc.sbuf_tensor(name, shape, dt))

        def mkbuf(name, shape, dt):
            return Buf(sb(name, shape, dt), mk_sem)

        X = sb("X", [128, 8, T], F32)
        Hb = sb("Hb", [128, 8, T], BF16)
        RING = [mkbuf("ring%d" % i, [128, SLOT], BF16) for i in range(2)]
        MOD = mkbuf("MOD", [128, 72, 17], F32)
        MODKV = mkbuf("MODKV", [128, 16, 17], F32)
        SC = mkbuf("SC", [128, 8, 17], BF16)
        Ab = mkbuf("Ab", [128, 8, 17], F32)
        Gb = mkbuf("Gb", [128, 8, 17], F32)
        VEC = mkbuf("VEC", [128, NV], F32)
        IDF = mkbuf("IDF", [128, 128], F32)
        IDB = mkbuf("IDB", [128, 128], BF16)
        ONES = mkbuf("ONES", [128, 128], BF16)
        MASKS = mkbuf("MASKS", [128, 6, 128], BF16)
        ESINK = mkbuf("ESINK", [128, 8], F32)
        ACTP = Rot([mkbuf("act%d" % i, [128, 4, 512], BF16) for i in range(2)])
        T32 = Rot([mkbuf("t32_%d" % i, [128, 520], F32) for i in range(4)])
        RS = Rot([mkbuf("rs%d" % i, [128, 512], F32) for i in range(2)])
        TB = Rot([mkbuf("tb%d" % i, [128, 512], BF16) for i in range(4)])
        P2 = Rot([mkbuf("p2_%d" % i, [128, 2, 512], BF16) for i in range(4)])
        KTC = mkbuf("KTC", [128, 16, 128], BF16)
        CARRY = mkbuf("CARRY", [128, 2, 2], F32)
        CSOP = mkbuf("CSOP", [128, 8, 2], F32)
        CSOS = mkbuf("CSOS", [128, 8, 32], F32)
        SCS = mkbuf("SCS", [128, 8, 32], F32)
        KVO = Rot([mkbuf("kvo%d" % i, [128, 256], F32) for i in range(2)])

        banks = []
        for i in range(8):
            pt = es.enter_context(nc.psum_tensor("bk%d" % i, [128, 512], F32))
            banks.append((pt, Res()))
        bank_i = [0]

        def bank():
            b = banks[bank_i[0] % 6]
            bank_i[0] += 1
            return b

        nbank_i = [0]

        def nbank():
            b = banks[6 + nbank_i[0] % 2]
            nbank_i[0] += 1
            return b

        sems = {e: mk_sem() for e in Tracker.ENG}
        tr = Tracker(sems)

        TILES0 = [(0, 0, 130, "halo")] + [(1 + k, 130 + 512 * k, 512, "prompt") for k in range(4)] + [(5, 2178, 128, "sample")]
        TILES1 = TILES0[1:]
        XR = [[Res() for _ in range(8)] for _ in range(6)]
        HR = [[Res() for _ in range(8)] for _ in range(6)]
        KTSCR = [[Res() for _ in range(6)] for _ in range(4)]
        VSCR = [Res() for _ in range(6)]
        out_events = []

        def ACT(out, in_, func, reads, writes, bias=None, scale=None):
            kw = {}
            if bias is not None:
                kw["bias"] = bias
            if scale is not None:
                kw["scale"] = scale
            tr.op("act", lambda e: e.activation(out=out, in_=in_, func=func, **kw), reads, writes)

        def TT(out, in0, in1, op, reads, writes):
            tr.op("dve", lambda e: e.tensor_tensor(out=out, in0=in0, in1=in1, op=op), reads, writes)

        def PTT(out, in0, in1, op, reads, writes):
            tr.op("pool", lambda e: e.tensor_tensor(out=out, in0=in0, in1=in1, op=op), reads, writes)

        def TS(out, in0, s1, s2, op0, op1, reads, writes):
            if s2 is None:
                tr.op("dve", lambda e: e.tensor_scalar(out=out, in0=in0, scalar1=s1, scalar2=None, op0=op0), reads, writes)
            else:
                tr.op("dve", lambda e: e.tensor_scalar(out=out, in0=in0, scalar1=s1, scalar2=s2, op0=op0, op1=op1), reads, writes)

        def STT(out, in0, scalar, in1, op0, op1, reads, writes):
            tr.op("dve", lambda e: e.scalar_tensor_tensor(out=out, in0=in0, scalar=scalar, in1=in1, op0=op0, op1=op1), reads, writes)

        def CPY(eng, out, in_, reads, writes):
            if eng == "act":
                tr.op("act", lambda e: e.copy(out=out, in_=in_), reads, writes)
            else:
                tr.op("dve", lambda e: e.tensor_copy(out=out, in_=in_), reads, writes)

        def TRANSP(out, in_, ident, reads, writes):
            tr.op("pe", lambda e: e.transpose(out, in_, ident), reads, writes)

        cp_i = [0]

        def evac(out, in_, reads, writes):
            cp_i[0] += 1
            CPY("act" if cp_i[0] % 2 else "dve", out, in_, reads, writes)

        def s3(ap):
            return ap.rearrange("p (b i) -> p b i", b=16)

        def bc8(ap16):
            return ap16.unsqueeze(2).broadcast_to([128, 16, 8])

        tr.dma("sp", [(VEC.t[:, :], vec_d)], VEC.ds, writes=[VEC.res])
        tr.dma("sp", [(IDF.t[:, :], ident_d)], IDF.ds, writes=[IDF.res])
        tr.dma("pool", [(IDB.t[:, :], ident_d)], IDB.ds, writes=[IDB.res])
        tr.dma("pool", [(MASKS.t[:, :, :], masks_d)], MASKS.ds, writes=[MASKS.res])
        tr.op("dve", lambda e: e.memset(ONES.t[:, :], 1.0), [], [ONES.res])
        ACT(ESINK.t[:, :], VEC.t[:, SINK:SINK + 8], AF.Exp, [VEC.res], [ESINK.res])

        def f32view(b):
            return b.t[:, :, :].rearrange("p a b -> p (a b)").bitcast(F32)

        cb = ACTP.get()
        cv_ = f32view(cb)
        tr.dma("sp", [(cv_[0:17, :], cin)], cb.ds, writes=[cb.res])
        ACT(cv_[0:17, :], cv_[0:17, :], AF.Silu, [cb.res], [cb.res])
        bk, rk = bank()
        for j in range(8):
            TRANSP(bk[:, j * 17:(j + 1) * 17], cv_[0:17, j * 128:(j + 1) * 128], IDF.t[0:17, 0:17], [cb.res, IDF.res], [rk])
        CPY("dve", SC.t[:, :, :], bk[:, 0:136].rearrange("p (j s) -> p j s", j=8), [rk], [SC.res])
        cb = ACTP.get()
        cv_ = f32view(cb)
        tr.dma("sp", [(cv_[0:32, :], scin)], cb.ds, writes=[cb.res])
        bk, rk = bank()
        for j in range(8):
            TRANSP(bk[:, j * 32:(j + 1) * 32], cv_[0:32, j * 128:(j + 1) * 128], IDF.t[0:32, 0:32], [cb.res, IDF.res], [rk])
        CPY("dve", SCS.t[:, :, :], bk[:, 0:256].rearrange("p (j s) -> p j s", j=8), [rk], [SCS.res])
        rowtiles = [(0, 2, 0, 0), (2, 128, 0, 2)] + [(130 + 128 * k, 128, 1 + k // 4, 130 + 128 * k) for k in range(16)] + [(2178, 128, 5, 2178)]
        for (r0, nr, ti, c0) in rowtiles:
            cb = ACTP.get()
            cv_ = f32view(cb)
            tr.dma("sp", [(cv_[0:nr, :], xin[r0:r0 + nr, :])], cb.ds, writes=[cb.res])
            for half in range(2):
                bk, rk = bank()
                for jj in range(4):
                    j = half * 4 + jj
                    TRANSP(bk[:, jj * 128:jj * 128 + nr], cv_[0:nr, j * 128:(j + 1) * 128], IDF.t[0:nr, 0:nr], [cb.res, IDF.res], [rk])
                evac(X[:, half * 4:half * 4 + 4, c0:c0 + nr], bk[:, :].rearrange("p (j c) -> p j c", j=4)[:, :, 0:nr],
                     [rk], [XR[ti][half * 4 + jj] for jj in range(4)])
        dsk = [mk_sem(), 0]
        out_events.append(tr.dma("sp", [(kout_s[:, 0:120, :], kcache[:, 8:128, :]), (vout_s[:, 0:120, :], vcache[:, 8:128, :])], dsk))

        parts = []
        ring_n = [0]

        def modv(m, j):
            return MOD.t[:, m * 8 + j, :]

        def gate_acc(kind_src):
            if kind_src[0] == "G":
                return (lambda i: Gb.t[:, i, 0:1]), (lambda i: Gb.t[:, i, 1:17]), Gb.res
            m = kind_src[1]
            return (lambda i: MOD.t[:, m * 8 + i, 0:1]), (lambda i: MOD.t[:, m * 8 + i, 1:17]), MOD.res

        def resid(tile, i, bk, rk, gacc):
            ti, c0, n, kind = tile
            gp, gs, gres = gacc
            xs = X[:, i, c0:c0 + n]
            if kind != "sample":
                STT(xs, bk[:, :n], gp(i), xs, ALU.mult, ALU.add, [rk, gres, XR[ti][i]], [XR[ti][i]])
            else:
                tb_ = T32.get()
                TT(s3(tb_.t[:, 0:128]), s3(bk[:, 0:128]), bc8(gs(i)), ALU.mult, [rk, gres], [tb_.res])
                TT(xs, xs, tb_.t[:, 0:128], ALU.add, [tb_.res, XR[ti][i]], [XR[ti][i]])

        def prep_mod(l, sub, ffn):
            ng = VEC.t[:, NG0 + (l * 3 + sub) * 8: NG0 + (l * 3 + sub) * 8 + 8]
            sc = MOD.t[:, (3 * sub + 1) * 8:(3 * sub + 1) * 8 + 8, :]
            TS(Ab.t[:, :, :], sc, 1.0, None, ALU.add, None, [MOD.res], [Ab.res])
            TT(Ab.t[:, :, :], Ab.t[:, :, :], ng.unsqueeze(2).broadcast_to([128, 8, 17]), ALU.mult, [Ab.res, VEC.res], [Ab.res])
            if ffn:
                gt = MOD.t[:, (3 * sub + 2) * 8:(3 * sub + 2) * 8 + 8, :]
                TS(Gb.t[:, :, :], gt, 0.5, None, ALU.mult, None, [MOD.res], [Gb.res])

        def prep_mod_kv():
            ng = VEC.t[:, KVG:KVG + 8]
            TS(Ab.t[:, :, :], MODKV.t[:, 8:16, :], 1.0, None, ALU.add, None, [MODKV.res], [Ab.res])
            TT(Ab.t[:, :, :], Ab.t[:, :, :], ng.unsqueeze(2).broadcast_to([128, 8, 17]), ALU.mult, [Ab.res, VEC.res], [Ab.res])

        def norm_items(tiles, bview, bres, pre=None, final=False):
            items = []
            first = [True]
            for tile in tiles:
                ti, c0, n, kind = tile
                st = {}

                def Afn(tile=tile, st=st):
                    ti, c0, n, kind = tile
                    if first[0] and pre is not None:
                        pre()
                    first[0] = False
                    bk, rk = nbank()
                    for j in range(8):
                        sq = TB.get()
                        ACT(sq.t[:, :n], X[:, j, c0:c0 + n], AF.Square, [XR[ti][j]], [sq.res])
                        tr.mm([(bk[:, :n], ONES.t[:, :], sq.t[:, :n], j == 0, j == 7)], [sq.res, ONES.res], [rk])
                    st["bk"] = (bk, rk)

                def Bfn(tile=tile, st=st):
                    ti, c0, n, kind = tile
                    bk, rk = st["bk"]
                    rs = RS.get()
                    ACT(rs.t[:, :n], bk[:, :n], AF.Ln, [rk], [rs.res], bias=EPS, scale=1.0 / D)
                    ACT(rs.t[:, :n], rs.t[:, :n], AF.Exp, [rs.res], [rs.res], scale=-0.5)
                    st["rs"] = rs
                    if final:
                        return
                    for j in range(8):
                        xs = X[:, j, c0:c0 + n]
                        tmp = T32.get()
                        if kind != "sample":
                            STT(tmp.t[:, :n], xs, Ab.t[:, j, 0:1], rs.t[:, :n], ALU.mult, ALU.mult, [XR[ti][j], Ab.res, rs.res], [tmp.res])
                            ACT(Hb[:, j, c0:c0 + n], tmp.t[:, :n], AF.Identity, [tmp.res, bres], [HR[ti][j]], bias=bview(j)[:, 0:1])
                        else:
                            TT(tmp.t[:, :n], xs, rs.t[:, :n], ALU.mult, [XR[ti][j], rs.res], [tmp.res])
                            TT(s3(tmp.t[:, :n]), s3(tmp.t[:, :n]), bc8(Ab.t[:, j, 1:17]), ALU.mult, [tmp.res, Ab.res], [tmp.res])
                            TT(s3(Hb[:, j, c0:c0 + n]), s3(tmp.t[:, :n]), bc8(bview(j)[:, 1:17]), ALU.add, [tmp.res, bres], [HR[ti][j]])

                items.append((Afn, Bfn, st))
            return items

        def add_plain(items):
            parts.append(dict(dma=None, items=lambda slot: [(a, b) for (a, b, *_r) in items]))

        def ada_part(src, c0, ncol, dstbuf, ch0, bcol0):
            nch = ncol // 128

            def dma(slot):
                v = slot.t[:, 0:8 * ncol].rearrange("p (k f) -> p k f", k=8)
                return [(v, src.rearrange("(k p) f -> p k f", p=128)[:, :, c0:c0 + ncol])]

            def items(slot):
                W = slot.t[:, 0:8 * ncol].rearrange("p (k f) -> p k f", k=8)

                def Afn():
                    bk, rk = bank()
                    for oc in range(nch):
                        tr.mm([(bk[:, oc * 17:(oc + 1) * 17], W[:, kk, oc * 128:(oc + 1) * 128], SC.t[:, kk, :], kk == 0, kk == 7) for kk in range(8)],
                              [slot.res, SC.res], [rk])
                    TT(dstbuf.t[:, ch0:ch0 + nch, :], bk[:, 0:nch * 17].rearrange("p (c s) -> p c s", c=nch),
                       VEC.t[:, bcol0:bcol0 + nch].unsqueeze(2).broadcast_to([128, nch, 17]), ALU.add, [rk, VEC.res], [dstbuf.res])

                return [(Afn, None)]

            parts.append(dict(dma=dma, items=items))

        def ada_micros(src, ncols, dstbuf, ch0, bcol0):
            return [(src, c * 128, dstbuf, ch0 + c, bcol0 + c) for c in range(ncols // 128)]

        class Side:
            def __init__(self, micros, per_item):
                self.todo = list(micros)
                self.pending = []
                self.k = per_item

            def _issue(self):
                (src, col0, dstbuf, ch, bcol) = self.todo.pop(0)
                pb_ = P2.get()
                v = pb_.t[:, :, :].rearrange("p a b -> p (a b)").rearrange("p (k f) -> p k f", k=8)
                tr.dma("pool", [(v, src.rearrange("(k p) f -> p k f", p=128)[:, :, col0:col0 + 128])], pb_.ds_pool, writes=[pb_.res])
                self.pending.append((pb_, v, dstbuf, ch, bcol))

            def _compute(self):
                (pb_, v, dstbuf, ch, bcol) = self.pending.pop(0)
                bk, rk = bank()
                tr.mm([(bk[:, 0:17], v[:, kk, :], SC.t[:, kk, :], kk == 0, kk == 7) for kk in range(8)], [pb_.res, SC.res], [rk])
                TT(dstbuf.t[:, ch, :], bk[:, 0:17], VEC.t[:, bcol:bcol + 1].broadcast_to([128, 17]), ALU.add, [rk, VEC.res], [dstbuf.res])

            def step(self):
                for _ in range(self.k):
                    if self.pending:
                        self._compute()
                for _ in range(self.k):
                    if self.todo:
                        self._issue()

            def flush(self):
                while self.pending or self.todo:
                    while self.pending:
                        self._compute()
                    for _ in range(3):
                        if self.todo:
                            self._issue()

        def ffn_parts(l, w, tiles, side=None):
            f0 = 0
            while f0 < 22:
                F = min(4, 22 - f0)

                def dma(slot, f0=f0, F=F):
                    c0 = f0 * 128
                    nc_ = F * 128
                    g = slot.t[:, 0:8 * nc_].rearrange("p (k f) -> p k f", k=8)
                    u = slot.t[:, 8 * nc_:16 * nc_].rearrange("p (k f) -> p k f", k=8)
                    dd = slot.t[:, 16 * nc_:16 * nc_ + F * D].rearrange("p (f d) -> p f d", f=F)
                    return [(g, wg_d[l, w].rearrange("(k p) f -> p k f", p=128)[:, :, c0:c0 + nc_]),
                            (u, wu_d[l, w].rearrange("(k p) f -> p k f", p=128)[:, :, c0:c0 + nc_]),
                            (dd, wd_d[l, w][c0:c0 + nc_, :].rearrange("(f p) d -> p f d", p=128))]

                def items(slot, F=F):
                    nc_ = F * 128
                    Wg = slot.t[:, 0:8 * nc_].rearrange("p (k f) -> p k f", k=8)
                    Wu = slot.t[:, 8 * nc_:16 * nc_].rearrange("p (k f) -> p k f", k=8)
                    Wd = slot.t[:, 16 * nc_:16 * nc_ + F * D].rearrange("p (f d) -> p f d", f=F)
                    gacc = gate_acc(("G",))
                    its = []
                    for tile in tiles:
                        st = {}

                        def Afn(tile=tile, st=st):
                            ti, c0, n, kind = tile
                            if side is not None:
                                side.step()
                            ab = ACTP.get()
                            st["ab"] = ab
                            for f in range(F):
                                bg, rg = bank()
                                bu, ru = bank()
                                tr.mm([(bg[:, :n], Wg[:, kk, f * 128:(f + 1) * 128], Hb[:, kk, c0:c0 + n], kk == 0, kk == 7) for kk in range(8)],
                                      [slot.res] + HR[ti], [rg])
                                tr.mm([(bu[:, :n], Wu[:, kk, f * 128:(f + 1) * 128], Hb[:, kk, c0:c0 + n], kk == 0, kk == 7) for kk in range(8)],
                                      [slot.res] + HR[ti], [ru])
                                sg = T32.get()
                                ACT(sg.t[:, :n], bg[:, :n], AF.Silu, [rg], [sg.res])
                                TT(ab.t[:, f, :n], sg.t[:, :n], bu[:, :n], ALU.mult, [sg.res, ru], [ab.res])

                        def Bfn(tile=tile, st=st):
                            ti, c0, n, kind = tile
                            ab = st["ab"]
                            for i in range(8):
                                bd, rd = bank()
                                tr.mm([(bd[:, :n], Wd[:, f, i * 128:(i + 1) * 128], ab.t[:, f, :n], f == 0, f == F - 1) for f in range(F)],
                                      [slot.res, ab.res], [rd])
                                resid(tile, i, bd, rd, gacc)

                        its.append((Afn, Bfn))
                    return its

                parts.append(dict(dma=dma, items=items))
                f0 += F

        def conv_parts(tiles):
            for p in range(4):
                def dma(slot, p=p):
                    prs = []
                    for q in range(3):
                        v = slot.t[:, q * 2048:(q + 1) * 2048].rearrange("p (k f) -> p k f", k=8)
                        prs.append((v, cwin.rearrange("(k p) f -> p k f", p=128)[:, :, q * D + p * 256:q * D + p * 256 + 256]))
                    v = slot.t[:, 6144:8192].rearrange("p (c d) -> p c d", c=2)
                    prs.append((v, cwout[p * 256:(p + 1) * 256, :].rearrange("(c p) d -> p c d", p=128)))
                    return prs

                def items(slot, p=p):
                    Wb = slot.t[:, 0:2048].rearrange("p (k f) -> p k f", k=8)
                    Wc = slot.t[:, 2048:4096].rearrange("p (k f) -> p k f", k=8)
                    Wv_ = slot.t[:, 4096:6144].rearrange("p (k f) -> p k f", k=8)
                    Wo_ = slot.t[:, 6144:8192].rearrange("p (c d) -> p c d", c=2)
                    gacc = gate_acc(("M", 5))
                    its = []
                    for tile in tiles:
                        st = {}

                        def Afn(tile=tile, st=st):
                            ti, c0, n, kind = tile
                            zb = P2.get()
                            st["zb"] = zb
                            if kind == "halo":
                                tr.op("dve", lambda e: e.memset(CARRY.t[:, :, :], 0.0), [], [CARRY.res])
                            for jl in range(2):
                                j = 2 * p + jl
                                bb, rb = bank()
                                bc, rc = bank()
                                bv, rv = bank()
                                for (bkk, rkk, W) in ((bc, rc, Wc), (bv, rv, Wv_), (bb, rb, Wb)):
                                    tr.mm([(bkk[:, :n], W[:, kk, jl * 128:(jl + 1) * 128], Hb[:, kk, c0:c0 + n], kk == 0, kk == 7) for kk in range(8)],
                                          [slot.res] + HR[ti], [rkk])
                                csb = T32.get()
                                ACT(csb.t[:, :n], bc[:, :n], AF.Identity, [rc], [csb.res])
                                U = T32.get()
                                cvb = T32.get()
                                w0 = VEC.t[:, CW + j:CW + j + 1]
                                w1 = VEC.t[:, CW + 8 + j:CW + 8 + j + 1]
                                w2 = VEC.t[:, CW + 16 + j:CW + 16 + j + 1]
                                if kind != "sample":
                                    CPY("dve", U.t[:, 0:2], CARRY.t[:, jl, :], [CARRY.res], [U.res])
                                    TT(U.t[:, 2:2 + n], csb.t[:, :n], bv[:, :n], ALU.mult, [csb.res, rv], [U.res])
                                    if kind == "halo":
                                        TS(U.t[:, 2:2 + n], U.t[:, 2:2 + n], VEC.t[:, FLAG:FLAG + 1], None, ALU.mult, None, [U.res, VEC.res], [U.res])
                                    CPY("dve", CARRY.t[:, jl, :], U.t[:, n:n + 2], [U.res], [CARRY.res])
                                    if ti == 4:
                                        CPY("dve", CSOP.t[:, j, :], U.t[:, n:n + 2], [U.res], [CSOP.res])
                                    TS(cvb.t[:, :n], U.t[:, 0:n], w0, None, ALU.mult, None, [U.res, VEC.res], [cvb.res])
                                    STT(cvb.t[:, :n], U.t[:, 1:n + 1], w1, cvb.t[:, :n], ALU.mult, ALU.add, [U.res, cvb.res], [cvb.res])
                                    STT(cvb.t[:, :n], U.t[:, 2:n + 2], w2, cvb.t[:, :n], ALU.mult, ALU.add, [U.res, cvb.res], [cvb.res])
                                    TT(zb.t[:, jl, :n], bb[:, :n], cvb.t[:, :n], ALU.mult, [rb, cvb.res], [zb.res])
                                else:
                                    U3 = U.t[:, 0:160].rearrange("p (b i) -> p b i", b=16)
                                    CPY("dve", U3[:, :, 0:2], SCS.t[:, j, :].rearrange("p (b i) -> p b i", b=16), [SCS.res], [U.res])
                                    TT(U3[:, :, 2:10], s3(csb.t[:, :n]), s3(bv[:, :n]), ALU.mult, [csb.res, rv], [U.res])
                                    CPY("dve", CSOS.t[:, j, :].rearrange("p (b i) -> p b i", b=16), U3[:, :, 8:10], [U.res], [CSOS.res])
                                    c3 = s3(cvb.t[:, :n])
                                    TS(c3, U3[:, :, 0:8], w0, None, ALU.mult, None, [U.res, VEC.res], [cvb.res])
                                    STT(c3, U3[:, :, 1:9], w1, c3, ALU.mult, ALU.add, [U.res, cvb.res], [cvb.res])
                                    STT(c3, U3[:, :, 2:10], w2, c3, ALU.mult, ALU.add, [U.res, cvb.res], [cvb.res])
                                    TT(zb.t[:, jl, :n], bb[:, :n], cvb.t[:, :n], ALU.mult, [rb, cvb.res], [zb.res])

                        def Bfn(tile=tile, st=st):
                            ti, c0, n, kind = tile
                            zb = st["zb"]
                            for i in range(8):
                                bd, rd = bank()
                                tr.mm([(bd[:, :n], Wo_[:, c, i * 128:(i + 1) * 128], zb.t[:, c, :n], c == 0, c == 1) for c in range(2)],
                                      [slot.res, zb.res], [rd])
                                resid(tile, i, bd, rd, gacc)

                        its.append((Afn, Bfn))
                    return its

                parts.append(dict(dma=dma, items=items))

        KVT = [(0, 2, 128, "halo")] + TILES1

        def kv_part():
            def dma(slot):
                prs = []
                for (off, src) in ((0, wk_d), (4096, wksw_d)):
                    v = slot.t[:, off:off + 4096].rearrange("p (k g e d) -> p k g e d", k=8, g=4, e=2)
                    s = src.rearrange("(k p) (g d) -> p k g d", p=128, g=4)
                    for e in range(2):
                        for g_ in range(4):
                            prs.append((v[:, :, g_, e, :], s[:, :, g_, :]))
                v = slot.t[:, 8192:8192 + 2048].rearrange("p (k d) -> p k d", k=8)
                prs.append((v, wv_d.rearrange("(k p) d -> p k d", p=128)))
                return prs

            def items(slot):
                Wk = slot.t[:, 0:4096].rearrange("p (k g m) -> p k g m", k=8, g=4)
                Wks = slot.t[:, 4096:8192].rearrange("p (k g m) -> p k g m", k=8, g=4)
                Wv_ = slot.t[:, 8192:8192 + 2048].rearrange("p (k d) -> p k d", k=8)
                its = []
                for tile in KVT:
                    def Afn(tile=tile):
                        ti, c0, n, kind = tile
                        rp = ACTP.get()
                        rv_ = f32view(rp).rearrange("p (a b) -> p a b", a=2)
                        tr.dma("sp", [(rv_[:, 0, 0:n], ropec[:, c0 - 2:c0 - 2 + n]), (rv_[:, 1, 0:n], ropes[:, c0 - 2:c0 - 2 + n])], rp.ds, writes=[rp.res])
                        is_out = (ti == 4) or (ti == 5)
                        for g in range(4):
                            bk_, rk_ = bank()
                            bs_, rs_ = bank()
                            tr.mm([(bk_[:, :n], Wk[:, kk, g, :], Hb[:, kk, c0:c0 + n], kk == 0, kk == 7) for kk in range(8)], [slot.res] + HR[ti], [rk_])
                            tr.mm([(bs_[:, :n], Wks[:, kk, g, :], Hb[:, kk, c0:c0 + n], kk == 0, kk == 7) for kk in range(8)], [slot.res] + HR[ti], [rs_])
                            t1 = T32.get()
                            t2 = T32.get()
                            TT(t1.t[:, :n], bk_[:, :n], rv_[:, 0, 0:n], ALU.mult, [rk_, rp.res], [t1.res])
                            TT(t2.t[:, :n], bs_[:, :n], rv_[:, 1, 0:n], ALU.mult, [rs_, rp.res], [t2.res])
                            TT(t1.t[:, :n], t1.t[:, :n], t2.t[:, :n], ALU.add, [t1.res, t2.res], [t1.res])
                            kb = TB.get()
                            ACT(kb.t[:, :n], t1.t[:, :n], AF.Identity, [t1.res], [kb.res])
                            if "a" not in KDBG:
                                tr.dma("sp", [(kt_scr[g, :, c0 - 2:c0 - 2 + n], kb.t[:, :n])], kb.ds, reads=[kb.res], writes=[KTSCR[g][ti]])
                            if is_out and "b" not in KDBG:
                                bt_, rt_ = bank()
                                TRANSP(bt_[:, 0:128], t1.t[:, n - 128:n], IDF.t[:, :], [t1.res, IDF.res], [rt_])
                                if ti not in st_k:
                                    st_k[ti] = KVO.get()
                                ko = st_k[ti]
                                evac(ko.t[:, g * 64:(g + 1) * 64], bt_[:, 0:64], [rt_], [ko.res])
                        if is_out and "b" not in KDBG:
                            ko = st_k[ti]
                            if ti == 4:
                                out_events.append(tr.dma("sp", [(kout_p, ko.t[:, :])], ko.ds, reads=[ko.res]))
                            else:
                                out_events.append(tr.dma("sp", [(kout_s[b, 120:128, :], ko.t[b * 8:(b + 1) * 8, :]) for b in range(16)], ko.ds, reads=[ko.res]))

                    def Bfn(tile=tile):
                        ti, c0, n, kind = tile
                        if "c" in KDBG:
                            return
                        nb = n // 128
                        vb = P2.get()
                        vv = vb.t[:, :, :].rearrange("p a b -> p (a b)").rearrange("p (b d) -> p b d", b=4)
                        is_out = (ti == 4) or (ti == 5)
                        for b_ in range(nb):
                            bk_, rk_ = bank()
                            tr.mm([(bk_[:, 0:256], Hb[:, kk, c0 + b_ * 128:c0 + (b_ + 1) * 128], Wv_[:, kk, :], kk == 0, kk == 7) for kk in range(8)],
                                  [slot.res] + HR[ti], [rk_])
                            if not (is_out and b_ == nb - 1):
                                evac(vv[:, b_, :], bk_[:, 0:256], [rk_], [vb.res])
                            else:
                                vo = KVO.get()
                                CPY("dve", vo.t[:, :], bk_[:, 0:256], [rk_], [vo.res])
                                CPY("act", vv[:, b_, :], vo.t[:, :], [vo.res], [vb.res])
                                if ti == 4:
                                    out_events.append(tr.dma("sp", [(vout_p, vo.t[:, :])], vo.ds, reads=[vo.res]))
                                else:
                                    out_events.append(tr.dma("sp", [(vout_s[b, 120:128, :], vo.t[b * 8:(b + 1) * 8, :]) for b in range(16)], vo.ds, reads=[vo.res]))
                        blk0 = (c0 - 2) // 128
                        if "d" not in KDBG:
                            tr.dma("sp", [(v_scr[blk0:blk0 + nb, :, :].rearrange("b p d -> p b d"), vv[:, 0:nb, :])], vb.ds, reads=[vb.res], writes=[VSCR[ti]])

                    its.append((Afn, Bfn))
                return its

            parts.append(dict(dma=dma, items=items))

        st_k = {}

        def attn_parts(tiles):
            for g in range(4):
                def dma(slot, g=g):
                    prs = []
                    prs.append((slot.t[:, 0:2048].rearrange("p (k f) -> p k f", k=8), wq_d.rearrange("(k p) f -> p k f", p=128)[:, :, g * 256:(g + 1) * 256]))
                    prs.append((slot.t[:, 2048:4096].rearrange("p (k f) -> p k f", k=8), wqsw_d.rearrange("(k p) f -> p k f", p=128)[:, :, g * 256:(g + 1) * 256]))
                    prs.append((slot.t[:, 4096:6144].rearrange("p (c d) -> p c d", c=2), wo_d[g * 256:(g + 1) * 256, :].rearrange("(c p) d -> p c d", p=128)))
                    prs.append((slot.t[:, 6144:6144 + TKV], kt_scr[g, :, :]))
                    prs.append((slot.t[:, 8448:8448 + 1152].rearrange("p (b d) -> p b d", b=18), v_scr[:, :, g * 64:(g + 1) * 64].rearrange("b p d -> p b d")))
                    prs.append((slot.t[:, 9600:9600 + 1024].rearrange("p (b d) -> p b d", b=16), kcache[:, :, g * 64:(g + 1) * 64].rearrange("b p d -> p b d")))
                    prs.append((slot.t[:, 10624:10624 + 1024].rearrange("p (b d) -> p b d", b=16), vcache[:, :, g * 64:(g + 1) * 64].rearrange("b p d -> p b d")))
                    return prs

                def items(slot, g=g):
                    Wq = slot.t[:, 0:2048].rearrange("p (k f) -> p k f", k=8)
                    Wqs = slot.t[:, 2048:4096].rearrange("p (k f) -> p k f", k=8)
                    Wo_ = slot.t[:, 4096:6144].rearrange("p (c d) -> p c d", c=2)
                    KT = slot.t[:, 6144:6144 + TKV]
                    Vg = slot.t[:, 8448:8448 + 1152].rearrange("p (b d) -> p b d", b=18)
                    kc = slot.t[:, 9600:9600 + 1024].rearrange("p (b d) -> p b d", b=16)
                    vc = slot.t[:, 10624:10624 + 1024].rearrange("p (b d) -> p b d", b=16)
                    gacc = gate_acc(("M", 5))
                    esk = ESINK.t[:, g * 2:(g + 1) * 2].unsqueeze(2).broadcast_to([128, 2, 128])
                    its = []

                    def Pfn():
                        for q4 in range(4):
                            bk_, rk_ = bank()
                            for bl in range(4):
                                b = q4 * 4 + bl
                                for e in range(2):
                                    tr.mm([(bk_[e * 64:(e + 1) * 64, bl * 128:(bl + 1) * 128], kc[:, b, :], IDB.t[:, :], True, True)], [slot.res, IDB.res], [rk_])
                            evac(KTC.t[:, q4 * 4:(q4 + 1) * 4, :], bk_[:, :].rearrange("p (b k) -> p b k", b=4), [rk_], [KTC.res])

                    for tile in tiles:
                        st = {}

                        def Afn(tile=tile, st=st):
                            ti, c0, n, kind = tile
                            if kind == "sample":
                                Pfn()
                            rp = ACTP.get()
                            rv_ = f32view(rp).rearrange("p (a b) -> p a b", a=2)
                            tr.dma("sp", [(rv_[:, 0, 0:n], ropec[:, c0 - 2:c0 - 2 + n]), (rv_[:, 1, 0:n], ropes[:, c0 - 2:c0 - 2 + n])], rp.ds, writes=[rp.res])
                            qt = P2.get()
                            st["qt"] = qt
                            for c in range(2):
                                bq, rq = bank()
                                bs_, rs_ = bank()
                                tr.mm([(bq[:, :n], Wq[:, kk, c * 128:(c + 1) * 128], Hb[:, kk, c0:c0 + n], kk == 0, kk == 7) for kk in range(8)], [slot.res] + HR[ti], [rq])
                                tr.mm([(bs_[:, :n], Wqs[:, kk, c * 128:(c + 1) * 128], Hb[:, kk, c0:c0 + n], kk == 0, kk == 7) for kk in range(8)], [slot.res] + HR[ti], [rs_])
                                t1 = T32.get()
                                t2 = T32.get()
                                TT(t1.t[:, :n], bq[:, :n], rv_[:, 0, 0:n], ALU.mult, [rq, rp.res], [t1.res])
                                TT(t2.t[:, :n], bs_[:, :n], rv_[:, 1, 0:n], ALU.mult, [rs_, rp.res], [t2.res])
                                PTT(qt.t[:, c, :n], t1.t[:, :n], t2.t[:, :n], ALU.add, [t1.res, t2.res], [qt.res])

                        def Bfn(tile=tile, st=st):
                            ti, c0, n, kind = tile
                            if "q" in KDBG or ("s" in KDBG and kind == "sample") or ("m" in KDBG and kind != "sample"):
                                return
                            qt = st["qt"]
                            ot = P2.get()
                            bst = {}

                            def stage1(blk):
                                q0 = blk * 128
                                kown = c0 - 2 + q0
                                be = [bank(), bank()]
                                pe_ = [TB.get(), TB.get()]
                                mlist = []
                                if kind != "sample":
                                    for (c_lo, k_lo) in ((0, kown - 128), (256, kown)):
                                        for e in range(2):
                                            es_ = slice(e * 64, (e + 1) * 64)
                                            mlist.append((be[e][0][:, c_lo:c_lo + 256].rearrange("p (c q) -> p c q", c=2), KT[es_, k_lo:k_lo + 128], qt.t[es_, :, q0:q0 + 128], True, True))
                                else:
                                    for b in range(16):
                                        for c_ in range(2):
                                            for e in range(2):
                                                es_ = slice(e * 64, (e + 1) * 64)
                                                o = be[e][0][:, c_ * 128 + b * 8:c_ * 128 + b * 8 + 8]
                                                mlist.append((o, KTC.t[es_, b, :], qt.t[es_, c_, b * 8:(b + 1) * 8], True, True))
                                    for e in range(2):
                                        es_ = slice(e * 64, (e + 1) * 64)
                                        mlist.append((be[e][0][:, 256:512].rearrange("p (c q) -> p c q", c=2), KT[es_, kown:kown + 128], qt.t[es_, :, q0:q0 + 128], True, True))
                                tr.mm(mlist, [slot.res, KTC.res, qt.res], [be[0][1], be[1][1]])
                                if kind != "sample":
                                    mi = 2 if (ti == 1 and blk == 0) else 0
                                else:
                                    mi = 4
                                msk = MASKS.t[:, mi:mi + 2, :].unsqueeze(2).broadcast_to([128, 2, 2, 128])
                                for e in range(2):
                                    bke, rke = be[e]
                                    ACT(pe_[e].t[:, :], bke[:, :], AF.Exp, [rke], [pe_[e].res], scale=0.125)
                                    p4 = pe_[e].t[:, :].rearrange("p (t c q) -> p t c q", t=2, c=2)
                                    (TT if e == 0 else PTT)(p4, p4, msk, ALU.mult, [pe_[e].res, MASKS.res], [pe_[e].res])
                                bst[blk] = pe_

                            def stage2(blk):
                                q0 = blk * 128
                                kown = c0 - 2 + q0
                                vown = kown // 128
                                pe_ = bst.pop(blk)
                                bd_, rd_ = bank()
                                tr.mm([(bd_[e * 64:(e + 1) * 64, 0:256], ONES.t[:, 0:64], pe_[e].t[:, t_ * 256:(t_ + 1) * 256], t_ == 0, t_ == 1) for t_ in range(2) for e in range(2)],
                                      [pe_[0].res, pe_[1].res, ONES.res], [rd_])
                                rden = RS.get()
                                r3 = rden.t[:, 0:256].rearrange("p (h q) -> p h q", h=2)
                                TT(r3, bd_[:, 0:256].rearrange("p (h q) -> p h q", h=2), esk, ALU.add, [rd_, ESINK.res], [rden.res])
                                tr.op("dve", lambda e, rden=rden: e.reciprocal(out=rden.t[:, 0:256], in_=rden.t[:, 0:256]), [rden.res], [rden.res])
                                bo, ro = bank()
                                mlist = []
                                if kind != "sample":
                                    for (vb_, lo, st_, sp_) in ((vown - 1, 0, True, False), (vown, 256, False, True)):
                                        for e in range(2):
                                            es_ = slice(e * 64, (e + 1) * 64)
                                            mlist.append((bo[es_, 0:256], Vg[:, vb_, :], pe_[e].t[:, lo:lo + 256], st_, sp_))
                                else:
                                    for e in range(2):
                                        es_ = slice(e * 64, (e + 1) * 64)
                                        mlist.append((bo[es_, 0:256], Vg[:, vown, :], pe_[e].t[:, 256:512], True, False))
                                    for b in range(16):
                                        for c_ in range(2):
                                            for e in range(2):
                                                es_ = slice(e * 64, (e + 1) * 64)
                                                o = bo[es_, c_ * 128 + b * 8:c_ * 128 + b * 8 + 8]
                                                r_ = pe_[e].t[:, c_ * 128 + b * 8:c_ * 128 + b * 8 + 8]
                                                mlist.append((o, vc[:, b, :], r_, False, (b == 15 and c_ == 1)))
                                tr.mm(mlist, [slot.res, pe_[0].res, pe_[1].res], [ro])
                                TT(ot.t[:, :, q0:q0 + 128], bo[:, 0:256].rearrange("p (c q) -> p c q", c=2),
                                   rden.t[:, 0:256].rearrange("p (c q) -> p c q", c=2), ALU.mult, [ro, rden.res], [ot.res])

                            nblk = n // 128
                            for blk in range(nblk):
                                stage1(blk)
                                if blk >= 1:
                                    stage2(blk - 1)
                            stage2(nblk - 1)
                            for i in range(8):
                                bd, rd = bank()
                                tr.mm([(bd[:, :n], Wo_[:, c, i * 128:(i + 1) * 128], ot.t[:, c, :n], c == 0, c == 1) for c in range(2)], [slot.res, ot.res], [rd])
                                resid(tile, i, bd, rd, gacc)

                        its.append((Afn, Bfn))
                    return its

                parts.append(dict(dma=dma, items=items, extra=[KTSCR[g][t_] for t_ in range(6)] + VSCR))

        def final_items(tiles):
            base = norm_items(tiles, None, None, final=True)
            its = []
            for (Afn, Bfn, st), tile in zip(base, tiles):
                def B2(tile=tile, st=st, Bfn=Bfn):
                    ti, c0, n, kind = tile
                    Bfn()
                    rs = st["rs"]
                    for blk in range(n // 128):
                        q0 = blk * 128
                        sg_ = ACTP.get()
                        sv_ = f32view(sg_)
                        for half in range(2):
                            bk_, rk_ = bank()
                            for jj in range(4):
                                j = half * 4 + jj
                                yt = T32.get()
                                STT(yt.t[:, 0:128], X[:, j, c0 + q0:c0 + q0 + 128], VEC.t[:, FNG + j:FNG + j + 1], rs.t[:, q0:q0 + 128], ALU.mult, ALU.mult,
                                    [XR[ti][j], VEC.res, rs.res], [yt.res])
                                TRANSP(bk_[:, jj * 128:(jj + 1) * 128], yt.t[:, 0:128], IDF.t[:, :], [yt.res, IDF.res], [rk_])
                            evac(sv_[:, half * 512:(half + 1) * 512], bk_[:, :], [rk_], [sg_.res])
                        if kind == "sample":
                            dst = y_s
                        else:
                            r0 = c0 - 130 + q0
                            dst = y_p[r0:r0 + 128, :]
                        out_events.append(tr.dma("sp", [(dst, sv_[:, :])], sg_.ds, reads=[sg_.res]))

                its.append((Afn, B2))
            parts.append(dict(dma=None, items=lambda slot: its))

        def cso_out():
            def Afn():
                sg_ = ACTP.get()
                sv_ = f32view(sg_)
                for half in range(2):
                    bk_, rk_ = bank()
                    for jj in range(4):
                        TRANSP(bk_[0:32, jj * 128:(jj + 1) * 128], CSOS.t[:, half * 4 + jj, :], IDF.t[:, :], [CSOS.res, IDF.res], [rk_])
                    evac(sv_[0:32, half * 512:(half + 1) * 512], bk_[0:32, :], [rk_], [sg_.res])
                out_events.append(tr.dma("sp", [(cso_s, sv_[0:32, :])], sg_.ds, reads=[sg_.res]))
                sg2 = ACTP.get()
                sv2 = f32view(sg2)
                for half in range(2):
                    bk3, rk3 = bank()
                    for jj in range(4):
                        TRANSP(bk3[0:2, jj * 128:(jj + 1) * 128], CSOP.t[:, half * 4 + jj, :], IDF.t[:, :], [CSOP.res, IDF.res], [rk3])
                    evac(sv2[0:2, half * 512:(half + 1) * 512], bk3[0:2, :], [rk3], [sg2.res])
                out_events.append(tr.dma("sp", [(cso_p, sv2[0:2, :])], sg2.ds, reads=[sg2.res]))

            parts.append(dict(dma=None, items=lambda slot: [(Afn, None)]))

        side_l0 = Side(ada_micros(w_ada[0], 9 * D, MOD, 0, BADA)[24:], 2)
        side_l1 = Side(ada_micros(wadakv, 2 * D, MODKV, 0, BKV) + ada_micros(w_ada[1], 9 * D, MOD, 0, BADA + 72), 3)

        def flush_part(sd):
            parts.append(dict(dma=None, items=lambda slot: [(sd.flush, None)]))

        ada_part(w_ada[0], 0, 1536, MOD, 0, BADA + 0)
        ada_part(w_ada[0], 1536, 1536, MOD, 12, BADA + 12)
        add_plain(norm_items(TILES0, lambda j: modv(0, j), MOD.res, pre=lambda: prep_mod(0, 0, True)))
        ffn_parts(0, 0, TILES0, side=side_l0)
        flush_part(side_l0)
        add_plain(norm_items(TILES0, lambda j: modv(3, j), MOD.res, pre=lambda: prep_mod(0, 1, False)))
        conv_parts(TILES0)
        cso_out()
        add_plain(norm_items(TILES0, lambda j: modv(6, j), MOD.res, pre=lambda: prep_mod(0, 2, True)))
        ffn_parts(0, 1, TILES0, side=side_l1)
        flush_part(side_l1)
        add_plain(norm_items(TILES0, lambda j: MODKV.t[:, j, :], MODKV.res, pre=prep_mod_kv))
        kv_part()
        add_plain(norm_items(TILES1, lambda j: modv(0, j), MOD.res, pre=lambda: prep_mod(1, 0, True)))
        ffn_parts(1, 0, TILES1)
        add_plain(norm_items(TILES1, lambda j: modv(3, j), MOD.res, pre=lambda: prep_mod(1, 1, False)))
        attn_parts(TILES1)
        add_plain(norm_items(TILES1, lambda j: modv(6, j), MOD.res, pre=lambda: prep_mod(1, 2, True)))
        ffn_parts(1, 1, TILES1)
        final_items(TILES1)

        import os as _os
        _ks = _os.environ.get("KSTOP")
        if _ks is not None:
            parts = parts[:int(_ks)]
        dparts = [p for p in parts if p["dma"] is not None]
        for k, p in enumerate(dparts):
            p["slot"] = RING[k % 2]
            p["didx"] = k

        def issue(k):
            if k < len(dparts):
                p = dparts[k]
                slot = p["slot"]
                tr.dma("pool", p["dma"](slot), slot.ds, reads=p.get("extra", []), writes=[slot.res])

        issue(0)
        prevB = [None]
        for p in parts:
            slot = p.get("slot")
            its = p["items"](slot)
            for n_, it in enumerate(its):
                a, b = it[0], it[1]
                if a:
                    a()
                if prevB[0]:
                    prevB[0]()
                prevB[0] = b
                if n_ == 0 and p["dma"] is not None:
                    issue(p["didx"] + 1)
        if prevB[0]:
            prevB[0]()

        tr.wait_only("sp", out_events)

        block = es.enter_context(nc.Block())

        @block.tensor
        def _(e):
            for f in tr.q["pe"]:
                f(e)

        @block.scalar
        def _(e):
            for f in tr.q["act"]:
                f(e)

        @block.vector
        def _(e):
            for f in tr.q["dve"]:
                f(e)

        @block.gpsimd
        def _(e):
            for f in tr.q["pool"]:
                f(e)

        @block.sync
        def _(e):
            for f in tr.q["sp"]:
                f(e)
    return nc


_NC = [None]


def _rope_tables(pos):
    inv = (np.float32(500000.0) ** (-np.arange(0, 16, 2, dtype=np.float32) / np.float32(16))).astype(np.float32)
    ang = pos.astype(np.float32)[None, :] * inv[:, None]
    cos = np.cos(ang).astype(np.float32)
    sin = np.sin(ang).astype(np.float32)
    n = pos.shape[0]
    C = np.ones((64, n), np.float32)
    S = np.zeros((64, n), np.float32)
    C[0:8] = cos
    C[8:16] = cos
    S[0:8] = -sin
    S[8:16] = sin
    return C, S


def _swap_cols(w, nheads):
    perm = []
    for h in range(nheads):
        base = h * 64
        perm += [base + 8 + d for d in range(8)] + [base + d for d in range(8)] + [base + d for d in range(16, 64)]
    return np.ascontiguousarray(w[:, perm])


def kernel(x_prompt, x_sample, state_conv, cache_k_win, cache_v_win, c_prompt, c_sample,
           norm_g, w_ada, b_ada, w_ffn_gate, w_ffn_up, w_ffn_down,
           conv_w_in, conv_w, conv_w_out, kv_norm_g, w_ada_kv, b_ada_kv, w_k, w_v,
           attn_w_q, attn_sinks, attn_w_o, final_norm_g):
    f = lambda a: np.ascontiguousarray(np.asarray(a, dtype=np.float32))
    x_prompt, x_sample, state_conv = f(x_prompt), f(x_sample), f(state_conv)
    cache_k_win, cache_v_win, c_prompt, c_sample = f(cache_k_win), f(cache_v_win), f(c_prompt), f(c_sample)
    if _NC[0] is None:
        _NC[0] = build()
    nc = _NC[0]

    def fm(v):
        v = f(v)
        lead = v.shape[:-1]
        return np.moveaxis(v.reshape(lead + (8, 128)), -1, 0)

    vec_common = np.zeros((128, NV), np.float32)
    vec_common[:, NG0:NG0 + 48] = fm(norm_g).reshape(128, 48)
    vec_common[:, KVG:KVG + 8] = fm(kv_norm_g).reshape(128, 8)
    vec_common[:, FNG:FNG + 8] = fm(final_norm_g).reshape(128, 8)
    ba = f(b_ada).reshape(2, 72, 128)
    vec_common[:, BADA:BADA + 144] = np.moveaxis(ba, -1, 0).reshape(128, 144)
    vec_common[:, BKV:BKV + 16] = np.moveaxis(f(b_ada_kv).reshape(16, 128), -1, 0)
    vec_common[:, CW:CW + 24] = fm(f(conv_w)[0]).reshape(128, 24)
    sk = f(attn_sinks)[0]
    for e in range(2):
        sperm = [4 * g + 2 * c + e for g in range(4) for c in range(2)]
        vec_common[e * 64:(e + 1) * 64, SINK:SINK + 8] = np.broadcast_to(sk[sperm][None, :], (64, 8))

    ident = np.eye(128, dtype=np.float32)
    jj = np.arange(128)[:, None]
    ii = np.arange(128)[None, :]
    mA = (jj > ii).astype(np.float32)
    mB = (jj <= ii).astype(np.float32)
    mAs = (jj > (ii % 8)).astype(np.float32)
    mBs = (((jj // 8) == (ii // 8)) & ((jj % 8) <= (ii % 8))).astype(np.float32)

    shared = dict(
        ident=ident, w_ada=f(w_ada), wg=f(w_ffn_gate), wu=f(w_ffn_up), wd=f(w_ffn_down),
        cwin=f(conv_w_in)[0], cwout=f(conv_w_out)[0], wadakv=f(w_ada_kv),
        wk=f(w_k), wksw=_swap_cols(f(w_k), 4), wv=f(w_v),
        wq=f(attn_w_q)[0], wqsw=_swap_cols(f(attn_w_q)[0], 16), wo=f(attn_w_o)[0],
    )
    in_maps = []
    for c in range(NCORE):
        b, q = c // 4, c % 4
        s0 = q * 2048
        xin = np.zeros((T, D), np.float32)
        if q > 0:
            xin[0:130] = x_prompt[b, s0 - 130:s0]
        xin[130:2178] = x_prompt[b, s0:s0 + 2048]
        xin[2178:2306] = x_sample[16 * c:16 * c + 16].reshape(128, D)
        cin = np.concatenate([c_prompt[b:b + 1], c_sample[16 * c:16 * c + 16]], axis=0)
        scin = state_conv[0, 16 * c:16 * c + 16].reshape(32, D)
        pos = np.concatenate([np.arange(s0 - 128, s0 + 2048), np.tile(16384 + np.arange(8), 16)]).astype(np.int64)
        C, S = _rope_tables(pos)
        masks = np.stack([mA, mB, mA if q > 0 else np.zeros_like(mA), mB, mAs, mBs], axis=1)
        vec = vec_common.copy()
        vec[:, FLAG] = 1.0 if q > 0 else 0.0
        m = dict(shared)
        m.update(xin=xin, cin=np.ascontiguousarray(cin), scin=np.ascontiguousarray(scin),
                 kcache=np.ascontiguousarray(cache_k_win[16 * c:16 * c + 16].reshape(16, 128, 256)),
                 vcache=np.ascontiguousarray(cache_v_win[16 * c:16 * c + 16].reshape(16, 128, 256)),
                 ropec=np.ascontiguousarray(np.concatenate([C, C], axis=0)), ropes=np.ascontiguousarray(np.concatenate([S, S], axis=0)),
                 masks=np.ascontiguousarray(masks), vec=vec)
        in_maps.append(m)
    res = run_bass_kernel_spmd(nc, in_maps, core_ids=list(range(NCORE)))
    R = res.results
    y_prompt = np.stack([np.concatenate([R[4 * b + q]["y_p"] for q in range(4)], axis=0) for b in range(2)], axis=0)
    y_sample = np.concatenate([R[c]["y_s"].reshape(16, 8, D) for c in range(NCORE)], axis=0)
    conv_p = np.stack([R[3]["cso_p"], R[7]["cso_p"]], axis=0)[None]
    conv_s = np.concatenate([R[c]["cso_s"].reshape(16, 2, D) for c in range(NCORE)], axis=0)[None]
    k_p = np.stack([R[3]["kout_p"], R[7]["kout_p"]], axis=0).reshape(2, 128, 4, 64)
    v_p = np.stack([R[3]["vout_p"], R[7]["vout_p"]], axis=0).reshape(2, 128, 4, 64)
    k_s = np.concatenate([R[c]["kout_s"] for c in range(NCORE)], axis=0).reshape(128, 128, 4, 64)
    v_s = np.concatenate([R[c]["vout_s"] for c in range(NCORE)], axis=0).reshape(128, 128, 4, 64)
    outs = (y_prompt, y_sample, conv_p, conv_s, k_p, v_p, k_s, v_s)
    return tuple(np.ascontiguousarray(o.astype(np.float32)) for o in outs)
```

```python
import numpy as np
from contextlib import ExitStack
import concourse.bass as bass
import concourse.mybir as mybir
from concourse.bass_utils import run_bass_kernel_spmd

F32 = mybir.dt.float32
BF16 = mybir.dt.bfloat16
AF = mybir.ActivationFunctionType
ALU = mybir.AluOpType

D = 1024
DFF = 2816
NCORE = 8
T = 2306
TKV = 2304
SLOT = 12288
EPS = 1e-6
NG0, KVG, FNG, BADA, BKV, CW, SINK, FLAG, NV = 0, 48, 56, 64, 208, 224, 248, 264, 265


class Res:
    __slots__ = ("w", "r")

    def __init__(self):
        self.w = None
        self.r = {}


class Tracker:
    ENG = ("pe", "act", "dve", "pool", "sp")

    def __init__(self, sems):
        self.sems = sems
        self.q = {e: [] for e in self.ENG}
        self.cnt = {e: 0 for e in self.ENG}
        self.waited = {e: {} for e in self.ENG}

    def _waits(self, eng, reads, writes, strict=False):
        need = {}

        def add(ev, same_ok):
            if ev is None:
                return
            sem, val, src = ev
            if src == eng and not same_ok and not strict:
                return
            k = id(sem)
            if k not in need or need[k][1] < val:
                need[k] = (sem, val)

        for r in reads:
            add(r.w, True)
        for w in writes:
            add(w.w, False)
            for ev in w.r.values():
                add(ev, False)
        out = []
        for k, (sem, val) in need.items():
            if self.waited[eng].get(k, 0) >= val:
                continue
            self.waited[eng][k] = val
            out.append((sem, val))
        return out

    def op(self, eng, fn, reads=(), writes=()):
        waits = self._waits(eng, reads, writes, strict=(eng != "pe"))
        self.cnt[eng] += 1
        sem = self.sems[eng]
        ev = (sem, self.cnt[eng], eng)

        def run(e):
            for s, v in waits:
                e.wait_ge(s, v)
            fn(e).then_inc(sem, 1)

        self.q[eng].append(run)
        for r in reads:
            r.r[eng] = ev
        for w in writes:
            w.w = ev
            w.r = {}
        return ev

    def mm(self, mms, reads, writes):
        def fn(e):
            ins = None
            for (o, l, r, st, sp) in mms:
                ins = e.matmul(o, l, r, start=st, stop=sp)
            return ins

        return self.op("pe", fn, reads, writes)

    def dma(self, qeng, pairs, dsem, reads=(), writes=()):
        waits = self._waits(qeng, reads, writes, strict=True)
        dsem[1] += 16 * len(pairs)
        sem = dsem[0]
        ev = (sem, dsem[1], "dma")

        def run(e):
            for s, v in waits:
                e.wait_ge(s, v)
            for (o, i) in pairs:
                e.dma_start(out=o, in_=i).then_inc(sem, 16)

        self.q[qeng].append(run)
        for r in reads:
            r.r["dma" + str(id(sem))] = ev
        for w in writes:
            w.w = ev
            w.r = {}
        return ev

    def wait_only(self, eng, evs):
        ws = []
        for (sem, val, _) in evs:
            ws.append((sem, val))

        def run(e):
            for s, v in ws:
                e.wait_ge(s, v)

        self.q[eng].append(run)


class Buf:
    def __init__(self, t, mk_sem):
        self.t = t
        self.res = Res()
        self._mk = mk_sem
        self._ds = None

    @property
    def ds(self):
        if self._ds is None:
            self._ds = [self._mk(), 0]
        return self._ds

    @property
    def ds_pool(self):
        if getattr(self, "_ds2", None) is None:
            self._ds2 = [self._mk(), 0]
        return self._ds2


class Rot:
    def __init__(self, bufs):
        self.bufs = bufs
        self.i = 0

    def get(self):
        b = self.bufs[self.i % len(self.bufs)]
        self.i += 1
        return b


def build():
    nc = bass.Bass("TRN2", target_bir_lowering=False)
    es = ExitStack()
    with es:
        import os as _os0
        KDBG = _os0.environ.get("KDBG", "")
        nsem = [0]

        def mk_sem():
            nsem[0] += 1
            return es.enter_context(nc.semaphore("s%d" % nsem[0]))

        def din(name, shape):
            return nc.dram_tensor(name, shape, F32, kind="ExternalInput").ap()

        def dout(name, shape):
            return nc.dram_tensor(name, shape, F32, kind="ExternalOutput").ap()

        xin = din("xin", [T, D])
        cin = din("cin", [17, D])
        scin = din("scin", [32, D])
        kcache = din("kcache", [16, 128, 256])
        vcache = din("vcache", [16, 128, 256])
        ropec = din("ropec", [128, TKV])
        ropes = din("ropes", [128, TKV])
        masks_d = din("masks", [128, 6, 128])
        vec_d = din("vec", [128, NV])
        ident_d = din("ident", [128, 128])
        w_ada = din("w_ada", [2, D, 9 * D])
        wg_d = din("wg", [2, 2, D, DFF])
        wu_d = din("wu", [2, 2, D, DFF])
        wd_d = din("wd", [2, 2, DFF, D])
        cwin = din("cwin", [D, 3 * D])
        cwout = din("cwout", [D, D])
        wadakv = din("wadakv", [D, 2 * D])
        wk_d = din("wk", [D, 256])
        wksw_d = din("wksw", [D, 256])
        wv_d = din("wv", [D, 256])
        wq_d = din("wq", [D, D])
        wqsw_d = din("wqsw", [D, D])
        wo_d = din("wo", [D, D])

        y_p = dout("y_p", [2048, D])
        y_s = dout("y_s", [128, D])
        cso_p = dout("cso_p", [2, D])
        cso_s = dout("cso_s", [32, D])
        kout_p = dout("kout_p", [128, 256])
        vout_p = dout("vout_p", [128, 256])
        kout_s = dout("kout_s", [16, 128, 256])
        vout_s = dout("vout_s", [16, 128, 256])

        kt_scr = nc.dram_tensor("kt_scr", [4, 128, TKV], BF16).ap()
        v_scr = nc.dram_tensor("v_scr", [18, 128, 256], BF16).ap()

        def sb(name, shape, dt):
            return es.enter_context(nc.sbuf_tensor(name, shape, dt))

        def mkbuf(name, shape, dt):
            return Buf(sb(name, shape, dt), mk_sem)

        X = sb("X", [128, 8, T], F32)
        Hb = sb("Hb", [128, 8, T], BF16)
        RING = [mkbuf("ring%d" % i, [128, SLOT], BF16) for i in range(2)]
        MOD = mkbuf("MOD", [128, 72, 17], F32)
        MODKV = mkbuf("MODKV", [128, 16, 17], F32)
        SC = mkbuf("SC", [128, 8, 17], BF16)
        Ab = mkbuf("Ab", [128, 8, 17], F32)
        Gb = mkbuf("Gb", [128, 8, 17], F32)
        VEC = mkbuf("VEC", [128, NV], F32)
        IDF = mkbuf("IDF", [128, 128], F32)
        IDB = mkbuf("IDB", [128, 128], BF16)
        ONES = mkbuf("ONES", [128, 128], BF16)
        MASKS = mkbuf("MASKS", [128, 6, 128], BF16)
        ESINK = mkbuf("ESINK", [128, 8], F32)
        ACTP = Rot([mkbuf("act%d" % i, [128, 4, 512], BF16) for i in range(2)])
        T32 = Rot([mkbuf("t32_%d" % i, [128, 520], F32) for i in range(4)])
        RS = Rot([mkbuf("rs%d" % i, [128, 512], F32) for i in range(2)])
        TB = Rot([mkbuf("tb%d" % i, [128, 512], BF16) for i in range(4)])
        P2 = Rot([mkbuf("p2_%d" % i, [128, 2, 512], BF16) for i in range(4)])
        KTC = mkbuf("KTC", [128, 16, 128], BF16)
        CARRY = mkbuf("CARRY", [128, 2, 2], F32)
        CSOP = mkbuf("CSOP", [128, 8, 2], F32)
        CSOS = mkbuf("CSOS", [128, 8, 32], F32)
        SCS = mkbuf("SCS", [128, 8, 32], F32)
        KVO = Rot([mkbuf("kvo%d" % i, [128, 256], F32) for i in range(2)])

        banks = []
        for i in range(8):
            pt = es.enter_context(nc.psum_tensor("bk%d" % i, [128, 512], F32))
            banks.append((pt, Res()))
        bank_i = [0]

        def bank():
            b = banks[bank_i[0] % 6]
            bank_i[0] += 1
            return b

        nbank_i = [0]

        def nbank():
            b = banks[6 + nbank_i[0] % 2]
            nbank_i[0] += 1
            return b

        sems = {e: mk_sem() for e in Tracker.ENG}
        tr = Tracker(sems)

        TILES0 = [(0, 0, 130, "halo")] + [(1 + k, 130 + 512 * k, 512, "prompt") for k in range(4)] + [(5, 2178, 128, "sample")]
        TILES1 = TILES0[1:]
        XR = [[Res() for _ in range(8)] for _ in range(6)]
        HR = [[Res() for _ in range(8)] for _ in range(6)]
        KTSCR = [[Res() for _ in range(6)] for _ in range(4)]
        VSCR = [Res() for _ in range(6)]
        out_events = []

        def ACT(out, in_, func, reads, writes, bias=None, scale=None):
            kw = {}
            if bias is not None:
                kw["bias"] = bias
            if scale is not None:
                kw["scale"] = scale
            tr.op("act", lambda e: e.activation(out=out, in_=in_, func=func, **kw), reads, writes)

        def TT(out, in0, in1, op, reads, writes):
            tr.op("dve", lambda e: e.tensor_tensor(out=out, in0=in0, in1=in1, op=op), reads, writes)

        def PTT(out, in0, in1, op, reads, writes):
            tr.op("pool", lambda e: e.tensor_tensor(out=out, in0=in0, in1=in1, op=op), reads, writes)

        def TS(out, in0, s1, s2, op0, op1, reads, writes):
            if s2 is None:
                tr.op("dve", lambda e: e.tensor_scalar(out=out, in0=in0, scalar1=s1, scalar2=None, op0=op0), reads, writes)
            else:
                tr.op("dve", lambda e: e.tensor_scalar(out=out, in0=in0, scalar1=s1, scalar2=s2, op0=op0, op1=op1), reads, writes)

        def STT(out, in0, scalar, in1, op0, op1, reads, writes):
            tr.op("dve", lambda e: e.scalar_tensor_tensor(out=out, in0=in0, scalar=scalar, in1=in1, op0=op0, op1=op1), reads, writes)

        def CPY(eng, out, in_, reads, writes):
            if eng == "act":
                tr.op("act", lambda e: e.copy(out=out, in_=in_), reads, writes)
            else:
                tr.op("dve", lambda e: e.tensor_copy(out=out, in_=in_), reads, writes)

        def TRANSP(out, in_, ident, reads, writes):
            tr.op("pe", lambda e: e.transpose(out, in_, ident), reads, writes)

        cp_i = [0]

        def evac(out, in_, reads, writes):
            cp_i[0] += 1
            CPY("act" if cp_i[0] % 2 else "dve", out, in_, reads, writes)

        def s3(ap):
            return ap.rearrange("p (b i) -> p b i", b=16)

        def bc8(ap16):
            return ap16.unsqueeze(2).broadcast_to([128, 16, 8])

        tr.dma("sp", [(VEC.t[:, :], vec_d)], VEC.ds, writes=[VEC.res])
        tr.dma("sp", [(IDF.t[:, :], ident_d)], IDF.ds, writes=[IDF.res])
        tr.dma("pool", [(IDB.t[:, :], ident_d)], IDB.ds, writes=[IDB.res])
        tr.dma("pool", [(MASKS.t[:, :, :], masks_d)], MASKS.ds, writes=[MASKS.res])
        tr.op("dve", lambda e: e.memset(ONES.t[:, :], 1.0), [], [ONES.res])
        ACT(ESINK.t[:, :], VEC.t[:, SINK:SINK + 8], AF.Exp, [VEC.res], [ESINK.res])

        def f32view(b):
            return b.t[:, :, :].rearrange("p a b -> p (a b)").bitcast(F32)

        cb = ACTP.get()
        cv_ = f32view(cb)
        tr.dma("sp", [(cv_[0:17, :], cin)], cb.ds, writes=[cb.res])
        ACT(cv_[0:17, :], cv_[0:17, :], AF.Silu, [cb.res], [cb.res])
        bk, rk = bank()
        for j in range(8):
            TRANSP(bk[:, j * 17:(j + 1) * 17], cv_[0:17, j * 128:(j + 1) * 128], IDF.t[0:17, 0:17], [cb.res, IDF.res], [rk])
        CPY("dve", SC.t[:, :, :], bk[:, 0:136].rearrange("p (j s) -> p j s", j=8), [rk], [SC.res])
        cb = ACTP.get()
        cv_ = f32view(cb)
        tr.dma("sp", [(cv_[0:32, :], scin)], cb.ds, writes=[cb.res])
        bk, rk = bank()
        for j in range(8):
            TRANSP(bk[:, j * 32:(j + 1) * 32], cv_[0:32, j * 128:(j + 1) * 128], IDF.t[0:32, 0:32], [cb.res, IDF.res], [rk])
        CPY("dve", SCS.t[:, :, :], bk[:, 0:256].rearrange("p (j s) -> p j s", j=8), [rk], [SCS.res])
        rowtiles = [(0, 2, 0, 0), (2, 128, 0, 2)] + [(130 + 128 * k, 128, 1 + k // 4, 130 + 128 * k) for k in range(16)] + [(2178, 128, 5, 2178)]
        for (r0, nr, ti, c0) in rowtiles:
            cb = ACTP.get()
            cv_ = f32view(cb)
            tr.dma("sp", [(cv_[0:nr, :], xin[r0:r0 + nr, :])], cb.ds, writes=[cb.res])
            for half in range(2):
                bk, rk = bank()
                for jj in range(4):
                    j = half * 4 + jj
                    TRANSP(bk[:, jj * 128:jj * 128 + nr], cv_[0:nr, j * 128:(j + 1) * 128], IDF.t[0:nr, 0:nr], [cb.res, IDF.res], [rk])
                evac(X[:, half * 4:half * 4 + 4, c0:c0 + nr], bk[:, :].rearrange("p (j c) -> p j c", j=4)[:, :, 0:nr],
                     [rk], [XR[ti][half * 4 + jj] for jj in range(4)])
        dsk = [mk_sem(), 0]
        out_events.append(tr.dma("sp", [(kout_s[:, 0:120, :], kcache[:, 8:128, :]), (vout_s[:, 0:120, :], vcache[:, 8:128, :])], dsk))

        parts = []
        ring_n = [0]

        def modv(m, j):
            return MOD.t[:, m * 8 + j, :]

        def gate_acc(kind_src):
            if kind_src[0] == "G":
                return (lambda i: Gb.t[:, i, 0:1]), (lambda i: Gb.t[:, i, 1:17]), Gb.res
            m = kind_src[1]
            return (lambda i: MOD.t[:, m * 8 + i, 0:1]), (lambda i: MOD.t[:, m * 8 + i, 1:17]), MOD.res

        def resid(tile, i, bk, rk, gacc):
            ti, c0, n, kind = tile
            gp, gs, gres = gacc
            xs = X[:, i, c0:c0 + n]
            if kind != "sample":
                STT(xs, bk[:, :n], gp(i), xs, ALU.mult, ALU.add, [rk, gres, XR[ti][i]], [XR[ti][i]])
            else:
                tb_ = T32.get()
                TT(s3(tb_.t[:, 0:128]), s3(bk[:, 0:128]), bc8(gs(i)), ALU.mult, [rk, gres], [tb_.res])
                TT(xs, xs, tb_.t[:, 0:128], ALU.add, [tb_.res, XR[ti][i]], [XR[ti][i]])

        def prep_mod(l, sub, ffn):
            ng = VEC.t[:, NG0 + (l * 3 + sub) * 8: NG0 + (l * 3 + sub) * 8 + 8]
            sc = MOD.t[:, (3 * sub + 1) * 8:(3 * sub + 1) * 8 + 8, :]
            TS(Ab.t[:, :, :], sc, 1.0, None, ALU.add, None, [MOD.res], [Ab.res])
            TT(Ab.t[:, :, :], Ab.t[:, :, :], ng.unsqueeze(2).broadcast_to([128, 8, 17]), ALU.mult, [Ab.res, VEC.res], [Ab.res])
            if ffn:
                gt = MOD.t[:, (3 * sub + 2) * 8:(3 * sub + 2) * 8 + 8, :]
                TS(Gb.t[:, :, :], gt, 0.5, None, ALU.mult, None, [MOD.res], [Gb.res])

        def prep_mod_kv():
            ng = VEC.t[:, KVG:KVG + 8]
            TS(Ab.t[:, :, :], MODKV.t[:, 8:16, :], 1.0, None, ALU.add, None, [MODKV.res], [Ab.res])
            TT(Ab.t[:, :, :], Ab.t[:, :, :], ng.unsqueeze(2).broadcast_to([128, 8, 17]), ALU.mult, [Ab.res, VEC.res], [Ab.res])

        def norm_items(tiles, bview, bres, pre=None, final=False):
            items = []
            first = [True]
            for tile in tiles:
                ti, c0, n, kind = tile
                st = {}

                def Afn(tile=tile, st=st):
                    ti, c0, n, kind = tile
                    if first[0] and pre is not None:
                        pre()
                    first[0] = False
                    bk, rk = nbank()
                    for j in range(8):
                        sq = TB.get()
                        ACT(sq.t[:, :n], X[:, j, c0:c0 + n], AF.Square, [XR[ti][j]], [sq.res])
                        tr.mm([(bk[:, :n], ONES.t[:, :], sq.t[:, :n], j == 0, j == 7)], [sq.res, ONES.res], [rk])
                    st["bk"] = (bk, rk)

                def Bfn(tile=tile, st=st):
                    ti, c0, n, kind = tile
                    bk, rk = st["bk"]
                    rs = RS.get()
                    ACT(rs.t[:, :n], bk[:, :n], AF.Ln, [rk], [rs.res], bias=EPS, scale=1.0 / D)
                    ACT(rs.t[:, :n], rs.t[:, :n], AF.Exp, [rs.res], [rs.res], scale=-0.5)
                    st["rs"] = rs
                    if final:
                        return
                    for j in range(8):
                        xs = X[:, j, c0:c0 + n]
                        tmp = T32.get()
                        if kind != "sample":
                            STT(tmp.t[:, :n], xs, Ab.t[:, j, 0:1], rs.t[:, :n], ALU.mult, ALU.mult, [XR[ti][j], Ab.res, rs.res], [tmp.res])
                            ACT(Hb[:, j, c0:c0 + n], tmp.t[:, :n], AF.Identity, [tmp.res, bres], [HR[ti][j]], bias=bview(j)[:, 0:1])
                        else:
                            TT(tmp.t[:, :n], xs, rs.t[:, :n], ALU.mult, [XR[ti][j], rs.res], [tmp.res])
                            TT(s3(tmp.t[:, :n]), s3(tmp.t[:, :n]), bc8(Ab.t[:, j, 1:17]), ALU.mult, [tmp.res, Ab.res], [tmp.res])
                            TT(s3(Hb[:, j, c0:c0 + n]), s3(tmp.t[:, :n]), bc8(bview(j)[:, 1:17]), ALU.add, [tmp.res, bres], [HR[ti][j]])

                items.append((Afn, Bfn, st))
            return items

        def add_plain(items):
            parts.append(dict(dma=None, items=lambda slot: [(a, b) for (a, b, *_r) in items]))

        def ada_part(src, c0, ncol, dstbuf, ch0, bcol0):
            nch = ncol // 128

            def dma(slot):
                v = slot.t[:, 0:8 * ncol].rearrange("p (k f) -> p k f", k=8)
                return [(v, src.rearrange("(k p) f -> p k f", p=128)[:, :, c0:c0 + ncol])]

            def items(slot):
                W = slot.t[:, 0:8 * ncol].rearrange("p (k f) -> p k f", k=8)

                def Afn():
                    bk, rk = bank()
                    for oc in range(nch):
                        tr.mm([(bk[:, oc * 17:(oc + 1) * 17], W[:, kk, oc * 128:(oc + 1) * 128], SC.t[:, kk, :], kk == 0, kk == 7) for kk in range(8)],
                              [slot.res, SC.res], [rk])
                    TT(dstbuf.t[:, ch0:ch0 + nch, :], bk[:, 0:nch * 17].rearrange("p (c s) -> p c s", c=nch),
                       VEC.t[:, bcol0:bcol0 + nch].unsqueeze(2).broadcast_to([128, nch, 17]), ALU.add, [rk, VEC.res], [dstbuf.res])

                return [(Afn, None)]

            parts.append(dict(dma=dma, items=items))

        def ada_micros(src, ncols, dstbuf, ch0, bcol0):
            return [(src, c * 128, dstbuf, ch0 + c, bcol0 + c) for c in range(ncols // 128)]

        class Side:
            def __init__(self, micros, per_item):
                self.todo = list(micros)
                self.pending = []
                self.k = per_item

            def _issue(self):
                (src, col0, dstbuf, ch, bcol) = self.todo.pop(0)
                pb_ = P2.get()
                v = pb_.t[:, :, :].rearrange("p a b -> p (a b)").rearrange("p (k f) -> p k f", k=8)
                tr.dma("pool", [(v, src.rearrange("(k p) f -> p k f", p=128)[:, :, col0:col0 + 128])], pb_.ds_pool, writes=[pb_.res])
                self.pending.append((pb_, v, dstbuf, ch, bcol))

            def _compute(self):
                (pb_, v, dstbuf, ch, bcol) = self.pending.pop(0)
                bk, rk = bank()
                tr.mm([(bk[:, 0:17], v[:, kk, :], SC.t[:, kk, :], kk == 0, kk == 7) for kk in range(8)], [pb_.res, SC.res], [rk])
                TT(dstbuf.t[:, ch, :], bk[:, 0:17], VEC.t[:, bcol:bcol + 1].broadcast_to([128, 17]), ALU.add, [rk, VEC.res], [dstbuf.res])

            def step(self):
                for _ in range(self.k):
                    if self.pending:
                        self._compute()
                for _ in range(self.k):
                    if self.todo:
                        self._issue()

            def flush(self):
                while self.pending or self.todo:
                    while self.pending:
                        self._compute()
                    for _ in range(3):
                        if self.todo:
                            self._issue()

        def ffn_parts(l, w, tiles, side=None):
            f0 = 0
            while f0 < 22:
                F = min(4, 22 - f0)

                def dma(slot, f0=f0, F=F):
                    c0 = f0 * 128
                    nc_ = F * 128
                    g = slot.t[:, 0:8 * nc_].rearrange("p (k f) -> p k f", k=8)
                    u = slot.t[:, 8 * nc_:16 * nc_].rearrange("p (k f) -> p k f", k=8)
                    dd = slot.t[:, 16 * nc_:16 * nc_ + F * D].rearrange("p (f d) -> p f d", f=F)
                    return [(g, wg_d[l, w].rearrange("(k p) f -> p k f", p=128)[:, :, c0:c0 + nc_]),
                            (u, wu_d[l, w].rearrange("(k p) f -> p k f", p=128)[:, :, c0:c0 + nc_]),
                            (dd, wd_d[l, w][c0:c0 + nc_, :].rearrange("(f p) d -> p f d", p=128))]

                def items(slot, F=F):
                    nc_ = F * 128
                    Wg = slot.t[:, 0:8 * nc_].rearrange("p (k f) -> p k f", k=8)
                    Wu = slot.t[:, 8 * nc_:16 * nc_].rearrange("p (k f) -> p k f", k=8)
                    Wd = slot.t[:, 16 * nc_:16 * nc_ + F * D].rearrange("p (f d) -> p f d", f=F)
                    gacc = gate_acc(("G",))
                    its = []
                    for tile in tiles:
                        st = {}

                        def Afn(tile=tile, st=st):
                            ti, c0, n, kind = tile
                            if side is not None:
                                side.step()
                            ab = ACTP.get()
                            st["ab"] = ab
                            for f in range(F):
                                bg, rg = bank()
                                bu, ru = bank()
                                tr.mm([(bg[:, :n], Wg[:, kk, f * 128:(f + 1) * 128], Hb[:, kk, c0:c0 + n], kk == 0, kk == 7) for kk in range(8)],
                                      [slot.res] + HR[ti], [rg])
                                tr.mm([(bu[:, :n], Wu[:, kk, f * 128:(f + 1) * 128], Hb[:, kk, c0:c0 + n], kk == 0, kk == 7) for kk in range(8)],
                                      [slot.res] + HR[ti], [ru])
                                sg = T32.get()
                                ACT(sg.t[:, :n], bg[:, :n], AF.Silu, [rg], [sg.res])
                                TT(ab.t[:, f, :n], sg.t[:, :n], bu[:, :n], ALU.mult, [sg.res, ru], [ab.res])

                        def Bfn(tile=tile, st=st):
                            ti, c0, n, kind = tile
                            ab = st["ab"]
                            for i in range(8):
                                bd, rd = bank()
                                tr.mm([(bd[:, :n], Wd[:, f, i * 128:(i + 1) * 128], ab.t[:, f, :n], f == 0, f == F - 1) for f in range(F)],
                                      [slot.res, ab.res], [rd])
                                resid(tile, i, bd, rd, gacc)

                        its.append((Afn, Bfn))
                    return its

                parts.append(dict(dma=dma, items=items))
                f0 += F

        def conv_parts(tiles):
            for p in range(4):
                def dma(slot, p=p):
                    prs = []
                    for q in range(3):
                        v = slot.t[:, q * 2048:(q + 1) * 2048].rearrange("p (k f) -> p k f", k=8)
                        prs.append((v, cwin.rearrange("(k p) f -> p k f", p=128)[:, :, q * D + p * 256:q * D + p * 256 + 256]))
                    v = slot.t[:, 6144:8192].rearrange("p (c d) -> p c d", c=2)
                    prs.append((v, cwout[p * 256:(p + 1) * 256, :].rearrange("(c p) d -> p c d", p=128)))
                    return prs

                def items(slot, p=p):
                    Wb = slot.t[:, 0:2048].rearrange("p (k f) -> p k f", k=8)
                    Wc = slot.t[:, 2048:4096].rearrange("p (k f) -> p k f", k=8)
                    Wv_ = slot.t[:, 4096:6144].rearrange("p (k f) -> p k f", k=8)
                    Wo_ = slot.t[:, 6144:8192].rearrange("p (c d) -> p c d", c=2)
                    gacc = gate_acc(("M", 5))
                    its = []
                    for tile in tiles:
                        st = {}

                        def Afn(tile=tile, st=st):
                            ti, c0, n, kind = tile
                            zb = P2.get()
                            st["zb"] = zb
                            if kind == "halo":
                                tr.op("dve", lambda e: e.memset(CARRY.t[:, :, :], 0.0), [], [CARRY.res])
                            for jl in range(2):
                                j = 2 * p + jl
                                bb, rb = bank()
                                bc, rc = bank()
                                bv, rv = bank()
                                for (bkk, rkk, W) in ((bc, rc, Wc), (bv, rv, Wv_), (bb, rb, Wb)):
                                    tr.mm([(bkk[:, :n], W[:, kk, jl * 128:(jl + 1) * 128], Hb[:, kk, c0:c0 + n], kk == 0, kk == 7) for kk in range(8)],
                                          [slot.res] + HR[ti], [rkk])
                                csb = T32.get()
                                ACT(csb.t[:, :n], bc[:, :n], AF.Identity, [rc], [csb.res])
                                U = T32.get()
                                cvb = T32.get()
                                w0 = VEC.t[:, CW + j:CW + j + 1]
                                w1 = VEC.t[:, CW + 8 + j:CW + 8 + j + 1]
                                w2 = VEC.t[:, CW + 16 + j:CW + 16 + j + 1]
                                if kind != "sample":
                                    CPY("dve", U.t[:, 0:2], CARRY.t[:, jl, :], [CARRY.res], [U.res])
                                    TT(U.t[:, 2:2 + n], csb.t[:, :n], bv[:, :n], ALU.mult, [csb.res, rv], [U.res])
                                    if kind == "halo":
                                        TS(U.t[:, 2:2 + n], U.t[:, 2:2 + n], VEC.t[:, FLAG:FLAG + 1], None, ALU.mult, None, [U.res, VEC.res], [U.res])
                                    CPY("dve", CARRY.t[:, jl, :], U.t[:, n:n + 2], [U.res], [CARRY.res])
                                    if ti == 4:
                                        CPY("dve", CSOP.t[:, j, :], U.t[:, n:n + 2], [U.res], [CSOP.res])
                                    ACT(cvb.t[:, :n], U.t[:, 0:n], AF.Identity, [U.res, VEC.res], [cvb.res], scale=w0)
                                    STT(cvb.t[:, :n], U.t[:, 1:n + 1], w1, cvb.t[:, :n], ALU.mult, ALU.add, [U.res, cvb.res], [cvb.res])
                                    STT(cvb.t[:, :n], U.t[:, 2:n + 2], w2, cvb.t[:, :n], ALU.mult, ALU.add, [U.res, cvb.res], [cvb.res])
                                    TT(zb.t[:, jl, :n], bb[:, :n], cvb.t[:, :n], ALU.mult, [rb, cvb.res], [zb.res])
                                else:
                                    U3 = U.t[:, 0:160].rearrange("p (b i) -> p b i", b=16)
                                    CPY("dve", U3[:, :, 0:2], SCS.t[:, j, :].rearrange("p (b i) -> p b i", b=16), [SCS.res], [U.res])
                                    TT(U3[:, :, 2:10], s3(csb.t[:, :n]), s3(bv[:, :n]), ALU.mult, [csb.res, rv], [U.res])
                                    CPY("dve", CSOS.t[:, j, :].rearrange("p (b i) -> p b i", b=16), U3[:, :, 8:10], [U.res], [CSOS.res])
                                    c3 = s3(cvb.t[:, :n])
                                    ACT(c3, U3[:, :, 0:8], AF.Identity, [U.res, VEC.res], [cvb.res], scale=w0)
                                    STT(c3, U3[:, :, 1:9], w1, c3, ALU.mult, ALU.add, [U.res, cvb.res], [cvb.res])
                                    STT(c3, U3[:, :, 2:10], w2, c3, ALU.mult, ALU.add, [U.res, cvb.res], [cvb.res])
                                    TT(zb.t[:, jl, :n], bb[:, :n], cvb.t[:, :n], ALU.mult, [rb, cvb.res], [zb.res])

                        def Bfn(tile=tile, st=st):
                            ti, c0, n, kind = tile
                            zb = st["zb"]
                            for i in range(8):
                                bd, rd = bank()
                                tr.mm([(bd[:, :n], Wo_[:, c, i * 128:(i + 1) * 128], zb.t[:, c, :n], c == 0, c == 1) for c in range(2)],
                                      [slot.res, zb.res], [rd])
                                resid(tile, i, bd, rd, gacc)

                        its.append((Afn, Bfn))
                    return its

                parts.append(dict(dma=dma, items=items))

        KVT = [(0, 2, 128, "halo")] + TILES1

        def kv_part():
            def dma(slot):
                prs = []
                for (off, src) in ((0, wk_d), (4096, wksw_d)):
                    v = slot.t[:, off:off + 4096].rearrange("p (k g e d) -> p k g e d", k=8, g=4, e=2)
                    s = src.rearrange("(k p) (g d) -> p k g d", p=128, g=4)
                    for e in range(2):
                        for g_ in range(4):
                            prs.append((v[:, :, g_, e, :], s[:, :, g_, :]))
                v = slot.t[:, 8192:8192 + 2048].rearrange("p (k d) -> p k d", k=8)
                prs.append((v, wv_d.rearrange("(k p) d -> p k d", p=128)))
                return prs

            def items(slot):
                Wk = slot.t[:, 0:4096].rearrange("p (k g m) -> p k g m", k=8, g=4)
                Wks = slot.t[:, 4096:8192].rearrange("p (k g m) -> p k g m", k=8, g=4)
                Wv_ = slot.t[:, 8192:8192 + 2048].rearrange("p (k d) -> p k d", k=8)
                its = []
                for tile in KVT:
                    def Afn(tile=tile):
                        ti, c0, n, kind = tile
                        rp = ACTP.get()
                        rv_ = f32view(rp).rearrange("p (a b) -> p a b", a=2)
                        tr.dma("sp", [(rv_[:, 0, 0:n], ropec[:, c0 - 2:c0 - 2 + n]), (rv_[:, 1, 0:n], ropes[:, c0 - 2:c0 - 2 + n])], rp.ds, writes=[rp.res])
                        is_out = (ti == 4) or (ti == 5)
                        for g in range(4):
                            bk_, rk_ = bank()
                            bs_, rs_ = bank()
                            tr.mm([(bk_[:, :n], Wk[:, kk, g, :], Hb[:, kk, c0:c0 + n], kk == 0, kk == 7) for kk in range(8)], [slot.res] + HR[ti], [rk_])
                            tr.mm([(bs_[:, :n], Wks[:, kk, g, :], Hb[:, kk, c0:c0 + n], kk == 0, kk == 7) for kk in range(8)], [slot.res] + HR[ti], [rs_])
                            t1 = T32.get()
                            t2 = T32.get()
                            TT(t1.t[:, :n], bk_[:, :n], rv_[:, 0, 0:n], ALU.mult, [rk_, rp.res], [t1.res])
                            TT(t2.t[:, :n], bs_[:, :n], rv_[:, 1, 0:n], ALU.mult, [rs_, rp.res], [t2.res])
                            TT(t1.t[:, :n], t1.t[:, :n], t2.t[:, :n], ALU.add, [t1.res, t2.res], [t1.res])
                            kb = TB.get()
                            ACT(kb.t[:, :n], t1.t[:, :n], AF.Identity, [t1.res], [kb.res])
                            if "a" not in KDBG:
                                tr.dma("sp", [(kt_scr[g, :, c0 - 2:c0 - 2 + n], kb.t[:, :n])], kb.ds, reads=[kb.res], writes=[KTSCR[g][ti]])
                            if is_out and "b" not in KDBG:
                                bt_, rt_ = bank()
                                TRANSP(bt_[:, 0:128], t1.t[:, n - 128:n], IDF.t[:, :], [t1.res, IDF.res], [rt_])
                                if ti not in st_k:
                                    st_k[ti] = KVO.get()
                                ko = st_k[ti]
                                evac(ko.t[:, g * 64:(g + 1) * 64], bt_[:, 0:64], [rt_], [ko.res])
                        if is_out and "b" not in KDBG:
                            ko = st_k[ti]
                            if ti == 4:
                                out_events.append(tr.dma("sp", [(kout_p, ko.t[:, :])], ko.ds, reads=[ko.res]))
                            else:
                                out_events.append(tr.dma("sp", [(kout_s[b, 120:128, :], ko.t[b * 8:(b + 1) * 8, :]) for b in range(16)], ko.ds, reads=[ko.res]))

                    def Bfn(tile=tile):
                        ti, c0, n, kind = tile
                        if "c" in KDBG:
                            return
                        nb = n // 128
                        vb = P2.get()
                        vv = vb.t[:, :, :].rearrange("p a b -> p (a b)").rearrange("p (b d) -> p b d", b=4)
                        is_out = (ti == 4) or (ti == 5)
                        for b_ in range(nb):
                            bk_, rk_ = bank()
                            tr.mm([(bk_[:, 0:256], Hb[:, kk, c0 + b_ * 128:c0 + (b_ + 1) * 128], Wv_[:, kk, :], kk == 0, kk == 7) for kk in range(8)],
                                  [slot.res] + HR[ti], [rk_])
                            if not (is_out and b_ == nb - 1):
                                evac(vv[:, b_, :], bk_[:, 0:256], [rk_], [vb.res])
                            else:
                                vo = KVO.get()
                                CPY("dve", vo.t[:, :], bk_[:, 0:256], [rk_], [vo.res])
                                CPY("act", vv[:, b_, :], vo.t[:, :], [vo.res], [vb.res])
                                if ti == 4:
                                    out_events.append(tr.dma("sp", [(vout_p, vo.t[:, :])], vo.ds, reads=[vo.res]))
                                else:
                                    out_events.append(tr.dma("sp", [(vout_s[b, 120:128, :], vo.t[b * 8:(b + 1) * 8, :]) for b in range(16)], vo.ds, reads=[vo.res]))
                        blk0 = (c0 - 2) // 128
                        if "d" not in KDBG:
                            tr.dma("sp", [(v_scr[blk0:blk0 + nb, :, :].rearrange("b p d -> p b d"), vv[:, 0:nb, :])], vb.ds, reads=[vb.res], writes=[VSCR[ti]])

                    its.append((Afn, Bfn))
                return its

            parts.append(dict(dma=dma, items=items))

        st_k = {}

        def attn_parts(tiles):
            for g in range(4):
                def dma(slot, g=g):
                    prs = []
                    prs.append((slot.t[:, 0:2048].rearrange("p (k f) -> p k f", k=8), wq_d.rearrange("(k p) f -> p k f", p=128)[:, :, g * 256:(g + 1) * 256]))
                    prs.append((slot.t[:, 2048:4096].rearrange("p (k f) -> p k f", k=8), wqsw_d.rearrange("(k p) f -> p k f", p=128)[:, :, g * 256:(g + 1) * 256]))
                    prs.append((slot.t[:, 4096:6144].rearrange("p (c d) -> p c d", c=2), wo_d[g * 256:(g + 1) * 256, :].rearrange("(c p) d -> p c d", p=128)))
                    prs.append((slot.t[:, 6144:6144 + TKV], kt_scr[g, :, :]))
                    prs.append((slot.t[:, 8448:8448 + 1152].rearrange("p (b d) -> p b d", b=18), v_scr[:, :, g * 64:(g + 1) * 64].rearrange("b p d -> p b d")))
                    prs.append((slot.t[:, 9600:9600 + 1024].rearrange("p (b d) -> p b d", b=16), kcache[:, :, g * 64:(g + 1) * 64].rearrange("b p d -> p b d")))
                    prs.append((slot.t[:, 10624:10624 + 1024].rearrange("p (b d) -> p b d", b=16), vcache[:, :, g * 64:(g + 1) * 64].rearrange("b p d -> p b d")))
                    return prs

                def items(slot, g=g):
                    Wq = slot.t[:, 0:2048].rearrange("p (k f) -> p k f", k=8)
                    Wqs = slot.t[:, 2048:4096].rearrange("p (k f) -> p k f", k=8)
                    Wo_ = slot.t[:, 4096:6144].rearrange("p (c d) -> p c d", c=2)
                    KT = slot.t[:, 6144:6144 + TKV]
                    Vg = slot.t[:, 8448:8448 + 1152].rearrange("p (b d) -> p b d", b=18)
                    kc = slot.t[:, 9600:9600 + 1024].rearrange("p (b d) -> p b d", b=16)
                    vc = slot.t[:, 10624:10624 + 1024].rearrange("p (b d) -> p b d", b=16)
                    gacc = gate_acc(("M", 5))
                    esk = ESINK.t[:, g * 2:(g + 1) * 2].unsqueeze(2).broadcast_to([128, 2, 128])
                    its = []

                    def Pfn():
                        for q4 in range(4):
                            bk_, rk_ = bank()
                            for bl in range(4):
                                b = q4 * 4 + bl
                                for e in range(2):
                                    tr.mm([(bk_[e * 64:(e + 1) * 64, bl * 128:(bl + 1) * 128], kc[:, b, :], IDB.t[:, :], True, True)], [slot.res, IDB.res], [rk_])
                            evac(KTC.t[:, q4 * 4:(q4 + 1) * 4, :], bk_[:, :].rearrange("p (b k) -> p b k", b=4), [rk_], [KTC.res])

                    for tile in tiles:
                        st = {}

                        def Afn(tile=tile, st=st):
                            ti, c0, n, kind = tile
                            if kind == "sample":
                                Pfn()
                            rp = ACTP.get()
                            rv_ = f32view(rp).rearrange("p (a b) -> p a b", a=2)
                            tr.dma("sp", [(rv_[:, 0, 0:n], ropec[:, c0 - 2:c0 - 2 + n]), (rv_[:, 1, 0:n], ropes[:, c0 - 2:c0 - 2 + n])], rp.ds, writes=[rp.res])
                            qt = P2.get()
                            st["qt"] = qt
                            for c in range(2):
                                bq, rq = bank()
                                bs_, rs_ = bank()
                                tr.mm([(bq[:, :n], Wq[:, kk, c * 128:(c + 1) * 128], Hb[:, kk, c0:c0 + n], kk == 0, kk == 7) for kk in range(8)], [slot.res] + HR[ti], [rq])
                                tr.mm([(bs_[:, :n], Wqs[:, kk, c * 128:(c + 1) * 128], Hb[:, kk, c0:c0 + n], kk == 0, kk == 7) for kk in range(8)], [slot.res] + HR[ti], [rs_])
                                t1 = T32.get()
                                t2 = T32.get()
                                TT(t1.t[:, :n], bq[:, :n], rv_[:, 0, 0:n], ALU.mult, [rq, rp.res], [t1.res])
                                TT(t2.t[:, :n], bs_[:, :n], rv_[:, 1, 0:n], ALU.mult, [rs_, rp.res], [t2.res])
                                PTT(qt.t[:, c, :n], t1.t[:, :n], t2.t[:, :n], ALU.add, [t1.res, t2.res], [qt.res])

                        def Bfn(tile=tile, st=st):
                            ti, c0, n, kind = tile
                            if "q" in KDBG or ("s" in KDBG and kind == "sample") or ("m" in KDBG and kind != "sample"):
                                return
                            qt = st["qt"]
                            ot = P2.get()
                            bst = {}

                            def stage1(blk):
                                q0 = blk * 128
                                kown = c0 - 2 + q0
                                be = [bank(), bank()]
                                pe_ = [TB.get(), TB.get()]
                                mlist = []
                                if kind != "sample":
                                    for (c_lo, k_lo) in ((0, kown - 128), (256, kown)):
                                        for e in range(2):
                                            es_ = slice(e * 64, (e + 1) * 64)
                                            mlist.append((be[e][0][:, c_lo:c_lo + 256].rearrange("p (c q) -> p c q", c=2), KT[es_, k_lo:k_lo + 128], qt.t[es_, :, q0:q0 + 128], True, True))
                                else:
                                    for b in range(16):
                                        for c_ in range(2):
                                            for e in range(2):
                                                es_ = slice(e * 64, (e + 1) * 64)
                                                o = be[e][0][:, c_ * 128 + b * 8:c_ * 128 + b * 8 + 8]
                                                mlist.append((o, KTC.t[es_, b, :], qt.t[es_, c_, b * 8:(b + 1) * 8], True, True))
                                    for e in range(2):
                                        es_ = slice(e * 64, (e + 1) * 64)
                                        mlist.append((be[e][0][:, 256:512].rearrange("p (c q) -> p c q", c=2), KT[es_, kown:kown + 128], qt.t[es_, :, q0:q0 + 128], True, True))
                                tr.mm(mlist, [slot.res, KTC.res, qt.res], [be[0][1], be[1][1]])
                                if kind != "sample":
                                    mi = 2 if (ti == 1 and blk == 0) else 0
                                else:
                                    mi = 4
                                msk = MASKS.t[:, mi:mi + 2, :].unsqueeze(2).broadcast_to([128, 2, 2, 128])
                                for e in range(2):
                                    bke, rke = be[e]
                                    ACT(pe_[e].t[:, :], bke[:, :], AF.Exp, [rke], [pe_[e].res], scale=0.125)
                                    p4 = pe_[e].t[:, :].rearrange("p (t c q) -> p t c q", t=2, c=2)
                                    (TT if e == 0 else PTT)(p4, p4, msk, ALU.mult, [pe_[e].res, MASKS.res], [pe_[e].res])
                                bst[blk] = pe_

                            def stage2(blk):
                                q0 = blk * 128
                                kown = c0 - 2 + q0
                                vown = kown // 128
                                pe_ = bst.pop(blk)
                                bd_, rd_ = bank()
                                tr.mm([(bd_[e * 64:(e + 1) * 64, 0:256], ONES.t[:, 0:64], pe_[e].t[:, t_ * 256:(t_ + 1) * 256], t_ == 0, t_ == 1) for t_ in range(2) for e in range(2)],
                                      [pe_[0].res, pe_[1].res, ONES.res], [rd_])
                                rden = RS.get()
                                r3 = rden.t[:, 0:256].rearrange("p (h q) -> p h q", h=2)
                                TT(r3, bd_[:, 0:256].rearrange("p (h q) -> p h q", h=2), esk, ALU.add, [rd_, ESINK.res], [rden.res])
                                tr.op("dve", lambda e, rden=rden: e.reciprocal(out=rden.t[:, 0:256], in_=rden.t[:, 0:256]), [rden.res], [rden.res])
                                bo, ro = bank()
                                mlist = []
                                if kind != "sample":
                                    for (vb_, lo, st_, sp_) in ((vown - 1, 0, True, False), (vown, 256, False, True)):
                                        for e in range(2):
                                            es_ = slice(e * 64, (e + 1) * 64)
                                            mlist.append((bo[es_, 0:256], Vg[:, vb_, :], pe_[e].t[:, lo:lo + 256], st_, sp_))
                                else:
                                    for e in range(2):
                                        es_ = slice(e * 64, (e + 1) * 64)
                                        mlist.append((bo[es_, 0:256], Vg[:, vown, :], pe_[e].t[:, 256:512], True, False))
                                    for b in range(16):
                                        for c_ in range(2):
                                            for e in range(2):
                                                es_ = slice(e * 64, (e + 1) * 64)
                                                o = bo[es_, c_ * 128 + b * 8:c_ * 128 + b * 8 + 8]
                                                r_ = pe_[e].t[:, c_ * 128 + b * 8:c_ * 128 + b * 8 + 8]
                                                mlist.append((o, vc[:, b, :], r_, False, (b == 15 and c_ == 1)))
                                tr.mm(mlist, [slot.res, pe_[0].res, pe_[1].res], [ro])
                                TT(ot.t[:, :, q0:q0 + 128], bo[:, 0:256].rearrange("p (c q) -> p c q", c=2),
                                   rden.t[:, 0:256].rearrange("p (c q) -> p c q", c=2), ALU.mult, [ro, rden.res], [ot.res])

                            nblk = n // 128
                            for blk in range(nblk):
                                stage1(blk)
                                if blk >= 1:
                                    stage2(blk - 1)
                            stage2(nblk - 1)
                            for i in range(8):
                                bd, rd = bank()
                                tr.mm([(bd[:, :n], Wo_[:, c, i * 128:(i + 1) * 128], ot.t[:, c, :n], c == 0, c == 1) for c in range(2)], [slot.res, ot.res], [rd])
                                resid(tile, i, bd, rd, gacc)

                        its.append((Afn, Bfn))
                    return its

                parts.append(dict(dma=dma, items=items, extra=[KTSCR[g][t_] for t_ in range(6)] + VSCR))

        def final_items(tiles):
            base = norm_items(tiles, None, None, final=True)
            its = []
            for (Afn, Bfn, st), tile in zip(base, tiles):
                def B2(tile=tile, st=st, Bfn=Bfn):
                    ti, c0, n, kind = tile
                    Bfn()
                    rs = st["rs"]
                    for blk in range(n // 128):
                        q0 = blk * 128
                        sg_ = ACTP.get()
                        sv_ = f32view(sg_)
                        for half in range(2):
                            bk_, rk_ = bank()
                            for jj in range(4):
                                j = half * 4 + jj
                                yt = T32.get()
                                STT(yt.t[:, 0:128], X[:, j, c0 + q0:c0 + q0 + 128], VEC.t[:, FNG + j:FNG + j + 1], rs.t[:, q0:q0 + 128], ALU.mult, ALU.mult,
                                    [XR[ti][j], VEC.res, rs.res], [yt.res])
                                TRANSP(bk_[:, jj * 128:(jj + 1) * 128], yt.t[:, 0:128], IDF.t[:, :], [yt.res, IDF.res], [rk_])
                            evac(sv_[:, half * 512:(half + 1) * 512], bk_[:, :], [rk_], [sg_.res])
                        if kind == "sample":
                            dst = y_s
                        else:
                            r0 = c0 - 130 + q0
                            dst = y_p[r0:r0 + 128, :]
                        out_events.append(tr.dma("sp", [(dst, sv_[:, :])], sg_.ds, reads=[sg_.res]))

                its.append((Afn, B2))
            parts.append(dict(dma=None, items=lambda slot: its))

        def cso_out():
            def Afn():
                sg_ = ACTP.get()
                sv_ = f32view(sg_)
                for half in range(2):
                    bk_, rk_ = bank()
                    for jj in range(4):
                        TRANSP(bk_[0:32, jj * 128:(jj + 1) * 128], CSOS.t[:, half * 4 + jj, :], IDF.t[:, :], [CSOS.res, IDF.res], [rk_])
                    evac(sv_[0:32, half * 512:(half + 1) * 512], bk_[0:32, :], [rk_], [sg_.res])
                out_events.append(tr.dma("sp", [(cso_s, sv_[0:32, :])], sg_.ds, reads=[sg_.res]))
                sg2 = ACTP.get()
                sv2 = f32view(sg2)
                for half in range(2):
                    bk3, rk3 = bank()
                    for jj in range(4):
                        TRANSP(bk3[0:2, jj * 128:(jj + 1) * 128], CSOP.t[:, half * 4 + jj, :], IDF.t[:, :], [CSOP.res, IDF.res], [rk3])
                    evac(sv2[0:2, half * 512:(half + 1) * 512], bk3[0:2, :], [rk3], [sg2.res])
                out_events.append(tr.dma("sp", [(cso_p, sv2[0:2, :])], sg2.ds, reads=[sg2.res]))

            parts.append(dict(dma=None, items=lambda slot: [(Afn, None)]))

        side_l0 = Side(ada_micros(w_ada[0], 9 * D, MOD, 0, BADA)[24:], 2)
        side_l1 = Side(ada_micros(wadakv, 2 * D, MODKV, 0, BKV) + ada_micros(w_ada[1], 9 * D, MOD, 0, BADA + 72), 3)

        def flush_part(sd):
            parts.append(dict(dma=None, items=lambda slot: [(sd.flush, None)]))

        ada_part(w_ada[0], 0, 1536, MOD, 0, BADA + 0)
        ada_part(w_ada[0], 1536, 1536, MOD, 12, BADA + 12)
        add_plain(norm_items(TILES0, lambda j: modv(0, j), MOD.res, pre=lambda: prep_mod(0, 0, True)))
        ffn_parts(0, 0, TILES0, side=side_l0)
        flush_part(side_l0)
        add_plain(norm_items(TILES0, lambda j: modv(3, j), MOD.res, pre=lambda: prep_mod(0, 1, False)))
        conv_parts(TILES0)
        cso_out()
        add_plain(norm_items(TILES0, lambda j: modv(6, j), MOD.res, pre=lambda: prep_mod(0, 2, True)))
        ffn_parts(0, 1, TILES0, side=side_l1)
        flush_part(side_l1)
        add_plain(norm_items(TILES0, lambda j: MODKV.t[:, j, :], MODKV.res, pre=prep_mod_kv))
        kv_part()
        add_plain(norm_items(TILES1, lambda j: modv(0, j), MOD.res, pre=lambda: prep_mod(1, 0, True)))
        ffn_parts(1, 0, TILES1)
        add_plain(norm_items(TILES1, lambda j: modv(3, j), MOD.res, pre=lambda: prep_mod(1, 1, False)))
        attn_parts(TILES1)
        add_plain(norm_items(TILES1, lambda j: modv(6, j), MOD.res, pre=lambda: prep_mod(1, 2, True)))
        ffn_parts(1, 1, TILES1)
        final_items(TILES1)

        import os as _os
        _ks = _os.environ.get("KSTOP")
        if _ks is not None:
            parts = parts[:int(_ks)]
        dparts = [p for p in parts if p["dma"] is not None]
        for k, p in enumerate(dparts):
            p["slot"] = RING[k % 2]
            p["didx"] = k

        def issue(k):
            if k < len(dparts):
                p = dparts[k]
                slot = p["slot"]
                tr.dma("pool", p["dma"](slot), slot.ds, reads=p.get("extra", []), writes=[slot.res])

        issue(0)
        prevB = [None]
        for p in parts:
            slot = p.get("slot")
            its = p["items"](slot)
            for n_, it in enumerate(its):
                a, b = it[0], it[1]
                if a:
                    a()
                if prevB[0]:
                    prevB[0]()
                prevB[0] = b
                if n_ == 0 and p["dma"] is not None:
                    issue(p["didx"] + 1)
        if prevB[0]:
            prevB[0]()

        tr.wait_only("sp", out_events)

        block = es.enter_context(nc.Block())

        @block.tensor
        def _(e):
            for f in tr.q["pe"]:
                f(e)

        @block.scalar
        def _(e):
            for f in tr.q["act"]:
                f(e)

        @block.vector
        def _(e):
            for f in tr.q["dve"]:
                f(e)

        @block.gpsimd
        def _(e):
            for f in tr.q["pool"]:
                f(e)

        @block.sync
        def _(e):
            for f in tr.q["sp"]:
                f(e)
    return nc


_NC = [None]


def _rope_tables(pos):
    inv = (np.float32(500000.0) ** (-np.arange(0, 16, 2, dtype=np.float32) / np.float32(16))).astype(np.float32)
    ang = pos.astype(np.float32)[None, :] * inv[:, None]
    cos = np.cos(ang).astype(np.float32)
    sin = np.sin(ang).astype(np.float32)
    n = pos.shape[0]
    C = np.ones((64, n), np.float32)
    S = np.zeros((64, n), np.float32)
    C[0:8] = cos
    C[8:16] = cos
    S[0:8] = -sin
    S[8:16] = sin
    return C, S


def _swap_cols(w, nheads):
    perm = []
    for h in range(nheads):
        base = h * 64
        perm += [base + 8 + d for d in range(8)] + [base + d for d in range(8)] + [base + d for d in range(16, 64)]
    return np.ascontiguousarray(w[:, perm])


def kernel(x_prompt, x_sample, state_conv, cache_k_win, cache_v_win, c_prompt, c_sample,
           norm_g, w_ada, b_ada, w_ffn_gate, w_ffn_up, w_ffn_down,
           conv_w_in, conv_w, conv_w_out, kv_norm_g, w_ada_kv, b_ada_kv, w_k, w_v,
           attn_w_q, attn_sinks, attn_w_o, final_norm_g):
    f = lambda a: np.ascontiguousarray(np.asarray(a, dtype=np.float32))
    x_prompt, x_sample, state_conv = f(x_prompt), f(x_sample), f(state_conv)
    cache_k_win, cache_v_win, c_prompt, c_sample = f(cache_k_win), f(cache_v_win), f(c_prompt), f(c_sample)
    if _NC[0] is None:
        _NC[0] = build()
    nc = _NC[0]

    def fm(v):
        v = f(v)
        lead = v.shape[:-1]
        return np.moveaxis(v.reshape(lead + (8, 128)), -1, 0)

    vec_common = np.zeros((128, NV), np.float32)
    vec_common[:, NG0:NG0 + 48] = fm(norm_g).reshape(128, 48)
    vec_common[:, KVG:KVG + 8] = fm(kv_norm_g).reshape(128, 8)
    vec_common[:, FNG:FNG + 8] = fm(final_norm_g).reshape(128, 8)
    ba = f(b_ada).reshape(2, 72, 128)
    vec_common[:, BADA:BADA + 144] = np.moveaxis(ba, -1, 0).reshape(128, 144)
    vec_common[:, BKV:BKV + 16] = np.moveaxis(f(b_ada_kv).reshape(16, 128), -1, 0)
    vec_common[:, CW:CW + 24] = fm(f(conv_w)[0]).reshape(128, 24)
    sk = f(attn_sinks)[0]
    for e in range(2):
        sperm = [4 * g + 2 * c + e for g in range(4) for c in range(2)]
        vec_common[e * 64:(e + 1) * 64, SINK:SINK + 8] = np.broadcast_to(sk[sperm][None, :], (64, 8))

    ident = np.eye(128, dtype=np.float32)
    jj = np.arange(128)[:, None]
    ii = np.arange(128)[None, :]
    mA = (jj > ii).astype(np.float32)
    mB = (jj <= ii).astype(np.float32)
    mAs = (jj > (ii % 8)).astype(np.float32)
    mBs = (((jj // 8) == (ii // 8)) & ((jj % 8) <= (ii % 8))).astype(np.float32)

    shared = dict(
        ident=ident, w_ada=f(w_ada), wg=f(w_ffn_gate), wu=f(w_ffn_up), wd=f(w_ffn_down),
        cwin=f(conv_w_in)[0], cwout=f(conv_w_out)[0], wadakv=f(w_ada_kv),
        wk=f(w_k), wksw=_swap_cols(f(w_k), 4), wv=f(w_v),
        wq=f(attn_w_q)[0], wqsw=_swap_cols(f(attn_w_q)[0], 16), wo=f(attn_w_o)[0],
    )
    in_maps = []
    for c in range(NCORE):
        b, q = c // 4, c % 4
        s0 = q * 2048
        xin = np.zeros((T, D), np.float32)
        if q > 0:
            xin[0:130] = x_prompt[b, s0 - 130:s0]
        xin[130:2178] = x_prompt[b, s0:s0 + 2048]
        xin[2178:2306] = x_sample[16 * c:16 * c + 16].reshape(128, D)
        cin = np.concatenate([c_prompt[b:b + 1], c_sample[16 * c:16 * c + 16]], axis=0)
        scin = state_conv[0, 16 * c:16 * c + 16].reshape(32, D)
        pos = np.concatenate([np.arange(s0 - 128, s0 + 2048), np.tile(16384 + np.arange(8), 16)]).astype(np.int64)
        C, S = _rope_tables(pos)
        masks = np.stack([mA, mB, mA if q > 0 else np.zeros_like(mA), mB, mAs, mBs], axis=1)
        vec = vec_common.copy()
        vec[:, FLAG] = 1.0 if q > 0 else 0.0
        m = dict(shared)
        m.update(xin=xin, cin=np.ascontiguousarray(cin), scin=np.ascontiguousarray(scin),
                 kcache=np.ascontiguousarray(cache_k_win[16 * c:16 * c + 16].reshape(16, 128, 256)),
                 vcache=np.ascontiguousarray(cache_v_win[16 * c:16 * c + 16].reshape(16, 128, 256)),
                 ropec=np.ascontiguousarray(np.concatenate([C, C], axis=0)), ropes=np.ascontiguousarray(np.concatenate([S, S], axis=0)),
                 masks=np.ascontiguousarray(masks), vec=vec)
        in_maps.append(m)
    res = run_bass_kernel_spmd(nc, in_maps, core_ids=list(range(NCORE)))
    R = res.results
    y_prompt = np.stack([np.concatenate([R[4 * b + q]["y_p"] for q in range(4)], axis=0) for b in range(2)], axis=0)
    y_sample = np.concatenate([R[c]["y_s"].reshape(16, 8, D) for c in range(NCORE)], axis=0)
    conv_p = np.stack([R[3]["cso_p"], R[7]["cso_p"]], axis=0)[None]
    conv_s = np.concatenate([R[c]["cso_s"].reshape(16, 2, D) for c in range(NCORE)], axis=0)[None]
    k_p = np.stack([R[3]["kout_p"], R[7]["kout_p"]], axis=0).reshape(2, 128, 4, 64)
    v_p = np.stack([R[3]["vout_p"], R[7]["vout_p"]], axis=0).reshape(2, 128, 4, 64)
    k_s = np.concatenate([R[c]["kout_s"] for c in range(NCORE)], axis=0).reshape(128, 128, 4, 64)
    v_s = np.concatenate([R[c]["vout_s"] for c in range(NCORE)], axis=0).reshape(128, 128, 4, 64)
    outs = (y_prompt, y_sample, conv_p, conv_s, k_p, v_p, k_s, v_s)
    return tuple(np.ascontiguousarray(o.astype(np.float32)) for o in outs)
```
